# Optimizing a Trainium2 kernel written in Bass

```python
import jax
import jax.numpy as jnp
from jax import lax
import numpy as np

D_MODEL = 2048
BATCH = 8
SEQ = 2048
DEPTH = 4

MEM_LEN = 256
N_EVEN = (DEPTH + 1) // 2
N_ODD = DEPTH // 2
EPS = 1e-6
Q_BLOCK = 128

SB_HEADS = 8
SB_HEAD_DIM = 128
GDN_HEADS = 8
GDN_DK = 128
GDN_DV = 128
GDN_CONV = 4
GDN_CHUNK = 64
MLA_HEADS = 8
MLA_Q_RANK = 512
MLA_KV_RANK = 512
MLA_NOPE = 128
MLA_ROPE = 64
MLA_V = 128
ROPE_THETA = 10000.0
GLA_HEADS = 4
GLA_DK = 128
GLA_DV = 256
GLA_GATE_RANK = 16
GLA_GATE_TAU = 16.0
GLA_CHUNK = 16
XA_HEADS = 4
XA_HEAD_DIM = 128
D_FF = 5632
FFN_CONV = 3

SB_W = SB_HEADS * SB_HEAD_DIM
GDN_KW = GDN_HEADS * GDN_DK
GDN_VW = GDN_HEADS * GDN_DV
GDN_QKV_W = 2 * GDN_KW + GDN_VW
EVEN_IN = 3 * SB_W + GDN_QKV_W + 2 * GDN_HEADS + GDN_VW
EVEN_MIX = SB_W + GDN_VW
MLA_QK = MLA_NOPE + MLA_ROPE
GLA_KW = GLA_HEADS * GLA_DK
GLA_VW = GLA_HEADS * GLA_DV
ODD_IN = MLA_Q_RANK + MLA_KV_RANK + MLA_ROPE + 2 * GLA_KW + GLA_VW + GLA_GATE_RANK + GLA_VW
ODD_MIX = MLA_HEADS * MLA_V + GLA_VW
XA_W = XA_HEADS * XA_HEAD_DIM

kernel_name = 'hybrid_sb_gdn_mla_gla_trunk'


def rmsnorm(x, g):
    xf = x.astype(jnp.float32)
    y = xf * lax.rsqrt(jnp.mean(xf * xf, axis=-1, keepdims=True) + EPS)
    return (y * g.astype(jnp.float32)).astype(x.dtype)


def l2norm(x):
    xf = x.astype(jnp.float32)
    return xf * lax.rsqrt(jnp.sum(xf * xf, axis=-1, keepdims=True) + EPS)


def split_cols(x, sizes):
    return jnp.split(x, [int(i) for i in np.cumsum(sizes)[:-1]], axis=-1)


def split_heads(x, n_heads):
    b, s, _ = x.shape
    return x.reshape(b, s, n_heads, -1).transpose(0, 2, 1, 3)


def merge_heads(x):
    b, h, s, d = x.shape
    return x.transpose(0, 2, 1, 3).reshape(b, s, h * d)


def causal_dwconv(x, w):
    k = w.shape[0]
    return lax.conv_general_dilated(x, w[:, None, :].astype(x.dtype), window_strides=(1,), padding=[(k - 1, 0)], dimension_numbers=('NWC', 'WIO', 'NWC'), feature_group_count=x.shape[-1])


def rope(x, cos, sin):
    half = x.shape[-1] // 2
    xf = x.astype(jnp.float32)
    x1, x2 = xf[..., :half], xf[..., half:]
    return jnp.concatenate([x1 * cos - x2 * sin, x2 * cos + x1 * sin], axis=-1).astype(x.dtype)


def stick_breaking_attention(q, k, v):
    s_len, d = q.shape[2], q.shape[3]
    scale = d ** -0.5
    outs = []
    for blk in range(s_len // Q_BLOCK):
        q0, q1 = blk * Q_BLOCK, (blk + 1) * Q_BLOCK
        z = jnp.einsum('bhqd,bhkd->bhqk', q[:, :, q0:q1], k[:, :, :q1]).astype(jnp.float32) * scale
        strict = jnp.arange(q1)[None, :] < (q0 + jnp.arange(Q_BLOCK))[:, None]
        log_1m_beta = jnp.where(strict, jax.nn.log_sigmoid(-z), 0.0)
        suffix = lax.cumsum(log_1m_beta, axis=3, reverse=True) - log_1m_beta
        w = jnp.where(strict, jnp.exp(jax.nn.log_sigmoid(z) + suffix), 0.0)
        outs.append(jnp.einsum('bhqk,bhkd->bhqd', w.astype(v.dtype), v[:, :, :q1]))
    return jnp.concatenate(outs, axis=2)


def causal_softmax_attention(q, k, v, scale):
    s_len = q.shape[2]
    outs = []
    for blk in range(s_len // Q_BLOCK):
        q0, q1 = blk * Q_BLOCK, (blk + 1) * Q_BLOCK
        s = jnp.einsum('bhqd,bhkd->bhqk', q[:, :, q0:q1], k[:, :, :q1]).astype(jnp.float32) * scale
        mask = jnp.arange(q1)[None, :] <= (q0 + jnp.arange(Q_BLOCK))[:, None]
        p = jax.nn.softmax(jnp.where(mask, s, -jnp.inf), axis=-1)
        outs.append(jnp.einsum('bhqk,bhkd->bhqd', p.astype(v.dtype), v[:, :, :q1]))
    return jnp.concatenate(outs, axis=2)


def gated_delta_rule(q, k, v, g, beta):
    b, h, s_len, dk = q.shape
    dv = v.shape[-1]
    c = GDN_CHUNK
    n = s_len // c
    f32 = jnp.float32
    q = (q.astype(f32) * dk ** -0.5).reshape(b, h, n, c, dk)
    k = k.astype(f32).reshape(b, h, n, c, dk)
    v = v.astype(f32).reshape(b, h, n, c, dv)
    beta = beta.astype(f32).reshape(b, h, n, c)
    gc = jnp.cumsum(g.astype(f32).reshape(b, h, n, c), axis=-1)
    tri_incl = jnp.tril(jnp.ones((c, c), bool))
    tri_strict = jnp.tril(jnp.ones((c, c), bool), -1)
    decay = jnp.exp(jnp.where(tri_incl, gc[..., :, None] - gc[..., None, :], -jnp.inf))
    k_beta = k * beta[..., None]
    neg_n = -jnp.where(tri_strict, jnp.einsum('bhnid,bhnjd->bhnij', k_beta, k) * decay, 0.0)
    t_inv = jnp.eye(c, dtype=f32) + neg_n
    p = neg_n
    for _ in range(c.bit_length() - 2):
        p = p @ p
        t_inv = t_inv + t_inv @ p
    u = t_inv @ (v * beta[..., None])
    w = t_inv @ (k_beta * jnp.exp(gc)[..., None])
    q_g = q * jnp.exp(gc)[..., None]
    attn = jnp.einsum('bhnid,bhnjd->bhnij', q, k) * decay
    k_dec = k * jnp.exp(gc[..., -1:] - gc)[..., None]
    g_last = jnp.exp(gc[..., -1])

    def step(state, xs):
        q_c, k_c, u_c, w_c, a_c, gl = xs
        v_new = u_c - w_c @ state
        o = q_c @ state + a_c @ v_new
        state = state * gl[..., None, None] + jnp.swapaxes(k_c, -1, -2) @ v_new
        return state, o

    xs = tuple(jnp.moveaxis(t, 2, 0) for t in (q_g, k_dec, u, w, attn, g_last))
    _, o = lax.scan(step, jnp.zeros((b, h, dk, dv), f32), xs)
    return jnp.moveaxis(o, 0, 2).reshape(b, h, s_len, dv)


def gla_chunked(q, k, v, log_a):
    b, h, s_len, dk = q.shape
    dv = v.shape[-1]
    c = GLA_CHUNK
    n = s_len // c
    f32 = jnp.float32
    q = (q.astype(f32) * dk ** -0.5).reshape(b, h, n, c, dk)
    k = k.astype(f32).reshape(b, h, n, c, dk)
    v = v.astype(f32).reshape(b, h, n, c, dv)
    cum = jnp.cumsum(log_a.astype(f32).reshape(b, h, n, c, dk), axis=3)
    q_t = q * jnp.exp(cum)
    k_t = k * jnp.exp(-cum)
    tri = jnp.tril(jnp.ones((c, c), bool))
    attn = jnp.where(tri, jnp.einsum('bhnid,bhnjd->bhnij', q_t, k_t), 0.0)
    o_intra = attn @ v
    cum_last = cum[:, :, :, -1:, :]
    k_dec = k * jnp.exp(cum_last - cum)
    a_last = jnp.exp(cum_last[:, :, :, 0, :])

    def step(state, xs):
        q_c, k_c, v_c, a_c = xs
        o = q_c @ state
        state = state * a_c[..., None] + jnp.swapaxes(k_c, -1, -2) @ v_c
        return state, o

    xs = tuple(jnp.moveaxis(t, 2, 0) for t in (q_t, k_dec, v, a_last))
    _, o_inter = lax.scan(step, jnp.zeros((b, h, dk, dv), f32), xs)
    o = jnp.moveaxis(o_inter, 0, 2) + o_intra
    return o.reshape(b, h, s_len, dv)


def even_mixer(h, w_in, sconv_w, a_log, dt_bias, gdn_norm, w_out):
    sb_q, sb_k, sb_v, gdn_qkv, gdn_a, gdn_b, gdn_gate = split_cols(h @ w_in, [SB_W, SB_W, SB_W, GDN_QKV_W, GDN_HEADS, GDN_HEADS, GDN_VW])
    o_a = merge_heads(stick_breaking_attention(split_heads(sb_q, SB_HEADS), split_heads(sb_k, SB_HEADS), split_heads(sb_v, SB_HEADS)))
    gq, gk, gv = split_cols(jax.nn.silu(causal_dwconv(gdn_qkv, sconv_w)), [GDN_KW, GDN_KW, GDN_VW])
    log_decay = -jnp.exp(a_log.astype(jnp.float32)) * jax.nn.softplus(gdn_a.astype(jnp.float32) + dt_bias.astype(jnp.float32))
    beta = jax.nn.sigmoid(gdn_b.astype(jnp.float32))
    o_b = gated_delta_rule(l2norm(split_heads(gq, GDN_HEADS)), l2norm(split_heads(gk, GDN_HEADS)), split_heads(gv, GDN_HEADS), log_decay.transpose(0, 2, 1), beta.transpose(0, 2, 1))
    o_b = rmsnorm(o_b, gdn_norm) * jax.nn.silu(split_heads(gdn_gate, GDN_HEADS).astype(jnp.float32))
    mix = jnp.concatenate([o_a, merge_heads(o_b).astype(h.dtype)], axis=-1)
    return mix @ w_out


def odd_mixer(h, positions, w_in, q_norm, kv_norm, w_uq, w_ukv, gla_w2, gla_b2, gla_norm, w_out):
    b, s, _ = h.shape
    c_q, c_kv, k_rope, lq, lk, lv, lg, lr = split_cols(h @ w_in, [MLA_Q_RANK, MLA_KV_RANK, MLA_ROPE, GLA_KW, GLA_KW, GLA_VW, GLA_GATE_RANK, GLA_VW])
    q = (rmsnorm(c_q, q_norm) @ w_uq).reshape(b, s, MLA_HEADS, MLA_QK)
    kv = (rmsnorm(c_kv, kv_norm) @ w_ukv).reshape(b, s, MLA_HEADS, MLA_NOPE + MLA_V)
    inv_freq = ROPE_THETA ** (-jnp.arange(0, MLA_ROPE, 2, dtype=jnp.float32) / MLA_ROPE)
    ang = positions.astype(jnp.float32)[..., None] * inv_freq
    cos, sin = jnp.cos(ang), jnp.sin(ang)
    q_pe = rope(q[..., MLA_NOPE:], cos[:, :, None], sin[:, :, None])
    k_pe = rope(k_rope, cos, sin)
    qh = jnp.concatenate([q[..., :MLA_NOPE], q_pe], axis=-1).transpose(0, 2, 1, 3)
    kh = jnp.concatenate([kv[..., :MLA_NOPE], jnp.broadcast_to(k_pe[:, :, None, :], (b, s, MLA_HEADS, MLA_ROPE))], axis=-1).transpose(0, 2, 1, 3)
    vh = kv[..., MLA_NOPE:].transpose(0, 2, 1, 3)
    o_c = merge_heads(causal_softmax_attention(qh, kh, vh, MLA_QK ** -0.5))
    log_a = jax.nn.log_sigmoid((lg @ gla_w2 + gla_b2).astype(jnp.float32)) / GLA_GATE_TAU
    o_d = gla_chunked(split_heads(lq, GLA_HEADS), split_heads(lk, GLA_HEADS), split_heads(lv, GLA_HEADS), split_heads(log_a, GLA_HEADS))
    o_d = rmsnorm(o_d, gla_norm) * jax.nn.silu(split_heads(lr, GLA_HEADS).astype(jnp.float32))
    mix = jnp.concatenate([o_c, merge_heads(o_d).astype(h.dtype)], axis=-1)
    return mix @ w_out


def memory_cross_attention(h, mem_n, wq, wk, wv, wo):
    q = split_heads(h @ wq, XA_HEADS)
    k = split_heads(mem_n @ wk, XA_HEADS)
    v = split_heads(mem_n @ wv, XA_HEADS)
    s = jnp.einsum('bhqd,bhkd->bhqk', q, k).astype(jnp.float32) * XA_HEAD_DIM ** -0.5
    p = jax.nn.softmax(s, axis=-1)
    return merge_heads(jnp.einsum('bhqk,bhkd->bhqd', p.astype(v.dtype), v)) @ wo


def conv_ffn(h, w_in, conv_w, conv_b, w_out):
    uz = causal_dwconv(h @ w_in, conv_w) + conv_b
    u, z = jnp.split(uz, 2, axis=-1)
    return (jax.nn.silu(z) * u) @ w_out


def setup_inputs(seed: int = 0) -> dict:
    key = jax.random.key(seed)
    ks = iter(jax.random.split(key, 64))
    res_scale = (3 * DEPTH) ** -0.5

    def nrm(shape, scale):
        return jax.random.normal(next(ks), shape, jnp.float32) * scale

    def gain(shape):
        return 1.0 + nrm(shape, 0.02)

    dt = jnp.exp(jax.random.uniform(next(ks), (N_EVEN, GDN_HEADS), jnp.float32, minval=float(np.log(1e-3)), maxval=float(np.log(1e-1))))
    return {
        'x': nrm((BATCH, SEQ, D_MODEL), 1.0),
        'mem': nrm((BATCH, MEM_LEN, D_MODEL), 1.0),
        'positions': jnp.arange(SEQ, dtype=jnp.int32)[None, :] + jax.random.randint(next(ks), (BATCH, 1), 0, 4096, dtype=jnp.int32),
        'norm_mix': gain((DEPTH, D_MODEL)),
        'norm_xattn': gain((DEPTH, D_MODEL)),
        'norm_ffn': gain((DEPTH, D_MODEL)),
        'mem_norm': gain((D_MODEL,)),
        'final_norm': gain((D_MODEL,)),
        'ev_w_in': nrm((N_EVEN, D_MODEL, EVEN_IN), D_MODEL ** -0.5),
        'ev_sconv': nrm((N_EVEN, GDN_CONV, GDN_QKV_W), GDN_CONV ** -0.5),
        'ev_a_log': jnp.log(jax.random.uniform(next(ks), (N_EVEN, GDN_HEADS), jnp.float32, minval=1.0, maxval=16.0)),
        'ev_dt_bias': dt + jnp.log(-jnp.expm1(-dt)),
        'ev_gdn_norm': gain((N_EVEN, GDN_DV)),
        'ev_w_out': nrm((N_EVEN, EVEN_MIX, D_MODEL), EVEN_MIX ** -0.5 * res_scale),
        'od_w_in': nrm((N_ODD, D_MODEL, ODD_IN), D_MODEL ** -0.5),
        'od_q_norm': gain((N_ODD, MLA_Q_RANK)),
        'od_kv_norm': gain((N_ODD, MLA_KV_RANK)),
        'od_w_uq': nrm((N_ODD, MLA_Q_RANK, MLA_HEADS * MLA_QK), MLA_Q_RANK ** -0.5),
        'od_w_ukv': nrm((N_ODD, MLA_KV_RANK, MLA_HEADS * (MLA_NOPE + MLA_V)), MLA_KV_RANK ** -0.5),
        'od_gla_w2': nrm((N_ODD, GLA_GATE_RANK, GLA_KW), GLA_GATE_RANK ** -0.5),
        'od_gla_b2': nrm((N_ODD, GLA_KW), 0.01),
        'od_gla_norm': gain((N_ODD, GLA_DV)),
        'od_w_out': nrm((N_ODD, ODD_MIX, D_MODEL), ODD_MIX ** -0.5 * res_scale),
        'xa_wq': nrm((DEPTH, D_MODEL, XA_W), D_MODEL ** -0.5),
        'xa_wk': nrm((DEPTH, D_MODEL, XA_W), D_MODEL ** -0.5),
        'xa_wv': nrm((DEPTH, D_MODEL, XA_W), D_MODEL ** -0.5),
        'xa_wo': nrm((DEPTH, XA_W, D_MODEL), XA_W ** -0.5 * res_scale),
        'ffn_w_in': nrm((DEPTH, D_MODEL, 2 * D_FF), D_MODEL ** -0.5),
        'ffn_conv': nrm((DEPTH, FFN_CONV, 2 * D_FF), FFN_CONV ** -0.5),
        'ffn_conv_b': nrm((DEPTH, 2 * D_FF), 0.01),
        'ffn_w_out': nrm((DEPTH, D_FF, D_MODEL), D_FF ** -0.5 * res_scale),
    }


def reference(x, mem, positions, norm_mix, norm_xattn, norm_ffn, mem_norm, final_norm, ev_w_in, ev_sconv, ev_a_log, ev_dt_bias, ev_gdn_norm, ev_w_out, od_w_in, od_q_norm, od_kv_norm, od_w_uq, od_w_ukv, od_gla_w2, od_gla_b2, od_gla_norm, od_w_out, xa_wq, xa_wk, xa_wv, xa_wo, ffn_w_in, ffn_conv, ffn_conv_b, ffn_w_out):
    mem_n = rmsnorm(mem, mem_norm)
    h = x
    for layer in range(DEPTH):
        hn = rmsnorm(h, norm_mix[layer])
        if layer % 2 == 0:
            e = layer // 2
            h = h + even_mixer(hn, ev_w_in[e], ev_sconv[e], ev_a_log[e], ev_dt_bias[e], ev_gdn_norm[e], ev_w_out[e])
        else:
            o = layer // 2
            h = h + odd_mixer(hn, positions, od_w_in[o], od_q_norm[o], od_kv_norm[o], od_w_uq[o], od_w_ukv[o], od_gla_w2[o], od_gla_b2[o], od_gla_norm[o], od_w_out[o])
        h = h + memory_cross_attention(rmsnorm(h, norm_xattn[layer]), mem_n, xa_wq[layer], xa_wk[layer], xa_wv[layer], xa_wo[layer])
        h = h + conv_ffn(rmsnorm(h, norm_ffn[layer]), ffn_w_in[layer], ffn_conv[layer], ffn_conv_b[layer], ffn_w_out[layer])
    return rmsnorm(h, final_norm)
```

```python
import numpy as np
import concourse.bass as bass
import concourse.mybir as mybir
from concourse.bass_utils import run_bass_kernel_spmd

F32 = mybir.dt.float32
BF16 = mybir.dt.bfloat16
I32 = mybir.dt.int32
AF = mybir.ActivationFunctionType
ALU = mybir.AluOpType
AX = mybir.AxisListType
P = 128
S = 2048
D = 2048
DEPTH = 4
MEM = 256
DFF = 5632
EPS = 1e-6
EVEN_IN = 7184
ODD_IN = 4176
SEG_OPS = 10 ** 9
ENGS = ['pe', 'act', 'dve', 'pool', 'sp']


class Buf:
    def __init__(self, name):
        self.name = name
        self.writers = {}
        self.readers = {}
        self.sem = None
        self.dcount = 0
        self.epoch = {}


class Op:
    __slots__ = ('eng', 'fn', 'signal', 'dma', 'key', 'sigval', 'deps', 'idx', 'inc', 'seg', 'dname')


class KB:
    def __init__(self, nc):
        self.nc = nc
        self.ops = {e: [] for e in ENGS}
        self.pending = {e: {} for e in ENGS}
        self.all_last = {}
        self.n = 0
        self.dcount = {}
        self.seg = 0
        self.seg_start_idx = 0

    def op(self, eng, fn, r=(), w=(), dma=None, accum=False, inc=16):
        o = Op()
        o.eng = eng
        o.fn = fn
        o.signal = False
        o.dma = dma
        o.inc = inc
        o.idx = self.n
        o.seg = self.seg
        self.n += 1
        if dma is not None:
            o.dname = dma.name + '_' + eng
            self.dcount[o.dname] = self.dcount.get(o.dname, 0) + 1
            o.key = ('d', o.dname)
            o.sigval = inc * self.dcount[o.dname]
            o.signal = True
        else:
            o.key = eng
            o.sigval = None
        deps = self.pending[eng]
        self.pending[eng] = {}

        def add(d):
            if d.key == 'pe' and o.key == 'pe':
                return
            c = deps.get(d.key)
            if c is None or c.idx < d.idx:
                deps[d.key] = d
        for b in r:
            for d in b.writers.values():
                add(d)
        for b in w:
            for d in b.readers.values():
                add(d)
            if accum:
                for d in b.epoch.values():
                    add(d)
            else:
                for d in b.writers.values():
                    add(d)
        o.deps = deps
        for b in w:
            if accum:
                b.epoch.update(b.readers)
                b.writers[o.key] = o
            else:
                ep = dict(b.writers)
                ep.update(b.readers)
                b.epoch = ep
                b.writers = {o.key: o}
            b.readers = {}
        for b in r:
            b.readers[o.key] = o
        self.ops[eng].append(o)
        self.all_last[o.key] = o
        return o

    def barrier(self):
        for e in ENGS:
            p = self.pending[e]
            for k, d in self.all_last.items():
                c = p.get(k)
                if c is None or c.idx < d.idx:
                    p[k] = d
        if self.n - self.seg_start_idx > SEG_OPS:
            self.seg += 1
            self.seg_start_idx = self.n

    def emit(self):
        nc = self.nc
        self.barrier()
        self.op('sp', lambda e: e.nop())
        nseg = self.seg + 1
        esem = {e: nc.alloc_semaphore('es_' + e) for e in ENGS}
        dsem = {n: nc.alloc_semaphore('ds_%s' % n) for n in self.dcount}
        allsems = list(esem.values()) + list(dsem.values())
        print('n dma sems', len(dsem), 'n ops', {e: len(self.ops[e]) for e in ENGS}, 'segments', nseg)
        segops = [{e: [] for e in ENGS} for _ in range(nseg)]
        for e in ENGS:
            for o in self.ops[e]:
                segops[o.seg][e].append(o)

        def semof(d):
            if d.dma is not None:
                return dsem[d.dname]
            return esem[d.eng]
        for sg in range(nseg):
            ops = segops[sg]
            for e in ENGS:
                for o in ops[e]:
                    o.deps = {k: d for k, d in o.deps.items() if d.seg == sg}
                    for d in o.deps.values():
                        d.signal = True
            dcnt = {}
            lastd = {}
            for e in ENGS:
                cnt = 0
                for o in ops[e]:
                    if o.dma is None:
                        if o.signal:
                            cnt += 1
                            o.sigval = cnt
            dmaops = sorted([o for e in ENGS for o in ops[e] if o.dma is not None], key=lambda o: o.idx)
            for o in dmaops:
                dcnt[o.dname] = dcnt.get(o.dname, 0) + 1
                o.sigval = o.inc * dcnt[o.dname]
                lastd[o.dname] = o
            for sm in allsems:
                nc.sync.sem_clear(sm)
            nc.all_engine_barrier()

            def body(engname, ops=ops, lastd=lastd):
                def f(e):
                    waited = {}
                    for o in ops[engname]:
                        for k, d in o.deps.items():
                            v = d.sigval
                            if waited.get(k, 0) < v:
                                e.wait_ge(semof(d), v)
                                waited[k] = v
                        ins = o.fn(e)
                        if o.dma is not None:
                            ins.then_inc(dsem[o.dname], o.inc)
                        elif o.signal:
                            ins.then_inc(esem[engname], 1)
                    if engname == 'sp':
                        for n_, d in lastd.items():
                            if waited.get(d.key, 0) < d.sigval:
                                e.wait_ge(dsem[n_], d.sigval)
                return f
            with nc.Block() as block:
                block.tensor(body('pe'))
                block.scalar(body('act'))
                block.vector(body('dve'))
                block.gpsimd(body('pool'))
                block.sync(body('sp'))


class Arena:
    def __init__(self, tensor, nwords):
        self.t = tensor
        self.n = nwords
        self.top = 0
        self.marks = []

    def alloc(self, nwords):
        nwords = (nwords + 7) // 8 * 8
        off = self.top
        self.top += nwords
        assert self.top <= self.n, ('SBUF arena overflow', self.top, self.n)
        return off

    def f32(self, n):
        off = self.alloc(n)
        return self.t[:, off:off + n]

    def bf16(self, n):
        off = self.alloc((n + 1) // 2)
        return self.t[:, off:off + (n + 1) // 2].bitcast(BF16)[:, 0:n]

    def push(self):
        self.marks.append(self.top)

    def pop(self):
        self.top = self.marks.pop()


def _rows128(a):
    a = np.ascontiguousarray(a, dtype=np.float32).reshape(-1)
    assert a.size % 128 == 0
    return a.reshape(-1, 128)


SMALL_KEYS = ['norm_mix', 'norm_xattn', 'norm_ffn', 'mem_norm', 'final_norm', 'ffn_conv', 'ffn_conv_b',
              'ev_sconv', 'od_q_norm', 'od_kv_norm', 'od_gla_b2', 'ev_gdn_norm', 'od_gla_norm']
SMALL_SIZES = {'norm_mix': 4 * 2048, 'norm_xattn': 4 * 2048, 'norm_ffn': 4 * 2048, 'mem_norm': 2048,
               'final_norm': 2048, 'ffn_conv': 4 * 3 * 11264, 'ffn_conv_b': 4 * 11264, 'ev_sconv': 2 * 4 * 3072,
               'od_q_norm': 2 * 512, 'od_kv_norm': 2 * 512, 'od_gla_b2': 2 * 512, 'ev_gdn_norm': 2 * 128,
               'od_gla_norm': 2 * 256}
SMALL_OFF = {}
_o = 0
for _k in SMALL_KEYS:
    SMALL_OFF[_k] = _o
    _o += SMALL_SIZES[_k] // 128
SMALL_ROWS = _o
SMALL_ROWS_PAD = (SMALL_ROWS + 127) // 128 * 128

C_IDENT = 0
C_ONES = 128
C_LOW = 256
C_SLOW = 384
C_UP = 512
C_SUP = 640
C_NEGUP = 768
C_SEL = 896
NCONST = 896 + 1024


def make_consts():
    c = np.zeros((128, NCONST), np.float32)
    p = np.arange(128)[:, None]
    x = np.arange(128)[None, :]
    c[:, C_IDENT:C_IDENT + 128] = (p == x)
    c[:, C_ONES:C_ONES + 128] = 1.0
    c[:, C_LOW:C_LOW + 128] = (p >= x)
    c[:, C_SLOW:C_SLOW + 128] = (p > x)
    c[:, C_UP:C_UP + 128] = (p <= x)
    c[:, C_SUP:C_SUP + 128] = (p < x)
    c[:, C_NEGUP:C_NEGUP + 128] = np.where(p >= x, 0.0, -30000.0)
    for h in range(8):
        c[h, C_SEL + h * 128:C_SEL + (h + 1) * 128] = 1.0
    return c


BIGW = {
    'ev_w_in': (2, 2048, EVEN_IN), 'ev_w_out': (2, 2048, 2048), 'od_w_in': (2, 2048, ODD_IN),
    'od_w_uq': (2, 512, 1536), 'od_w_ukv': (2, 512, 2048), 'od_w_out': (2, 2048, 2048),
    'xa_wq': (4, 2048, 512), 'xa_wk': (4, 2048, 512), 'xa_wv': (4, 2048, 512), 'xa_wo': (4, 512, 2048),
    'ffn_w_in': (4, 2048, 2 * DFF), 'ffn_w_out': (4, DFF, 2048),
}


def layer_weights(layer):
    names = ['ev_w_in', 'ev_w_out'] if layer % 2 == 0 else ['od_w_in', 'od_w_uq', 'od_w_ukv', 'od_w_out']
    names += ['xa_wq', 'xa_wk', 'xa_wv', 'xa_wo', 'ffn_w_in', 'ffn_w_out']
    return [(n, layer // 2 if n[:2] in ('ev', 'od') else layer) for n in names]


def build(ncores, layers=(0, 1, 2, 3), do_mixer=True, do_xattn=True, do_ffn=True, dbg=None, GDN_ON=True, GL=9, GH=8, GLA_ON=True):
    nc = bass.Bass("TRN2", target_bir_lowering=False)
    kb = KB(nc)
    x_in = nc.dram_tensor("x", [S, D], F32, kind="ExternalInput").ap()
    mem_in = nc.dram_tensor("mem", [MEM, D], F32, kind="ExternalInput").ap()
    pos_in = nc.dram_tensor("positions", [1, S], I32, kind="ExternalInput").ap()
    small_in = nc.dram_tensor("smallp", [SMALL_ROWS_PAD, 128], F32, kind="ExternalInput").ap()
    const_in = nc.dram_tensor("consts", [128, NCONST], F32, kind="ExternalInput").ap()
    hp_in = nc.dram_tensor("hp", [8, 4], F32, kind="ExternalInput").ap()
    w2_in = nc.dram_tensor("gla_w2", [16, 2, 512], F32, kind="ExternalInput").ap()
    invf_in = nc.dram_tensor("invf", [32, 1], F32, kind="ExternalInput").ap()
    out_t = nc.dram_tensor("out", [S, D], F32, kind="ExternalOutput").ap()

    W = {}
    wbufs = {}
    needed = []
    for l in layers:
        for (n, li) in layer_weights(l):
            if (n, li) not in needed:
                needed.append((n, li))
    cc_list = []
    wl = {}
    for l in layers:
        for it in layer_weights(l):
            wl.setdefault(it, l)
    LWB = {l: Buf('ccL%d' % l) for l in layers}
    B_bounce = Buf('bounce')
    for (n, li) in needed:
        L, K, N = BIGW[n]
        if ncores == 1:
            t = nc.dram_tensor("%s_%d" % (n, li), [K, N], F32, kind="ExternalInput").ap()
            W[(n, li)] = t
            wbufs[(n, li)] = Buf('w1')
        else:
            sh = nc.dram_tensor("%s_%d" % (n, li), [K // ncores, N], F32, kind="ExternalInput").ap()
            bo = nc.dram_tensor("b_%s_%d" % (n, li), [K // ncores, N], F32).ap()
            g = nc.dram_tensor("g_%s_%d" % (n, li), [K, N], F32).ap()
            W[(n, li)] = g
            wbufs[(n, li)] = LWB[wl[(n, li)]]
            cc_list.append((sh, bo, g, wbufs[(n, li)], B_bounce))

    hT = nc.dram_tensor("hT", [D, S], F32).ap()
    B_hT = Buf('hT')
    gT = nc.dram_tensor("gT", [DFF, S], BF16).ap()
    B_gT = Buf('gT')
    sbqk = nc.dram_tensor("sbqk", [2048, S], BF16).ap()
    B_sbqk = Buf('sbqk')
    sbv = nc.dram_tensor("sbv", [S, 1024], BF16).ap()
    B_sbv = Buf('sbv')
    gqkv = nc.dram_tensor("gqkv", [3072, S], F32).ap()
    B_gqkv = Buf('gqkv')
    cqkv = nc.dram_tensor("cqkv", [1024, S], F32).ap()
    B_cqkv = Buf('cqkv')
    glqk = nc.dram_tensor("glqk", [1024, S], F32).ap()
    B_glqk = Buf('glqk')
    glv = nc.dram_tensor("glv", [S, 1024], F32).ap()
    B_glv = Buf('glv')
    mqr = nc.dram_tensor("mqr", [512, S], BF16).ap()
    B_mqr = Buf('mqr')
    gateT = nc.dram_tensor("gateT", [1024, S], BF16).ap()
    B_gateT = Buf('gateT')

    NW = 53200
    arena_t = nc.alloc_sbuf_tensor("arena", [P, NW], F32)
    ar = Arena(arena_t, NW)
    ps_t = nc.alloc_psum_tensor("ps", [P, 4096], F32)
    PSB = [Buf('psb%d' % i) for i in range(8)]

    def psv(b0, nb=1):
        return ps_t[:, b0 * 512:(b0 + nb) * 512]

    bounce_sem = Buf('bounce')
    for (sh, bo, g, wb, bb) in cc_list:
        kb.op('pool', lambda e, sh=sh, bo=bo: e.dma_start(out=bo, in_=sh), w=[bb], dma=bb, accum=True)
    for (sh, bo, g, wb, bb) in cc_list:
        kb.op('pool', lambda e, bo=bo, g=g: e.collective_compute(
            "AllGather", ALU.bypass, replica_groups=[list(range(ncores))], ins=[bo.opt()], outs=[g.opt()]),
            r=[bb], w=[wb], dma=wb, inc=1, accum=True)

    consts = ar.f32(NCONST)
    B_const = Buf('consts')
    kb.op('sp', lambda e: e.dma_start(out=consts, in_=const_in), w=[B_const], dma=B_const)
    ident = consts[:, C_IDENT:C_IDENT + 128]
    ones = consts[:, C_ONES:C_ONES + 128]
    identb = ar.bf16(128)
    B_identb = Buf('identb')
    kb.op('dve', lambda e: e.tensor_copy(out=identb, in_=ident), r=[B_const], w=[B_identb])
    smallc = ar.f32(SMALL_ROWS_PAD)
    B_small = Buf('small')
    ar.push()
    stg = ar.f32(128)
    B_stg = Buf('stg')
    for i in range(SMALL_ROWS_PAD // 128):
        kb.op('sp', lambda e, i=i: e.dma_start(out=stg, in_=small_in[i * 128:(i + 1) * 128, :]), w=[B_stg], dma=B_stg)
        kb.op('pe', lambda e: e.transpose(psv(0)[:, 0:128], stg, ident), r=[B_stg, B_const], w=[PSB[0]])
        kb.op('dve', lambda e, i=i: e.tensor_copy(out=smallc[:, i * 128:(i + 1) * 128], in_=psv(0)[:, 0:128]),
              r=[PSB[0]], w=[B_small], accum=True)
    ar.pop()

    def scol(key, idx):
        c = SMALL_OFF[key] + idx
        return smallc[:, c:c + 1]

    memnT = ar.bf16(16 * MEM).rearrange("p (c t) -> p c t", c=16)
    B_memnT = Buf('memnT')

    dbg_t = nc.dram_tensor("dbg", [128, 8 * 2048], F32, kind="ExternalOutput").ap() if dbg else None
    dstage = ar.f32(2048) if dbg else None
    B_dstage = Buf('dstage')
    dbg_slots = {}

    def dump(name, ap, bufs):
        if not dbg or name not in dbg:
            return
        n = ap.shape[-1]
        slot = len(dbg_slots)
        dbg_slots[name] = (slot, n)
        kb.barrier()
        kb.op('dve', lambda e: e.tensor_copy(out=dstage[0:ap.shape[0], 0:n], in_=ap), r=bufs, w=[B_dstage])
        kb.op('sp', lambda e: e.dma_start(out=dbg_t[0:ap.shape[0], slot * 2048:slot * 2048 + n], in_=dstage[0:ap.shape[0], 0:n]), r=[B_dstage], w=[Buf('dbgd')], dma=B_dstage)
        kb.barrier()
    build.dbg_slots = dbg_slots

    def to_featmajor(src, ntok, dstT, B_dst):
        kb.barrier()
        ar.push()
        xin = [ar.f32(D), ar.f32(D)]
        B_xin = [Buf('xin0'), Buf('xin1')]
        st = [ar.f32(16 * 128), ar.f32(16 * 128)]
        B_st = [Buf('st0'), Buf('st1')]
        dT = dstT.rearrange("(c p) t -> p c t", p=128)
        for tb in range(ntok // 128):
            sl = tb % 2
            kb.op('sp', lambda e, sl=sl, tb=tb: e.dma_start(out=xin[sl], in_=src[tb * 128:(tb + 1) * 128, :]),
                  w=[B_xin[sl]], dma=B_xin[sl])
            for q in range(4):
                bk = (tb * 4 + q) % 8
                for j in range(4):
                    fc = q * 4 + j
                    kb.op('pe', lambda e, sl=sl, fc=fc, bk=bk, j=j: e.transpose(
                        psv(bk)[:, j * 128:(j + 1) * 128], xin[sl][:, fc * 128:(fc + 1) * 128], ident),
                        r=[B_xin[sl], B_const], w=[PSB[bk]])
                eng = 'act' if q % 2 == 0 else 'dve'
                if eng == 'act':
                    kb.op('act', lambda e, sl=sl, q=q, bk=bk: e.copy(st[sl][:, q * 512:(q + 1) * 512], psv(bk)),
                          r=[PSB[bk]], w=[B_st[sl]], accum=(q > 0))
                else:
                    kb.op('dve', lambda e, sl=sl, q=q, bk=bk: e.tensor_copy(out=st[sl][:, q * 512:(q + 1) * 512], in_=psv(bk)),
                          r=[PSB[bk]], w=[B_st[sl]], accum=True)
            kb.op('pool', lambda e, sl=sl, tb=tb: e.dma_start(
                out=dT[:, :, tb * 128:(tb + 1) * 128], in_=st[sl].rearrange("p (c t) -> p c t", c=16)),
                r=[B_st[sl]], w=[B_dst], dma=B_st[sl], accum=True)
        ar.pop()

    def norm_T(srcT, B_src, KC, gcol, dst, B_dst, ncols=S, eps_div=None):
        kb.barrier()
        ar.push()
        TW = min(512, ncols)
        xt = [ar.f32(KC * TW).rearrange("p (c t) -> p c t", c=KC) for _ in range(2)]
        B_xt = [Buf('nxt0'), Buf('nxt1')]
        sq = [ar.f32(TW), ar.f32(TW)]
        B_sq = [Buf('nsq0'), Buf('nsq1')]
        rs = [ar.f32(TW), ar.f32(TW)]
        B_rs = [Buf('nrs0'), Buf('nrs1')]
        sT = srcT.rearrange("(c p) t -> p c t", p=128)
        F = KC * 128
        for t in range(ncols // TW):
            sl = t % 2
            kb.op('sp', lambda e, sl=sl, t=t: e.dma_start(out=xt[sl], in_=sT[:, :, t * TW:(t + 1) * TW]),
                  r=[B_src], w=[B_xt[sl]], dma=B_xt[sl])
            bk = sl
            for c in range(KC):
                s2 = c % 2
                kb.op('act', lambda e, sl=sl, c=c, s2=s2: e.activation(out=sq[s2], in_=xt[sl][:, c, :], func=AF.Square),
                      r=[B_xt[sl]], w=[B_sq[s2]])
                kb.op('pe', lambda e, c=c, s2=s2, bk=bk: e.matmul(psv(bk)[:, 0:TW], ones, sq[s2], start=(c == 0), stop=(c == KC - 1)),
                      r=[B_sq[s2], B_const], w=[PSB[bk]])
            kb.op('act', lambda e, sl=sl, bk=bk: e.activation(out=rs[sl], in_=psv(bk)[:, 0:TW], func=AF.Sqrt, scale=1.0 / F, bias=epsc),
                  r=[PSB[bk], B_eps], w=[B_rs[sl]])
            kb.op('dve', lambda e, sl=sl: e.reciprocal(out=rs[sl], in_=rs[sl]), r=[B_rs[sl]], w=[B_rs[sl]])
            for c in range(KC):
                kb.op('dve', lambda e, sl=sl, c=c, t=t: e.scalar_tensor_tensor(
                    out=dst[:, c, t * TW:(t + 1) * TW], in0=xt[sl][:, c, :], scalar=gcol(c), in1=rs[sl],
                    op0=ALU.mult, op1=ALU.mult), r=[B_xt[sl], B_rs[sl], B_small], w=[B_dst], accum=True)
        ar.pop()

    epsc = ar.f32(1)
    B_eps = Buf('eps')
    kb.op('pool', lambda e: e.memset(epsc, EPS), w=[B_eps])

    def linear_T(xT, B_x, KC, Wap, B_w, pieces, epi, ntok=S, wdt=BF16, SW=256):
        kb.barrier()
        ar.push()
        wf = [ar.f32(KC * SW).rearrange("p (c n) -> p c n", c=KC) for _ in range(2)]
        B_wf = [Buf('wf0'), Buf('wf1')]
        wb = [ar.bf16(KC * SW).rearrange("p (c n) -> p c n", c=KC) for _ in range(2)]
        B_wb = [Buf('wb0'), Buf('wb1')]
        Wr = Wap.rearrange("(c p) n -> p c n", p=128)
        slabs = []
        for pi, (c0, n) in enumerate(pieces):
            if slabs and slabs[-1][0] + slabs[-1][1] == c0 and slabs[-1][1] + n <= SW:
                slabs[-1][1] += n
                slabs[-1][2].append(pi)
            else:
                slabs.append([c0, n, [pi]])
        nt = (ntok + 511) // 512
        cnt = 0
        for si, (c0, n, pis) in enumerate(slabs):
            sl = si % 2
            kb.op('sp', lambda e, sl=sl, c0=c0, n=n: e.dma_start(out=wf[sl][:, :, 0:n], in_=Wr[:, :, c0:c0 + n]),
                  r=[B_w], w=[B_wf[sl]], dma=B_wf[sl])
            kb.op('pool', lambda e, sl=sl, n=n: e.tensor_copy(out=wb[sl][:, :, 0:n], in_=wf[sl][:, :, 0:n]),
                  r=[B_wf[sl]], w=[B_wb[sl]])
            for pi in pis:
                pc0, pn = pieces[pi]
                lo = pc0 - c0
                half = cnt % 2
                cnt += 1
                for t in range(nt):
                    tw = min(512, ntok - t * 512)
                    bk = half * 4 + t
                    for kc in range(KC):
                        kb.op('pe', lambda e, sl=sl, kc=kc, lo=lo, pn=pn, bk=bk, t=t, tw=tw: e.matmul(
                            psv(bk)[0:pn, 0:tw], wb[sl][:, kc, lo:lo + pn], xT[:, kc, t * 512:t * 512 + tw],
                            start=(kc == 0), stop=(kc == KC - 1)), r=[B_wb[sl], B_x], w=[PSB[bk]])
                epi(pi, half)
        ar.pop()

    def ps_half(half, n=P, ntok=S):
        return ps_t[0:n, half * 2048:half * 2048 + ntok]

    def PSH(half):
        return PSB[half * 4:half * 4 + 4]

    to_featmajor(x_in, S, hT, B_hT)
    memT = nc.dram_tensor("memT", [D, MEM], F32).ap()
    B_memT = Buf('memT')
    to_featmajor(mem_in, MEM, memT, B_memT)
    norm_T(memT, B_memT, 16, lambda c: scol('mem_norm', c), memnT, B_memnT, ncols=MEM)

    actbuf_off = ar.alloc(22 * S // 2)
    actbuf = arena_t[:, actbuf_off:actbuf_off + 22 * S // 2].bitcast(BF16)
    B_act = Buf('actbuf')

    def act_view(KC, ntok=S):
        return actbuf[:, 0:KC * ntok].rearrange("p (c t) -> p c t", c=KC)

    def add_residual_epi(ar_bufs):
        hb, B_hb = ar_bufs
        hTr = hT.rearrange("(c p) t -> c p t", p=128)

        def epi(pi, half):
            sl = pi % 2
            kb.op('sp', lambda e, sl=sl, pi=pi: e.dma_start(out=hb[sl], in_=hTr[pi]), r=[B_hT], w=[B_hb[sl]], dma=B_hb[sl])
            kb.op('dve', lambda e, sl=sl, half=half: e.tensor_tensor(out=hb[sl], in0=ps_half(half), in1=hb[sl], op=ALU.add),
                  r=PSH(half) + [B_hb[sl]], w=[B_hb[sl]])
            kb.op('pool', lambda e, sl=sl, pi=pi: e.dma_start(out=hTr[pi], in_=hb[sl]), r=[B_hb[sl]], w=[B_hT], dma=B_hb[sl], accum=True)
        return epi


    def linear_tok(xT, B_x, KC, Wsel, B_w, ncols, ntok, epi, CW=256):
        kb.barrier()
        ar.push()
        wf = [ar.f32(KC * CW).rearrange("p (c n) -> p c n", c=KC) for _ in range(2)]
        B_wf = [Buf('wf0'), Buf('wf1')]
        wb = [ar.bf16(KC * CW).rearrange("p (c n) -> p c n", c=KC) for _ in range(2)]
        B_wb = [Buf('wb0'), Buf('wb1')]
        cnt = 0
        for cb in range((ncols + CW - 1) // CW):
            n = min(CW, ncols - cb * CW)
            sl = cb % 2
            parts = Wsel(cb * CW, n)
            if not isinstance(parts, list):
                parts = [(0, n, parts)]
            for ip, (poff, pw, pap) in enumerate(parts):
                kb.op('sp', lambda e, sl=sl, poff=poff, pw=pw, pap=pap: e.dma_start(out=wf[sl][:, :, poff:poff + pw], in_=pap),
                      r=[B_w], w=[B_wf[sl]], dma=B_wf[sl], accum=(ip > 0))
            kb.op('pool', lambda e, sl=sl, n=n: e.tensor_copy(out=wb[sl][:, :, 0:n], in_=wf[sl][:, :, 0:n]),
                  r=[B_wf[sl]], w=[B_wb[sl]])
            for tb in range(ntok // 128):
                bk = cnt % 8
                cnt += 1
                for kc in range(KC):
                    kb.op('pe', lambda e, sl=sl, kc=kc, n=n, bk=bk, tb=tb: e.matmul(
                        psv(bk)[:, 0:n], xT[:, kc, tb * 128:(tb + 1) * 128], wb[sl][:, kc, 0:n],
                        start=(kc == 0), stop=(kc == KC - 1)), r=[B_wb[sl], B_x], w=[PSB[bk]])
                epi(tb, cb, n, bk)
        ar.pop()

    def copy_epi(dst_of, B_dst, ntok=S):
        def epi(pi, half, pieces=None):
            d = dst_of(pi)
            n = d.shape[0]
            if pi % 2 == 0:
                kb.op('act', lambda e: e.copy(d, ps_half(half, n, ntok)), r=PSH(half), w=[B_dst], accum=True)
            else:
                kb.op('dve', lambda e: e.tensor_copy(out=d, in_=ps_half(half, n, ntok)), r=PSH(half), w=[B_dst], accum=True)
        return epi

    def psbf(bk):
        return psv(bk).bitcast(BF16)


    hpt = ar.f32(8)
    B_hp = Buf('hp')
    kb.op('sp', lambda e: e.dma_start(out=hpt[0:8, 0:4], in_=hp_in), w=[B_hp], dma=B_hp)

    def gdn_mixer(ev, aT, B_aT, bT, B_bT, mixT):
        kb.barrier()
        ar.push()
        a8, b8 = aT[0:8, :], bT[0:8, :]
        sel = lambda h: consts[0:8, C_SEL + h * 128:C_SEL + (h + 1) * 128]
        low, slow, up, sup = (consts[:, c:c + 128] for c in (C_LOW, C_SLOW, C_UP, C_SUP))
        sm = ar.f32(8)
        B_sm = Buf('gsm')
        names = ['qn', 'kn', 'kbt', 'vb', 'egc', 'gcb', 'oT']
        Bf = {n: ar.f32(S) for n in names}
        BB = {n: Buf('g_' + n) for n in names}
        rmask = Bf['oT']
        B_rm = BB['oT']
        kb.op('pool', lambda e: e.memset(rmask[0:8, :], 1.0), w=[B_rm])
        kb.op('pool', lambda e: e.memset(rmask[0:8, :].rearrange("p (c t) -> p c t", t=128)[:, :, 0:1], 0.0), r=[B_rm], w=[B_rm])
        kb.op('act', lambda e: e.activation(out=sm[0:8, 0:1], in_=hpt[0:8, 2 * ev:2 * ev + 1], func=AF.Exp), r=[B_hp], w=[B_sm])
        kb.op('dve', lambda e: e.tensor_scalar(out=sm[0:8, 0:1], in0=sm[0:8, 0:1], scalar1=-1.0, scalar2=None, op0=ALU.mult), r=[B_sm], w=[B_sm])
        kb.op('act', lambda e: e.activation(out=a8, in_=a8, func=AF.Exp, bias=hpt[0:8, 2 * ev + 1:2 * ev + 2]), r=[B_aT, B_hp], w=[B_aT])
        kb.op('act', lambda e: e.activation(out=a8, in_=a8, func=AF.Ln, bias=1.0), r=[B_aT], w=[B_aT])
        kb.op('dve', lambda e: e.tensor_scalar(out=a8, in0=a8, scalar1=sm[0:8, 0:1], scalar2=None, op0=ALU.mult), r=[B_aT, B_sm], w=[B_aT])
        kb.op('dve', lambda e: e.tensor_tensor_scan(out=a8, data0=rmask[0:8, :], data1=a8, initial=0.0, op0=ALU.mult, op1=ALU.add),
              r=[B_aT, B_rm], w=[B_aT])
        kb.op('act', lambda e: e.activation(out=b8, in_=b8, func=AF.Sigmoid), r=[B_bT], w=[B_bT])
        gccol = ar.f32(128)
        B_gccol = Buf('gccol')
        for c in range(16):
            kb.op('pe', lambda e, c=c: e.transpose(psv(0)[:, c * 8:(c + 1) * 8], a8[:, c * 128:(c + 1) * 128], ident[0:8, 0:8]),
                  r=[B_aT, B_const], w=[PSB[0]])
        kb.op('dve', lambda e: e.tensor_copy(out=gccol, in_=psv(0)[:, 0:128]), r=[PSB[0]], w=[B_gccol])
        if GL < 1:
            ar.pop()
            return
        gate = ar.bf16(S)
        B_gate = Buf('g_gate')
        small = {}
        for n in ['Pm0', 'Pm1', 'PT0', 'PT1', 'TT', 'dec', 'decT', 'attnT', 'negwT', 'vnew', 'Sst', 't1', 'kbgc', 'qgc', 'kdc']:
            small[n] = (ar.f32(128), Buf('g_' + n))
        tok3, B_tok3 = ar.f32(384), Buf('g_tok3')
        egl, B_egl = ar.f32(16), Buf('g_egl')
        gqkvr = gqkv.rearrange("(c p) t -> c p t", p=128)
        gateTr = gateT.rearrange("(c p) t -> c p t", p=128)

        def PSALL():
            return PSB[0:4]
        for h in range(GH):
            kb.barrier()
            qn, kn, vb = Bf['qn'], Bf['kn'], Bf['vb']
            kb.op('sp', lambda e, h=h: e.dma_start(out=qn, in_=gqkvr[h]), r=[B_gqkv], w=[BB['qn']], dma=BB['qn'])
            kb.op('sp', lambda e, h=h: e.dma_start(out=kn, in_=gqkvr[8 + h]), r=[B_gqkv], w=[BB['kn']], dma=BB['kn'])
            kb.op('sp', lambda e, h=h: e.dma_start(out=vb, in_=gqkvr[16 + h]), r=[B_gqkv], w=[BB['vb']], dma=BB['vb'])
            kb.op('sp', lambda e, h=h: e.dma_start(out=gate, in_=gateTr[h]), r=[B_gateT], w=[B_gate], dma=B_gate)
            for nm, scl in (('qn', 128 ** -0.5), ('kn', 1.0)):
                x_ = Bf[nm]
                tmp, B_tmp = Bf['egc'], BB['egc']
                kb.op('act', lambda e, x_=x_, tmp=tmp: e.activation(out=tmp, in_=x_, func=AF.Square), r=[BB[nm]], w=[B_tmp])
                for t in range(4):
                    kb.op('pe', lambda e, t=t, tmp=tmp: e.matmul(psv(t), ones, tmp[:, t * 512:(t + 1) * 512], start=True, stop=True),
                          r=[B_tmp, B_const], w=[PSB[t]])
                kb.op('act', lambda e, tmp=tmp: e.activation(out=tmp, in_=ps_t[:, 0:S], func=AF.Sqrt, bias=epsc), r=PSALL() + [B_eps], w=[B_tmp])
                kb.op('dve', lambda e, tmp=tmp: e.reciprocal(out=tmp, in_=tmp), r=[B_tmp], w=[B_tmp])
                kb.op('dve', lambda e, x_=x_, tmp=tmp, scl=scl: e.scalar_tensor_tensor(out=x_, in0=x_, scalar=scl, in1=tmp, op0=ALU.mult, op1=ALU.mult),
                      r=[BB[nm], B_tmp], w=[BB[nm]])
            for t in range(4):
                kb.op('pe', lambda e, t=t, h=h: e.matmul(psv(t), sel(h), a8[:, t * 512:(t + 1) * 512], start=True, stop=True), r=[B_aT, B_const], w=[PSB[t]])
                kb.op('pe', lambda e, t=t, h=h: e.matmul(psv(4 + t), sel(h), b8[:, t * 512:(t + 1) * 512], start=True, stop=True), r=[B_bT, B_const], w=[PSB[4 + t]])
            gcb, egc = Bf['gcb'], Bf['egc']
            kb.op('act', lambda e: e.copy(gcb, ps_t[:, 0:S]), r=PSALL(), w=[BB['gcb']])
            kb.op('act', lambda e: e.activation(out=egc, in_=ps_t[:, 0:S], func=AF.Exp), r=PSALL(), w=[BB['egc']])
            betab = ps_t[:, S:2 * S]
            kb.op('dve', lambda e: e.tensor_tensor(out=Bf['kbt'], in0=betab, in1=kn, op=ALU.mult), r=PSB[4:8] + [BB['kn']], w=[BB['kbt']])
            kb.op('dve', lambda e: e.tensor_tensor(out=vb, in0=betab, in1=vb, op=ALU.mult), r=PSB[4:8] + [BB['vb']], w=[BB['vb']])
            gl = gcb.rearrange("p (c t) -> p c t", t=128)[:, :, 127]
            kb.op('act', lambda e: e.activation(out=egl, in_=gl, func=AF.Exp), r=[BB['gcb']], w=[B_egl])
            Sst, B_S = small['Sst']
            kb.op('pool', lambda e: e.memset(Sst, 0.0), w=[B_S])
            for c in range(16 if GL >= 2 else 0):
                cs = slice(c * 128, (c + 1) * 128)
                gcc = gccol[:, c * 8 + h:c * 8 + h + 1]
                kbgc, B_kbgc = small['kbgc']
                kb.op('dve', lambda e, cs=cs: e.tensor_tensor(out=kbgc, in0=Bf['kbt'][:, cs], in1=Bf['egc'][:, cs], op=ALU.mult), r=[BB['kbt'], BB['egc']], w=[B_kbgc])
                qgc, B_qgc = small['qgc']
                kdc, B_kdc = small['kdc']
                kb.op('dve', lambda e, cs=cs: e.tensor_tensor(out=qgc, in0=qn[:, cs], in1=Bf['egc'][:, cs], op=ALU.mult), r=[BB['qn'], BB['egc']], w=[B_qgc])
                kb.op('act', lambda e, c=c, cs=cs: e.activation(out=kdc, in_=gcb[:, cs], func=AF.Exp, scale=-1.0, bias=gcb[:, c * 128 + 127:c * 128 + 128]),
                      r=[BB['gcb']], w=[B_kdc])
                kb.op('dve', lambda e, cs=cs: e.tensor_tensor(out=kdc, in0=kdc, in1=kn[:, cs], op=ALU.mult), r=[B_kdc, BB['kn']], w=[B_kdc])
                for i_, (src_, B_src_) in enumerate(((Bf['vb'][:, cs], BB['vb']), (kbgc, B_kbgc), (kdc, B_kdc))):
                    kb.op('pe', lambda e, i_=i_, src_=src_: e.transpose(psv(4)[:, i_ * 128:(i_ + 1) * 128], src_, ident), r=[B_src_, B_const], w=[PSB[4]])
                kb.op('act', lambda e: e.copy(tok3, psv(4)[:, 0:384]), r=[PSB[4]], w=[B_tok3])
                vb_tok, kbg_tok, kdec_tok = tok3[:, 0:128], tok3[:, 128:256], tok3[:, 256:384]
                dec, B_dec = small['dec']
                decT, B_decT = small['decT']
                kb.op('dve', lambda e, cs=cs, gcc=gcc: e.tensor_scalar(out=dec, in0=gcb[:, cs], scalar1=gcc, scalar2=0.0, op0=ALU.subtract, op1=ALU.max),
                      r=[BB['gcb'], B_gccol], w=[B_dec])
                kb.op('dve', lambda e, cs=cs, gcc=gcc: e.tensor_scalar(out=decT, in0=gcb[:, cs], scalar1=gcc, scalar2=0.0, op0=ALU.subtract, op1=ALU.min),
                      r=[BB['gcb'], B_gccol], w=[B_decT])
                kb.op('act', lambda e: e.activation(out=dec, in_=dec, func=AF.Exp, scale=-1.0), r=[B_dec], w=[B_dec])
                kb.op('act', lambda e: e.activation(out=decT, in_=decT, func=AF.Exp), r=[B_decT], w=[B_decT])
                kb.op('pe', lambda e, cs=cs: e.matmul(psv(5)[:, 0:128], Bf['kbt'][:, cs], kn[:, cs], start=True, stop=True), r=[BB['kbt'], BB['kn']], w=[PSB[5]])
                kb.op('pe', lambda e, cs=cs: e.matmul(psv(5)[:, 128:256], kn[:, cs], Bf['kbt'][:, cs], start=True, stop=True), r=[BB['kbt'], BB['kn']], w=[PSB[5]])
                kb.op('pe', lambda e, cs=cs: e.matmul(psv(5)[:, 256:384], kn[:, cs], qn[:, cs], start=True, stop=True), r=[BB['qn'], BB['kn']], w=[PSB[5]])
                t1, B_t1 = small['t1']
                Pm, B_Pm = small['Pm0']
                PT, B_PT = small['PT0']
                TT, B_TT = small['TT']
                attnT, B_attnT = small['attnT']
                kb.op('dve', lambda e: e.tensor_tensor(out=t1, in0=psv(5)[:, 0:128], in1=dec, op=ALU.mult), r=[PSB[5], B_dec], w=[B_t1])
                kb.op('dve', lambda e: e.scalar_tensor_tensor(out=Pm, in0=t1, scalar=-1.0, in1=slow, op0=ALU.mult, op1=ALU.mult), r=[B_t1, B_const], w=[B_Pm])
                kb.op('dve', lambda e: e.tensor_tensor(out=t1, in0=psv(5)[:, 128:256], in1=decT, op=ALU.mult), r=[PSB[5], B_decT], w=[B_t1])
                kb.op('dve', lambda e: e.scalar_tensor_tensor(out=PT, in0=t1, scalar=-1.0, in1=sup, op0=ALU.mult, op1=ALU.mult), r=[B_t1, B_const], w=[B_PT])
                kb.op('dve', lambda e: e.tensor_tensor(out=t1, in0=psv(5)[:, 256:384], in1=decT, op=ALU.mult), r=[PSB[5], B_decT], w=[B_t1])
                kb.op('dve', lambda e: e.tensor_tensor(out=attnT, in0=t1, in1=up, op=ALU.mult), r=[B_t1, B_const], w=[B_attnT])
                kb.op('dve', lambda e: e.tensor_copy(out=TT, in_=ident), r=[B_const], w=[B_TT])
                if GL < 3:
                    continue
                cur = 0
                for m in range(7):
                    Pc, B_Pc = small['Pm%d' % cur]
                    PTc, B_PTc = small['PT%d' % cur]
                    Pn, B_Pn = small['Pm%d' % (1 - cur)]
                    PTn, B_PTn = small['PT%d' % (1 - cur)]
                    if m < 6:
                        kb.op('pe', lambda e, Pc=Pc, PTc=PTc: e.matmul(psv(6)[:, 0:128], PTc, Pc, start=True, stop=True), r=[B_Pc, B_PTc], w=[PSB[6]])
                    if m < 5:
                        kb.op('pe', lambda e, Pc=Pc, PTc=PTc: e.matmul(psv(6)[:, 128:256], Pc, PTc, start=True, stop=True), r=[B_Pc, B_PTc], w=[PSB[6]])
                    kb.op('pe', lambda e, Pc=Pc: e.matmul(psv(7)[:, 0:128], Pc, TT, start=True, stop=True), r=[B_Pc, B_TT], w=[PSB[7]])
                    if m < 6:
                        kb.op('act', lambda e, Pn=Pn: e.copy(Pn, psv(6)[:, 0:128]), r=[PSB[6]], w=[B_Pn])
                    if m < 5:
                        kb.op('act', lambda e, PTn=PTn: e.copy(PTn, psv(6)[:, 128:256]), r=[PSB[6]], w=[B_PTn])
                    kb.op('dve', lambda e: e.tensor_tensor(out=TT, in0=psv(7)[:, 0:128], in1=TT, op=ALU.add), r=[PSB[7], B_TT], w=[B_TT])
                    cur = 1 - cur
                if GL < 4:
                    continue
                negwT, B_nw = small['negwT']
                vnew, B_vn = small['vnew']
                kb.op('pe', lambda e: e.matmul(psv(6)[:, 256:384], kbg_tok, TT, start=True, stop=True), r=[B_tok3, B_TT], w=[PSB[6]])
                kb.op('act', lambda e: e.mul(negwT, psv(6)[:, 256:384], -1.0), r=[PSB[6]], w=[B_nw])
                kb.op('pe', lambda e: e.matmul(psv(7)[:, 128:256], TT, vb_tok, start=True, stop=False), r=[B_tok3, B_TT], w=[PSB[7]])
                kb.op('pe', lambda e: e.matmul(psv(7)[:, 128:256], negwT, Sst, start=False, stop=True), r=[B_nw, B_S], w=[PSB[7]])
                kb.op('act', lambda e: e.copy(vnew, psv(7)[:, 128:256]), r=[PSB[7]], w=[B_vn])
                kb.op('pe', lambda e, cs=cs: e.matmul(psv(7)[:, 256:384], Sst, qgc, start=True, stop=False), r=[B_S, B_qgc], w=[PSB[7]])
                kb.op('pe', lambda e: e.matmul(psv(7)[:, 256:384], vnew, attnT, start=False, stop=True), r=[B_vn, B_attnT], w=[PSB[7]])
                kb.op('act', lambda e, cs=cs: e.copy(Bf['oT'][:, cs], psv(7)[:, 256:384]), r=[PSB[7]], w=[BB['oT']], accum=True)
                kb.op('pe', lambda e: e.matmul(psv(6)[:, 384:512], kdec_tok, vnew, start=True, stop=True), r=[B_tok3, B_vn], w=[PSB[6]])
                kb.op('dve', lambda e, c=c: e.scalar_tensor_tensor(out=Sst, in0=Sst, scalar=egl[:, c:c + 1], in1=psv(6)[:, 384:512], op0=ALU.mult, op1=ALU.add),
                      r=[B_S, B_egl, PSB[6]], w=[B_S])
            oT = Bf['oT']
            tmp, B_tmp = Bf['egc'], BB['egc']
            kb.op('act', lambda e: e.activation(out=tmp, in_=oT, func=AF.Square), r=[BB['oT']], w=[B_tmp])
            for t in range(4):
                kb.op('pe', lambda e, t=t: e.matmul(psv(t), ones, tmp[:, t * 512:(t + 1) * 512], start=True, stop=True), r=[B_tmp, B_const], w=[PSB[t]])
            kb.op('act', lambda e: e.activation(out=tmp, in_=ps_t[:, 0:S], func=AF.Sqrt, scale=1.0 / 128, bias=epsc), r=PSALL() + [B_eps], w=[B_tmp])
            kb.op('dve', lambda e: e.reciprocal(out=tmp, in_=tmp), r=[B_tmp], w=[B_tmp])
            kb.op('dve', lambda e: e.scalar_tensor_tensor(out=oT, in0=oT, scalar=scol('ev_gdn_norm', ev), in1=tmp, op0=ALU.mult, op1=ALU.mult),
                  r=[BB['oT'], B_tmp, B_small], w=[BB['oT']])
            kb.op('dve', lambda e, h=h: e.tensor_tensor(out=mixT[:, 8 + h, :], in0=oT, in1=gate, op=ALU.mult), r=[BB['oT'], B_gate], w=[B_act], accum=True)
        ar.pop()


    def gla_mixer(od, lgT, B_lgT, mixT):
        kb.barrier()
        ar.push()
        up64 = consts[0:64, C_UP:C_UP + 64]
        w2t = ar.f32(512)
        B_w2 = Buf('w2t')
        kb.op('sp', lambda e: e.dma_start(out=w2t[0:16, :], in_=w2_in[:, od, :]), w=[B_w2], dma=B_w2)
        names = ['qt', 'kt', 'cum', 'tmp', 'oT0', 'oT1']
        Bf = {n: ar.f32(S) for n in names}
        BB = {n: Buf('l_' + n) for n in names}
        vtb = ar.f32(16 * 256)
        vt = vtb[0:64, :].rearrange("p (c n) -> p c n", c=16)
        B_vt = Buf('l_vt')
        gate = ar.bf16(S)
        B_gate = Buf('l_gate')
        negb2 = ar.f32(8)
        B_nb = Buf('l_nb')
        alast = ar.f32(32)
        B_al = Buf('l_al')
        attnT, B_attnT = ar.f32(64), Buf('l_attnT')
        kdc, B_kdc = ar.f32(64), Buf('l_kdc')
        kdtok, B_kdtok = ar.f32(128), Buf('l_kdtok')
        Sst, B_S = ar.f32(256), Buf('l_S')
        rmask = Bf['oT0']
        B_rm = BB['oT0']
        rmask2 = ar.f32(S)
        B_rm2 = Buf('l_rm')
        kb.op('pool', lambda e: e.memset(rmask2, 1.0), w=[B_rm2])
        kb.op('pool', lambda e: e.memset(rmask2.rearrange("p (c t) -> p c t", t=64)[:, :, 0:1], 0.0), r=[B_rm2], w=[B_rm2])
        for h in range(4):
            kb.op('dve', lambda e, h=h: e.tensor_scalar(out=negb2[:, h:h + 1], in0=scol('od_gla_b2', od * 4 + h), scalar1=-1.0, scalar2=None, op0=ALU.mult),
                  r=[B_small], w=[B_nb], accum=(h > 0))
        glqkr = glqk.rearrange("(c p) t -> c p t", p=128)
        gateTr = gateT.rearrange("(c p) t -> c p t", p=128)
        qt, kt, cum, tmp = Bf['qt'], Bf['kt'], Bf['cum'], Bf['tmp']
        oTs = [Bf['oT0'], Bf['oT1']]
        B_oTs = [BB['oT0'], BB['oT1']]
        for h in range(4):
            kb.barrier()
            kb.op('sp', lambda e, h=h: e.dma_start(out=qt, in_=glqkr[h]), r=[B_glqk], w=[BB['qt']], dma=BB['qt'])
            kb.op('sp', lambda e, h=h: e.dma_start(out=kt, in_=glqkr[4 + h]), r=[B_glqk], w=[BB['kt']], dma=BB['kt'])
            for t in range(4):
                kb.op('pe', lambda e, t=t, h=h: e.matmul(psv(t), w2t[0:16, h * 128:(h + 1) * 128], lgT[0:16, t * 512:(t + 1) * 512], start=True, stop=True),
                      r=[B_w2, B_lgT], w=[PSB[t]])
            kb.op('act', lambda e, h=h: e.activation(out=tmp, in_=ps_t[:, 0:S], func=AF.Exp, scale=-1.0, bias=negb2[:, h:h + 1]), r=PSB[0:4] + [B_nb], w=[BB['tmp']])
            kb.op('act', lambda e: e.activation(out=tmp, in_=tmp, func=AF.Ln, bias=1.0), r=[BB['tmp']], w=[BB['tmp']])
            kb.op('dve', lambda e: e.tensor_scalar(out=tmp, in0=tmp, scalar1=-1.0 / 16.0, scalar2=None, op0=ALU.mult), r=[BB['tmp']], w=[BB['tmp']])
            kb.op('dve', lambda e: e.tensor_tensor_scan(out=cum, data0=rmask2, data1=tmp, initial=0.0, op0=ALU.mult, op1=ALU.add), r=[BB['tmp'], B_rm2], w=[BB['cum']])
            kb.op('act', lambda e: e.activation(out=tmp, in_=cum, func=AF.Exp), r=[BB['cum']], w=[BB['tmp']])
            kb.op('dve', lambda e: e.tensor_copy(out=alast, in_=tmp.rearrange("p (c t) -> p c t", t=64)[:, :, 63]), r=[BB['tmp']], w=[B_al])
            kb.op('dve', lambda e: e.scalar_tensor_tensor(out=qt, in0=qt, scalar=128 ** -0.5, in1=tmp, op0=ALU.mult, op1=ALU.mult), r=[BB['qt'], BB['tmp']], w=[BB['qt']])
            kb.op('act', lambda e: e.activation(out=tmp, in_=cum, func=AF.Exp, scale=-1.0), r=[BB['cum'], BB['qt']], w=[BB['tmp']])
            kb.op('dve', lambda e: e.tensor_tensor(out=kt, in0=kt, in1=tmp, op=ALU.mult), r=[BB['kt'], BB['tmp']], w=[BB['kt']])
            kb.op('pool', lambda e: e.memset(Sst, 0.0), w=[B_S])
            for c in range(32):
                cs = slice(c * 64, (c + 1) * 64)
                ci = c % 16
                if ci == 0:
                    kb.op('sp', lambda e, c=c, h=h: e.dma_start(out=vt, in_=glv[c * 64:(c + 16) * 64, h * 256:(h + 1) * 256].rearrange("(c p) n -> p c n", p=64)),
                          r=[B_glv], w=[B_vt], dma=B_vt)
                kb.op('pe', lambda e, cs=cs: e.matmul(psv(4)[0:64, 0:64], kt[:, cs], qt[:, cs], start=True, stop=True), r=[BB['kt'], BB['qt']], w=[PSB[4]])
                kb.op('dve', lambda e: e.tensor_tensor(out=attnT[0:64, :], in0=psv(4)[0:64, 0:64], in1=up64, op=ALU.mult), r=[PSB[4], B_const], w=[B_attnT])
                kb.op('dve', lambda e, cs=cs, c=c: e.tensor_scalar(out=kdc, in0=kt[:, cs], scalar1=alast[:, c:c + 1], scalar2=None, op0=ALU.mult),
                      r=[BB['kt'], B_al], w=[B_kdc])
                kb.op('pe', lambda e: e.transpose(psv(3)[0:64, 0:128], kdc, ident), r=[B_kdc, B_const], w=[PSB[3]])
                kb.op('act', lambda e: e.copy(kdtok[0:64, :], psv(3)[0:64, 0:128]), r=[PSB[3]], w=[B_kdtok])
                for half in range(2):
                    hs = slice(half * 128, (half + 1) * 128)
                    kb.op('pe', lambda e, half=half, hs=hs, cs=cs: e.matmul(psv(5 + half)[:, 0:64], Sst[:, hs], qt[:, cs], start=True, stop=False),
                          r=[B_S, BB['qt']], w=[PSB[5 + half]])
                    kb.op('pe', lambda e, half=half, hs=hs, ci=ci: e.matmul(psv(5 + half)[:, 0:64], vt[:, ci, hs], attnT[0:64, :], start=False, stop=True),
                          r=[B_vt, B_attnT], w=[PSB[5 + half]])
                    kb.op('act', lambda e, half=half, cs=cs: e.copy(oTs[half][:, cs], psv(5 + half)[:, 0:64]), r=[PSB[5 + half]], w=[B_oTs[half]], accum=True)
                kb.op('pe', lambda e, ci=ci: e.matmul(psv(7)[:, 0:256], kdtok[0:64, :], vt[:, ci, :], start=True, stop=True), r=[B_kdtok, B_vt], w=[PSB[7]])
                kb.op('dve', lambda e, c=c: e.scalar_tensor_tensor(out=Sst, in0=Sst, scalar=alast[:, c:c + 1], in1=psv(7)[:, 0:256], op0=ALU.mult, op1=ALU.add),
                      r=[B_S, B_al, PSB[7]], w=[B_S])
            kb.op('act', lambda e: e.activation(out=tmp, in_=oTs[0], func=AF.Square), r=[B_oTs[0]], w=[BB['tmp']])
            kb.op('act', lambda e: e.activation(out=cum, in_=oTs[1], func=AF.Square), r=[B_oTs[1]], w=[BB['cum']])
            for t in range(4):
                kb.op('pe', lambda e, t=t: e.matmul(psv(t), ones, tmp[:, t * 512:(t + 1) * 512], start=True, stop=False), r=[BB['tmp'], B_const], w=[PSB[t]])
                kb.op('pe', lambda e, t=t: e.matmul(psv(t), ones, cum[:, t * 512:(t + 1) * 512], start=False, stop=True), r=[BB['cum'], B_const], w=[PSB[t]])
            kb.op('act', lambda e: e.activation(out=tmp, in_=ps_t[:, 0:S], func=AF.Sqrt, scale=1.0 / 256, bias=epsc), r=PSB[0:4] + [B_eps], w=[BB['tmp']])
            kb.op('dve', lambda e: e.reciprocal(out=tmp, in_=tmp), r=[BB['tmp']], w=[BB['tmp']])
            for half in range(2):
                kb.op('sp', lambda e, h=h, half=half: e.dma_start(out=gate, in_=gateTr[h * 2 + half]), r=[B_gateT], w=[B_gate], dma=B_gate)
                kb.op('dve', lambda e, half=half: e.scalar_tensor_tensor(out=oTs[half], in0=oTs[half], scalar=scol('od_gla_norm', od * 2 + half), in1=tmp,
                                                                         op0=ALU.mult, op1=ALU.mult), r=[B_oTs[half], BB['tmp'], B_small], w=[B_oTs[half]])
                kb.op('dve', lambda e, h=h, half=half: e.tensor_tensor(out=mixT[:, 8 + h * 2 + half, :], in0=oTs[half], in1=gate, op=ALU.mult),
                      r=[B_oTs[half], B_gate], w=[B_act], accum=True)
        ar.pop()

    for layer in layers:
        if do_mixer and layer % 2 == 0:
            ev = layer // 2
            hn = act_view(16)
            norm_T(hT, B_hT, 16, lambda c, layer=layer: scol('norm_mix', layer * 16 + c), hn, B_act)
            kb.barrier()
            ar.push()
            Win = W[('ev_w_in', ev)]
            B_Win = wbufs[('ev_w_in', ev)]
            aT = ar.f32(S)
            B_aT = Buf('aT')
            bT = ar.f32(S)
            B_bT = Buf('bT')
            ar.push()
            stb = [ar.bf16(S), ar.bf16(S)]
            B_stb = [Buf('stb0'), Buf('stb1')]
            sbqkr = sbqk.rearrange("(c p) t -> c p t", p=128)

            def qk_epi(pi, half):
                sl = pi % 2
                kb.op('act', lambda e: e.copy(stb[sl], ps_half(half)), r=PSH(half), w=[B_stb[sl]])
                kb.op('pool', lambda e: e.dma_start(out=sbqkr[pi], in_=stb[sl]), r=[B_stb[sl]], w=[B_sbqk], dma=B_stb[sl], accum=True)
            linear_T(hn, B_act, 16, Win, B_Win, [(c * 128, 128) for c in range(16)], qk_epi)
            Wr_ = Win.rearrange("(c p) n -> p c n", p=128)
            stv = [ar.bf16(256), ar.bf16(256)]
            B_stv = [Buf('stv0'), Buf('stv1')]
            vcnt = [0]

            def v_epi(tb, cb, n, bk):
                sl = vcnt[0] % 2
                vcnt[0] += 1
                kb.op('act', lambda e: e.copy(stv[sl][:, 0:n], psv(bk)[:, 0:n]), r=[PSB[bk]], w=[B_stv[sl]])
                kb.op('pool', lambda e: e.dma_start(out=sbv[tb * 128:(tb + 1) * 128, cb * 256:cb * 256 + n], in_=stv[sl][:, 0:n]),
                      r=[B_stv[sl]], w=[B_sbv], dma=B_stv[sl], accum=True)
            linear_tok(hn, B_act, 16, lambda c0, n: Wr_[:, :, 2048 + c0:2048 + c0 + n], B_Win, 1024, S, v_epi)
            kb.barrier()
            ar.pop()
            ar.push()
            xb4 = [ar.f32(S + 8) for _ in range(2)]
            B_xb4 = [Buf('xb40'), Buf('xb41')]
            cv = [ar.f32(S), ar.f32(S)]
            B_cv = [Buf('cv0'), Buf('cv1')]
            for i in range(2):
                kb.op('pool', lambda e, i=i: e.memset(xb4[i][:, 0:8], 0.0), w=[B_xb4[i]])
            gqkvr = gqkv.rearrange("(c p) t -> c p t", p=128)

            def gq_epi(pi, half, ev=ev):
                sl = pi % 2

                def wc(tap):
                    return scol('ev_sconv', (ev * 4 + tap) * 24 + pi)
                kb.op('act', lambda e: e.copy(xb4[sl][:, 8:8 + S], ps_half(half)), r=PSH(half), w=[B_xb4[sl]], accum=True)
                kb.op('act', lambda e: e.activation(out=cv[sl], in_=xb4[sl][:, 8:8 + S], func=AF.Identity, scale=wc(3)),
                      r=[B_xb4[sl], B_small], w=[B_cv[sl]])
                for tap in range(3):
                    kb.op('dve', lambda e, tap=tap: e.scalar_tensor_tensor(out=cv[sl], in0=xb4[sl][:, 5 + tap:5 + tap + S], scalar=wc(tap), in1=cv[sl],
                                                                          op0=ALU.mult, op1=ALU.add), r=[B_xb4[sl], B_cv[sl], B_small], w=[B_cv[sl]])
                kb.op('act', lambda e: e.activation(out=cv[sl], in_=cv[sl], func=AF.Silu), r=[B_cv[sl]], w=[B_cv[sl]])
                kb.op('pool', lambda e: e.dma_start(out=gqkvr[pi], in_=cv[sl]), r=[B_cv[sl]], w=[B_gqkv], dma=B_cv[sl], accum=True)
            linear_T(hn, B_act, 16, Win, B_Win, [(3072 + c * 128, 128) for c in range(24)], gq_epi, SW=128)
            kb.barrier()
            ar.pop()
            ar.push()
            stb = [ar.bf16(S), ar.bf16(S)]
            B_stb = [Buf('stb0'), Buf('stb1')]

            def ab_epi(pi, half):
                d, B_d = (aT, B_aT) if pi == 0 else (bT, B_bT)
                kb.op('act', lambda e: e.copy(d[0:8, :], ps_half(half, 8)), r=PSH(half), w=[B_d])
            linear_T(hn, B_act, 16, Win, B_Win, [(6144, 8), (6152, 8)], ab_epi)
            gateTr = gateT.rearrange("(c p) t -> c p t", p=128)

            def gate_epi(pi, half):
                sl = pi % 2
                kb.op('act', lambda e: e.activation(out=stb[sl], in_=ps_half(half), func=AF.Silu), r=PSH(half), w=[B_stb[sl]])
                kb.op('pool', lambda e: e.dma_start(out=gateTr[pi], in_=stb[sl]), r=[B_stb[sl]], w=[B_gateT], dma=B_stb[sl], accum=True)
            linear_T(hn, B_act, 16, Win, B_Win, [(6160 + c * 128, 128) for c in range(8)], gate_epi)
            ar.pop()
            kb.barrier()
            mixT = act_view(16)
            ar.push()
            sc = 128 ** -0.5
            qh = ar.bf16(S)
            kh = ar.bf16(S)
            vh = ar.bf16(S).rearrange("p (c d) -> p c d", c=16)
            B_qkv = Buf('sbqkvh')
            onesS = ar.f32(S)
            B_onesS = Buf('onesS')
            kb.op('pool', lambda e: e.memset(onesS, 1.0), w=[B_onesS])
            ebuf = ar.f32(S)
            B_e = Buf('sbe')
            spb_full = ar.f32(S + 8)
            spb = spb_full[:, 8:8 + S]
            B_sp = Buf('sbsp')
            kb.op('pool', lambda e: e.memset(spb_full[:, 0:8], 0.0), w=[B_sp])
            cb_ = ar.f32(S)
            B_c = Buf('sbc')
            t2 = ar.f32(S)
            B_t2 = Buf('sbt2')
            wbuf = ar.bf16(S)
            B_w_ = Buf('sbw')
            wT = ar.bf16(S).rearrange("p (c q) -> p c q", c=16)
            B_wT = Buf('sbwT')
            ngT = ar.f32(8)
            B_ngT = Buf('sbngT')
            sbvr = sbv.rearrange("(c p) n -> p c n", p=128)
            slow = consts[:, C_SLOW:C_SLOW + 128]
            for h in range(8):
                kb.op('sp', lambda e, h=h: e.dma_start(out=qh, in_=sbqkr[h]), r=[B_sbqk], w=[B_qkv], dma=B_qkv)
                kb.op('sp', lambda e, h=h: e.dma_start(out=kh, in_=sbqkr[8 + h]), r=[B_sbqk], w=[B_qkv], dma=B_qkv, accum=True)
                kb.op('sp', lambda e, h=h: e.dma_start(out=vh, in_=sbvr[:, :, h * 128:(h + 1) * 128]), r=[B_sbv], w=[B_qkv], dma=B_qkv, accum=True)
                for qb in range(16):
                    nk = (qb + 1) * 128
                    nt = (nk + 511) // 512
                    for t in range(nt):
                        w_ = min(512, nk - t * 512)
                        kb.op('pe', lambda e, qb=qb, t=t, w_=w_: e.matmul(psv(t)[:, 0:w_], qh[:, qb * 128:(qb + 1) * 128], kh[:, t * 512:t * 512 + w_],
                                                                         start=True, stop=True), r=[B_qkv], w=[PSB[t]])
                    zps = ps_t[:, 0:nk]
                    ZB = PSB[0:nt]
                    dg = slice(nk - 128, nk)
                    kb.op('act', lambda e, zps=zps, nk=nk: e.activation(out=ebuf[:, 0:nk], in_=zps, func=AF.Exp, scale=sc), r=ZB, w=[B_e])
                    kb.op('act', lambda e, nk=nk: e.activation(out=spb[:, 0:nk], in_=ebuf[:, 0:nk], func=AF.Ln, bias=1.0), r=[B_e], w=[B_sp])
                    kb.op('dve', lambda e, dg=dg: e.tensor_tensor(out=spb[:, dg], in0=spb[:, dg], in1=slow, op=ALU.mult), r=[B_sp, B_const], w=[B_sp])
                    kb.op('dve', lambda e, nk=nk: e.tensor_tensor_scan(out=cb_[:, 0:nk], data0=onesS[:, 0:nk], data1=spb_full[:, 7:7 + nk], initial=0.0,
                                                                       op0=ALU.mult, op1=ALU.add), r=[B_sp, B_onesS], w=[B_c])
                    kb.op('dve', lambda e, nk=nk: e.tensor_scalar(out=ngT[:, 0:1], in0=cb_[:, nk - 1:nk], scalar1=-1.0, scalar2=None, op0=ALU.mult),
                          r=[B_c], w=[B_ngT])
                    kb.op('dve', lambda e, zps=zps, nk=nk: e.scalar_tensor_tensor(out=t2[:, 0:nk], in0=zps, scalar=sc, in1=cb_[:, 0:nk],
                                                                               op0=ALU.mult, op1=ALU.add), r=ZB + [B_c], w=[B_t2])
                    kb.op('act', lambda e, nk=nk: e.activation(out=wbuf[:, 0:nk], in_=t2[:, 0:nk], func=AF.Exp, bias=ngT[:, 0:1]),
                          r=[B_t2, B_ngT], w=[B_w_])
                    kb.op('dve', lambda e, dg=dg: e.tensor_tensor(out=wbuf[:, dg], in0=wbuf[:, dg], in1=slow, op=ALU.mult), r=[B_w_, B_const], w=[B_w_])
                    for k_ in range(qb + 1):
                        bk = 4 + k_ // 8
                        kb.op('pe', lambda e, k_=k_, bk=bk: e.transpose(psbf(bk)[:, (k_ % 8) * 128:(k_ % 8 + 1) * 128], wbuf[:, k_ * 128:(k_ + 1) * 128], identb),
                              r=[B_w_, B_identb], w=[PSB[bk]])
                    for g_ in range((qb + 8) // 8):
                        n_ = min(8, qb + 1 - g_ * 8)
                        if g_ == 0:
                            kb.op('act', lambda e, n_=n_: e.copy(wT.rearrange("p c q -> p (c q)")[:, 0:n_ * 128], psbf(4)[:, 0:n_ * 128]),
                                  r=[PSB[4]], w=[B_wT])
                        else:
                            kb.op('dve', lambda e, n_=n_: e.tensor_copy(out=wT.rearrange("p c q -> p (c q)")[:, 1024:1024 + n_ * 128], in_=psbf(5)[:, 0:n_ * 128]),
                                  r=[PSB[5]], w=[B_wT], accum=True)
                    ob = 6 + qb % 2
                    for k_ in range(qb + 1):
                        kb.op('pe', lambda e, k_=k_, ob=ob, qb=qb: e.matmul(psv(ob)[:, 0:128], vh[:, k_, :], wT[:, k_, :], start=(k_ == 0), stop=(k_ == qb)),
                              r=[B_qkv, B_wT], w=[PSB[ob]])
                    kb.op('act', lambda e, ob=ob, h=h, qb=qb: e.copy(mixT[:, h, qb * 128:(qb + 1) * 128], psv(ob)[:, 0:128]), r=[PSB[ob]], w=[B_act], accum=True)
            ar.pop()
            dump('sb0', mixT[:, 0, :], [B_act])
            if GDN_ON:
                gdn_mixer(ev, aT, B_aT, bT, B_bT, mixT)
                dump('gd0', mixT[:, 8, :], [B_act])
            kb.barrier()
            hb = [ar.f32(S), ar.f32(S)]
            B_hb = [Buf('hb0'), Buf('hb1')]
            linear_T(mixT, B_act, 16, W[('ev_w_out', ev)], wbufs[('ev_w_out', ev)], [(o * 128, 128) for o in range(16)],
                     add_residual_epi((hb, B_hb)))
            ar.pop()
        if do_mixer and layer % 2 == 1:
            od = layer // 2
            hn = act_view(16)
            norm_T(hT, B_hT, 16, lambda c, layer=layer: scol('norm_mix', layer * 16 + c), hn, B_act)
            kb.barrier()
            ar.push()
            oWin = W[('od_w_in', od)]
            B_Win = wbufs[('od_w_in', od)]
            lgT = ar.f32(S)
            B_lgT = Buf('lgT')
            ar.push()
            cosT = ar.f32(S)
            sinT = ar.f32(S)
            B_cs = Buf('cossin')
            kpe = [ar.bf16(S), ar.bf16(S)]
            B_kpe = Buf('kpe')
            ar.push()
            posi = ar.f32(S).bitcast(I32)
            B_posi = Buf('posi')
            invf = ar.f32(8)
            B_invf = Buf('invf')
            u_ = ar.f32(S)
            B_u = Buf('ropeu')
            ki = ar.f32(S).bitcast(I32)
            B_ki = Buf('ropeki')
            kf = ar.f32(S)
            B_kf = Buf('ropekf')
            m_ = ar.f32(S)
            B_m = Buf('ropem')
            kb.op('sp', lambda e: e.dma_start(out=posi[0:32, :], in_=pos_in.partition_broadcast(32)), w=[B_posi], dma=B_posi)
            kb.op('sp', lambda e: e.dma_start(out=invf[0:32, 0:1], in_=invf_in), w=[B_invf], dma=B_invf)
            kb.op('dve', lambda e: e.tensor_copy(out=u_[0:32, :], in_=posi[0:32, :]), r=[B_posi], w=[B_u])
            kb.op('dve', lambda e: e.tensor_scalar(out=u_[0:32, :], in0=u_[0:32, :], scalar1=invf[0:32, 0:1], scalar2=float(1.0 / (2 * np.pi)),
                                                   op0=ALU.mult, op1=ALU.mult), r=[B_u, B_invf], w=[B_u])
            kb.op('dve', lambda e: e.tensor_copy(out=ki[0:32, :], in_=u_[0:32, :]), r=[B_u], w=[B_ki])
            kb.op('dve', lambda e: e.tensor_copy(out=kf[0:32, :], in_=ki[0:32, :]), r=[B_ki], w=[B_kf])
            kb.op('dve', lambda e: e.tensor_tensor(out=u_[0:32, :], in0=u_[0:32, :], in1=kf[0:32, :], op=ALU.subtract), r=[B_u, B_kf], w=[B_u])
            for dst, shift in ((sinT, 0.0), (cosT, 0.25)):
                kb.op('dve', lambda e, shift=shift: e.tensor_scalar(out=kf[0:32, :], in0=u_[0:32, :], scalar1=shift, scalar2=None, op0=ALU.add), r=[B_u], w=[B_kf])
                for _rep in range(2):
                    kb.op('dve', lambda e: e.tensor_scalar(out=m_[0:32, :], in0=kf[0:32, :], scalar1=0.5, scalar2=None, op0=ALU.is_gt), r=[B_kf], w=[B_m])
                    kb.op('dve', lambda e: e.tensor_tensor(out=kf[0:32, :], in0=kf[0:32, :], in1=m_[0:32, :], op=ALU.subtract), r=[B_kf, B_m], w=[B_kf])
                    kb.op('dve', lambda e: e.tensor_scalar(out=m_[0:32, :], in0=kf[0:32, :], scalar1=-0.5, scalar2=None, op0=ALU.is_lt), r=[B_kf], w=[B_m])
                    kb.op('dve', lambda e: e.tensor_tensor(out=kf[0:32, :], in0=kf[0:32, :], in1=m_[0:32, :], op=ALU.add), r=[B_kf, B_m], w=[B_kf])
                kb.op('act', lambda e, dst=dst: e.activation(out=dst[0:32, :], in_=kf[0:32, :], func=AF.Sin, scale=float(2 * np.pi * 0.999999)), r=[B_kf], w=[B_cs], accum=True)
            ar.pop()
            kb.barrier()
            ar.push()
            stf = [ar.f32(S), ar.f32(S)]
            B_stf = [Buf('stf0'), Buf('stf1')]
            tmpr = [ar.f32(S), ar.f32(S)]
            B_tmpr = Buf('tmpr')

            def rope_apply(x1, x2, B_x, o1, o2, B_o, acc):
                c_, s_ = cosT[0:32, :], sinT[0:32, :]
                t1, t2 = tmpr[0][0:32, :], tmpr[1][0:32, :]
                kb.op('dve', lambda e: e.tensor_tensor(out=t1, in0=x1, in1=c_, op=ALU.mult), r=B_x + [B_cs], w=[B_tmpr])
                kb.op('dve', lambda e: e.tensor_tensor(out=t2, in0=x2, in1=s_, op=ALU.mult), r=B_x + [B_cs], w=[B_tmpr], accum=True)
                kb.op('dve', lambda e: e.tensor_tensor(out=o1, in0=t1, in1=t2, op=ALU.subtract), r=[B_tmpr], w=[B_o], accum=acc)
                kb.op('dve', lambda e: e.tensor_tensor(out=t1, in0=x2, in1=c_, op=ALU.mult), r=B_x + [B_cs, B_o], w=[B_tmpr])
                kb.op('dve', lambda e: e.tensor_tensor(out=t2, in0=x1, in1=s_, op=ALU.mult), r=B_x + [B_cs], w=[B_tmpr], accum=True)
                kb.op('dve', lambda e: e.tensor_tensor(out=o2, in0=t1, in1=t2, op=ALU.add), r=[B_tmpr], w=[B_o], accum=True)

            cqkvr = cqkv.rearrange("(c p) t -> c p t", p=128)
            glqkr = glqk.rearrange("(c p) t -> c p t", p=128)

            def f32_store_epi(dstr, B_d):
                def epi(pi, half):
                    sl = pi % 2
                    kb.op('act', lambda e: e.copy(stf[sl], ps_half(half)), r=PSH(half), w=[B_stf[sl]])
                    kb.op('pool', lambda e: e.dma_start(out=dstr[pi], in_=stf[sl]), r=[B_stf[sl]], w=[B_d], dma=B_stf[sl], accum=True)
                return epi
            linear_T(hn, B_act, 16, oWin, B_Win, [(c * 128, 128) for c in range(8)], f32_store_epi(cqkvr, B_cqkv), SW=128)
            linear_T(hn, B_act, 16, oWin, B_Win, [(1088 + c * 128, 128) for c in range(8)], f32_store_epi(glqkr, B_glqk), SW=128)
            def kr_epi(pi, half):
                if pi < 2:
                    kb.op('act', lambda e: e.copy(stf[pi][0:32, :], ps_half(half, 32)), r=PSH(half), w=[B_stf[pi]])
                    if pi == 1:
                        rope_apply(stf[0][0:32, :], stf[1][0:32, :], [B_stf[0], B_stf[1]], kpe[0][0:32, :], kpe[1][0:32, :], B_kpe, False)
                else:
                    kb.op('act', lambda e: e.copy(lgT[0:16, :], ps_half(half, 16)), r=PSH(half), w=[B_lgT])
            linear_T(hn, B_act, 16, oWin, B_Win, [(1024, 32), (1056, 32), (3136, 16)], kr_epi, SW=128)
            gateTr = gateT.rearrange("(c p) t -> c p t", p=128)
            stbb = [stf[0].bitcast(BF16)[:, 0:S], stf[1].bitcast(BF16)[:, 0:S]]

            def gate_epi(pi, half):
                sl = pi % 2
                kb.op('act', lambda e: e.activation(out=stbb[sl], in_=ps_half(half), func=AF.Silu), r=PSH(half), w=[B_stf[sl]])
                kb.op('pool', lambda e: e.dma_start(out=gateTr[pi], in_=stbb[sl]), r=[B_stf[sl]], w=[B_gateT], dma=B_stf[sl], accum=True)
            linear_T(hn, B_act, 16, oWin, B_Win, [(3152 + c * 128, 128) for c in range(8)], gate_epi, SW=128)
            oWr_ = oWin.rearrange("(c p) n -> p c n", p=128)
            lvc = [0]

            def lv_epi(tb, cb, n, bk):
                sl = lvc[0] % 2
                lvc[0] += 1
                kb.op('act', lambda e: e.copy(stf[sl][:, 0:n], psv(bk)[:, 0:n]), r=[PSB[bk]], w=[B_stf[sl]])
                kb.op('pool', lambda e: e.dma_start(out=glv[tb * 128:(tb + 1) * 128, cb * 128:cb * 128 + n], in_=stf[sl][:, 0:n]),
                      r=[B_stf[sl]], w=[B_glv], dma=B_stf[sl], accum=True)
            linear_tok(hn, B_act, 16, lambda c0, n: oWr_[:, :, 2112 + c0:2112 + c0 + n], B_Win, 1024, S, lv_epi, CW=128)
            cqn = actbuf[:, 0:4 * S].rearrange("p (c t) -> p c t", c=4)
            ckvn = actbuf[:, 4 * S:8 * S].rearrange("p (c t) -> p c t", c=4)
            B_cqn = Buf('cqn')
            B_ckvn = Buf('ckvn')
            norm_T(cqkv[0:512, :], B_cqkv, 4, lambda c, od=od: scol('od_q_norm', od * 4 + c), cqn, B_cqn)
            norm_T(cqkv[512:1024, :], B_cqkv, 4, lambda c, od=od: scol('od_kv_norm', od * 4 + c), ckvn, B_ckvn)
            kb.barrier()
            sbqkr = sbqk.rearrange("(c p) t -> c p t", p=128)
            mqrr = mqr.rearrange("(c p) t -> c p t", p=32)
            qrs = [stf[0][:, 1024:2048].bitcast(BF16), stf[1][:, 1024:2048].bitcast(BF16)]
            B_qrs = Buf('qrs')
            Wuq = W[('od_w_uq', od)]
            opieces = []
            for h in range(8):
                opieces += [(h * 192, 128), (h * 192 + 128, 32), (h * 192 + 160, 32)]

            def uq_epi(pi, half):
                h, k_ = pi // 3, pi % 3
                if k_ == 0:
                    sl = h % 2
                    kb.op('act', lambda e: e.copy(stbb[sl], ps_half(half)), r=PSH(half), w=[B_stf[sl]])
                    kb.op('pool', lambda e: e.dma_start(out=sbqkr[h], in_=stbb[sl]), r=[B_stf[sl]], w=[B_sbqk], dma=B_stf[sl], accum=True)
                else:
                    xb_ = tmpr
                    kb.op('act', lambda e: e.copy(xr[k_ - 1][0:32, :], ps_half(half, 32)), r=PSH(half), w=[B_xr[k_ - 1]])
                    if k_ == 2:
                        rope_apply(xr[0][0:32, :], xr[1][0:32, :], [B_xr[0], B_xr[1]], qrs[0][0:32, :], qrs[1][0:32, :], B_qrs, False)
                        for j in range(2):
                            kb.op('pool', lambda e, j=j: e.dma_start(out=mqrr[h * 2 + j], in_=qrs[j][0:32, :]), r=[B_qrs], w=[B_mqr], dma=B_qrs, accum=True)
            xr = [ar.f32(S), ar.f32(S)]
            B_xr = [Buf('xr0'), Buf('xr1')]
            linear_T(cqn, B_cqn, 4, Wuq, wbufs[('od_w_uq', od)], opieces, uq_epi, SW=128)
            Wukv = W[('od_w_ukv', od)]

            def uk_epi(pi, half):
                sl = pi % 2
                kb.op('act', lambda e: e.copy(stbb[sl], ps_half(half)), r=PSH(half), w=[B_stf[sl]])
                kb.op('pool', lambda e: e.dma_start(out=sbqkr[8 + pi], in_=stbb[sl]), r=[B_stf[sl]], w=[B_sbqk], dma=B_stf[sl], accum=True)
            linear_T(ckvn, B_ckvn, 4, Wukv, wbufs[('od_w_ukv', od)], [(h * 256, 128) for h in range(8)], uk_epi, SW=128)
            Wukvr = Wukv.rearrange("(c p) n -> p c n", p=128)
            mstv = [ar.bf16(256), ar.bf16(256)]
            B_stv = [Buf('stv0'), Buf('stv1')]
            mvcnt = [0]

            def mv_epi(tb, cb, n, bk):
                sl = mvcnt[0] % 2
                mvcnt[0] += 1
                kb.op('act', lambda e: e.copy(mstv[sl][:, 0:n], psv(bk)[:, 0:n]), r=[PSB[bk]], w=[B_stv[sl]])
                kb.op('pool', lambda e: e.dma_start(out=sbv[tb * 128:(tb + 1) * 128, cb * 128:cb * 128 + n], in_=mstv[sl][:, 0:n]),
                      r=[B_stv[sl]], w=[B_sbv], dma=B_stv[sl], accum=True)

            def vsel(c0, n):
                h0 = c0 // 128
                return [(j * 128, 128, Wukvr[:, :, (h0 + j) * 256 + 128:(h0 + j) * 256 + 256]) for j in range(n // 128)]
            linear_tok(ckvn, B_ckvn, 4, vsel, wbufs[('od_w_ukv', od)], 1024, S, mv_epi, CW=128)
            ar.pop()
            kb.barrier()
            mixT = act_view(16)
            ar.push()
            scm = 192 ** -0.5
            mqh = ar.bf16(S)
            kh_ = ar.bf16(S)
            mvh = ar.bf16(S).rearrange("p (c d) -> p c d", c=16)
            qr_ = [ar.bf16(S), ar.bf16(S)]
            B_qkv = Buf('sbqkvh')
            scb = ar.f32(S)
            B_scb = Buf('mlasc')
            mwbuf = ar.bf16(S)
            B_w_ = Buf('sbw')
            mwT = ar.bf16(S).rearrange("p (c q) -> p c q", c=16)
            B_wT = Buf('sbwT')
            mstat = ar.f32(8)
            B_stat = Buf('mlastat')
            sbvr = sbv.rearrange("(c p) n -> p c n", p=128)
            negup = consts[:, C_NEGUP:C_NEGUP + 128]
            for h in range(8):
                kb.op('sp', lambda e, h=h: e.dma_start(out=mqh, in_=sbqkr[h]), r=[B_sbqk], w=[B_qkv], dma=B_qkv)
                kb.op('sp', lambda e, h=h: e.dma_start(out=kh_, in_=sbqkr[8 + h]), r=[B_sbqk], w=[B_qkv], dma=B_qkv, accum=True)
                kb.op('sp', lambda e, h=h: e.dma_start(out=mvh, in_=sbvr[:, :, h * 128:(h + 1) * 128]), r=[B_sbv], w=[B_qkv], dma=B_qkv, accum=True)
                for j in range(2):
                    kb.op('sp', lambda e, h=h, j=j: e.dma_start(out=qr_[j][0:32, :], in_=mqrr[h * 2 + j]), r=[B_mqr], w=[B_qkv], dma=B_qkv, accum=True)
                for qb in range(16):
                    nk = (qb + 1) * 128
                    nt = (nk + 511) // 512
                    qs = slice(qb * 128, (qb + 1) * 128)
                    for t in range(nt):
                        w_ = min(512, nk - t * 512)
                        ks = slice(t * 512, t * 512 + w_)
                        kb.op('pe', lambda e, qs=qs, ks=ks, t=t, w_=w_: e.matmul(psv(t)[:, 0:w_], mqh[:, qs], kh_[:, ks], start=True, stop=False), r=[B_qkv], w=[PSB[t]])
                        kb.op('pe', lambda e, qs=qs, ks=ks, t=t, w_=w_: e.matmul(psv(t)[:, 0:w_], qr_[0][0:32, qs], kpe[0][0:32, ks], start=False, stop=False), r=[B_qkv, B_kpe], w=[PSB[t]])
                        kb.op('pe', lambda e, qs=qs, ks=ks, t=t, w_=w_: e.matmul(psv(t)[:, 0:w_], qr_[1][0:32, qs], kpe[1][0:32, ks], start=False, stop=True), r=[B_qkv, B_kpe], w=[PSB[t]])
                    zps = ps_t[:, 0:nk]
                    ZB = PSB[0:nt]
                    dg = slice(nk - 128, nk)
                    kb.op('act', lambda e, zps=zps, nk=nk: e.activation(out=scb[:, 0:nk], in_=zps, func=AF.Identity, scale=scm), r=ZB, w=[B_scb])
                    kb.op('dve', lambda e, dg=dg: e.tensor_tensor(out=scb[:, dg], in0=scb[:, dg], in1=negup, op=ALU.add), r=[B_scb, B_const], w=[B_scb])
                    kb.op('dve', lambda e, nk=nk: e.tensor_reduce(out=mstat[:, 0:1], in_=scb[:, 0:nk], axis=AX.X, op=ALU.max), r=[B_scb], w=[B_stat])
                    kb.op('dve', lambda e: e.tensor_scalar(out=mstat[:, 1:2], in0=mstat[:, 0:1], scalar1=-1.0, scalar2=None, op0=ALU.mult), r=[B_stat], w=[B_stat])
                    kb.op('act', lambda e, nk=nk: e.activation(out=mwbuf[:, 0:nk], in_=scb[:, 0:nk], func=AF.Exp, bias=mstat[:, 1:2]), r=[B_scb, B_stat], w=[B_w_])
                    kb.op('dve', lambda e, nk=nk: e.tensor_reduce(out=mstat[:, 2:3], in_=mwbuf[:, 0:nk], axis=AX.X, op=ALU.add), r=[B_w_], w=[B_stat])
                    kb.op('dve', lambda e: e.reciprocal(out=mstat[:, 3:4], in_=mstat[:, 2:3]), r=[B_stat], w=[B_stat])
                    kb.op('dve', lambda e, nk=nk: e.tensor_scalar(out=mwbuf[:, 0:nk], in0=mwbuf[:, 0:nk], scalar1=mstat[:, 3:4], scalar2=None, op0=ALU.mult), r=[B_w_, B_stat], w=[B_w_])
                    for k_ in range(qb + 1):
                        bk = 4 + k_ // 8
                        kb.op('pe', lambda e, k_=k_, bk=bk: e.transpose(psbf(bk)[:, (k_ % 8) * 128:(k_ % 8 + 1) * 128], mwbuf[:, k_ * 128:(k_ + 1) * 128], identb),
                              r=[B_w_, B_identb], w=[PSB[bk]])
                    for g_ in range((qb + 8) // 8):
                        n_ = min(8, qb + 1 - g_ * 8)
                        if g_ == 0:
                            kb.op('act', lambda e, n_=n_: e.copy(mwT.rearrange("p c q -> p (c q)")[:, 0:n_ * 128], psbf(4)[:, 0:n_ * 128]), r=[PSB[4]], w=[B_wT])
                        else:
                            kb.op('dve', lambda e, n_=n_: e.tensor_copy(out=mwT.rearrange("p c q -> p (c q)")[:, 1024:1024 + n_ * 128], in_=psbf(5)[:, 0:n_ * 128]),
                                  r=[PSB[5]], w=[B_wT], accum=True)
                    ob = 6 + qb % 2
                    for k_ in range(qb + 1):
                        kb.op('pe', lambda e, k_=k_, ob=ob, qb=qb: e.matmul(psv(ob)[:, 0:128], mvh[:, k_, :], mwT[:, k_, :], start=(k_ == 0), stop=(k_ == qb)),
                              r=[B_qkv, B_wT], w=[PSB[ob]])
                    kb.op('act', lambda e, ob=ob, h=h, qs=qs: e.copy(mixT[:, h, qs], psv(ob)[:, 0:128]), r=[PSB[ob]], w=[B_act], accum=True)
            ar.pop()
            ar.pop()
            dump('ml0', mixT[:, 0, :], [B_act])
            if GLA_ON:
                gla_mixer(od, lgT, B_lgT, mixT)
                dump('gl0', mixT[:, 8, :], [B_act])
            kb.barrier()
            ohb = [ar.f32(S), ar.f32(S)]
            B_hb = [Buf('hb0'), Buf('hb1')]
            linear_T(mixT, B_act, 16, W[('od_w_out', od)], wbufs[('od_w_out', od)], [(o_ * 128, 128) for o_ in range(16)],
                     add_residual_epi((ohb, B_hb)), SW=128)
            ar.pop()
        if do_xattn:
            hn = act_view(16)
            norm_T(hT, B_hT, 16, lambda c, layer=layer: scol('norm_xattn', layer * 16 + c), hn, B_act)
            kb.barrier()
            ar.push()
            qT = ar.bf16(4 * S).rearrange("p (c t) -> p c t", c=4)
            B_qT = Buf('xqT')
            kT = ar.bf16(4 * MEM).rearrange("p (c t) -> p c t", c=4)
            B_kT = Buf('xkT')
            vtk = ar.bf16(2 * 512).rearrange("p (c t) -> p c t", c=2)
            B_vtk = Buf('xv')
            xoT = ar.bf16(4 * S).rearrange("p (c t) -> p c t", c=4)
            B_xoT = Buf('xoT')
            linear_T(hn, B_act, 16, W[('xa_wq', layer)], wbufs[('xa_wq', layer)], [(h * 128, 128) for h in range(4)],
                     copy_epi(lambda pi: qT[:, pi, :], B_qT))
            linear_T(memnT, B_memnT, 16, W[('xa_wk', layer)], wbufs[('xa_wk', layer)], [(h * 128, 128) for h in range(4)],
                     copy_epi(lambda pi: kT[:, pi, :], B_kT, ntok=MEM), ntok=MEM)
            Wv = W[('xa_wv', layer)].rearrange("(c p) n -> p c n", p=128)

            def v_epi(tb, cb, n, bk):
                kb.op('act', lambda e: e.copy(vtk[:, tb, cb * 256:cb * 256 + n], psv(bk)[:, 0:n]), r=[PSB[bk]], w=[B_vtk], accum=True)
            linear_tok(memnT, B_memnT, 16, lambda c0, n: Wv[:, :, c0:c0 + n], wbufs[('xa_wv', layer)], 512, MEM, v_epi)
            dump('xq0', qT[:, 0, :], [B_qT])
            dump('xk0', kT[:, 0, :], [B_kT])
            dump('xv', vtk.rearrange("p c t -> p (c t)"), [B_vtk])
            kb.barrier()
            sc = 128 ** -0.5
            NS = 3
            pr = [ar.bf16(MEM) for _ in range(NS)]
            B_pr = [Buf('xpr%d' % i) for i in range(NS)]
            pT = [ar.bf16(MEM).rearrange("p (c t) -> p c t", c=2) for _ in range(NS)]
            B_pT = [Buf('xpT%d' % i) for i in range(NS)]
            stat = [ar.f32(8) for _ in range(NS)]
            B_stat = [Buf('xstat%d' % i) for i in range(NS)]
            it = 0
            for h in range(4):
                for qb in range(16):
                    sl = it % NS
                    b1 = (it * 2) % 8
                    b2 = (it * 2 + 1) % 8
                    it += 1
                    st_ = stat[sl]
                    kb.op('pe', lambda e, h=h, qb=qb, b1=b1: e.matmul(psv(b1)[:, 0:MEM], qT[:, h, qb * 128:(qb + 1) * 128], kT[:, h, :],
                                                                      start=True, stop=True), r=[B_qT, B_kT], w=[PSB[b1]])
                    kb.op('dve', lambda e, b1=b1, st_=st_: e.tensor_reduce(out=st_[:, 0:1], in_=psv(b1)[:, 0:MEM], axis=AX.X, op=ALU.max),
                          r=[PSB[b1]], w=[B_stat[sl]])
                    kb.op('dve', lambda e, st_=st_: e.tensor_scalar(out=st_[:, 1:2], in0=st_[:, 0:1], scalar1=-sc, scalar2=None, op0=ALU.mult),
                          r=[B_stat[sl]], w=[B_stat[sl]])
                    kb.op('act', lambda e, b1=b1, sl=sl, st_=st_: e.activation(out=pr[sl], in_=psv(b1)[:, 0:MEM], func=AF.Exp, scale=sc,
                                                                             bias=st_[:, 1:2]),
                          r=[PSB[b1], B_stat[sl]], w=[B_pr[sl]])
                    kb.op('dve', lambda e, sl=sl, st_=st_: e.tensor_reduce(out=st_[:, 2:3], in_=pr[sl], axis=AX.X, op=ALU.add),
                          r=[B_pr[sl]], w=[B_stat[sl]])
                    kb.op('dve', lambda e, st_=st_: e.reciprocal(out=st_[:, 3:4], in_=st_[:, 2:3]), r=[B_stat[sl]], w=[B_stat[sl]])
                    kb.op('dve', lambda e, sl=sl, st_=st_: e.tensor_scalar(out=pr[sl], in0=pr[sl], scalar1=st_[:, 3:4], scalar2=None, op0=ALU.mult),
                          r=[B_pr[sl], B_stat[sl]], w=[B_pr[sl]])
                    for mb in range(2):
                        kb.op('pe', lambda e, sl=sl, mb=mb, b2=b2: e.transpose(psbf(b2)[:, mb * 128:(mb + 1) * 128], pr[sl][:, mb * 128:(mb + 1) * 128], identb),
                              r=[B_pr[sl], B_identb], w=[PSB[b2]])
                    kb.op('act', lambda e, sl=sl, b2=b2: e.copy(pT[sl].rearrange("p c t -> p (c t)"), psbf(b2)[:, 0:256]), r=[PSB[b2]], w=[B_pT[sl]])
                    for mb in range(2):
                        kb.op('pe', lambda e, sl=sl, mb=mb, b2=b2, h=h: e.matmul(psv(b2)[:, 256:384], vtk[:, mb, h * 128:(h + 1) * 128], pT[sl][:, mb, :],
                                                                               start=(mb == 0), stop=(mb == 1)), r=[B_vtk, B_pT[sl]], w=[PSB[b2]])
                    kb.op('dve', lambda e, b2=b2, h=h, qb=qb: e.tensor_copy(out=xoT[:, h, qb * 128:(qb + 1) * 128], in_=psv(b2)[:, 256:384]),
                          r=[PSB[b2]], w=[B_xoT], accum=True)
            dump('xo0', xoT[:, 0, :], [B_xoT])
            kb.barrier()
            hb = [ar.f32(S), ar.f32(S)]
            B_hb = [Buf('hb0'), Buf('hb1')]
            linear_T(xoT, B_xoT, 4, W[('xa_wo', layer)], wbufs[('xa_wo', layer)], [(o * 128, 128) for o in range(16)],
                     add_residual_epi((hb, B_hb)))
            ar.pop()
        if do_ffn:
            hn = act_view(16)
            norm_T(hT, B_hT, 16, lambda c, layer=layer: scol('norm_ffn', layer * 16 + c), hn, B_act)
            kb.barrier()
            ar.push()
            xb = [ar.f32(S + 8) for _ in range(2)]
            B_xb = [Buf('xb0'), Buf('xb1')]
            cu = ar.f32(S)
            B_cu = Buf('cu')
            cz = ar.f32(S)
            B_cz = Buf('cz')
            gs = [ar.bf16(S), ar.bf16(S)]
            B_gs = [Buf('gs0'), Buf('gs1')]
            for i in range(2):
                kb.op('pool', lambda e, i=i: e.memset(xb[i][:, 0:8], 0.0), w=[B_xb[i]])
            pieces = []
            for j in range(44):
                pieces.append((j * 128, 128))
                pieces.append((DFF + j * 128, 128))
            gTr = gT.rearrange("(c p) t -> c p t", p=128)

            def ffn_epi(pi, half, layer=layer):
                j = pi // 2
                isz = pi % 2
                ch = j + 44 * isz
                sl = isz
                dstc = cz if isz else cu
                B_d = B_cz if isz else B_cu

                def wcol(tap):
                    return scol('ffn_conv', (layer * 3 + tap) * 88 + ch)
                bcol = scol('ffn_conv_b', layer * 88 + ch)
                kb.op('act', lambda e, sl=sl, half=half: e.copy(xb[sl][:, 8:8 + S], ps_half(half)),
                      r=PSH(half), w=[B_xb[sl]], accum=True)
                kb.op('act', lambda e, sl=sl: e.activation(out=dstc, in_=xb[sl][:, 8:8 + S], func=AF.Identity,
                                                            scale=wcol(2), bias=bcol), r=[B_xb[sl], B_small], w=[B_d])
                kb.op('dve', lambda e, sl=sl: e.scalar_tensor_tensor(out=dstc, in0=xb[sl][:, 7:7 + S], scalar=wcol(1), in1=dstc,
                                                                       op0=ALU.mult, op1=ALU.add), r=[B_xb[sl], B_d, B_small], w=[B_d])
                kb.op('dve', lambda e, sl=sl: e.scalar_tensor_tensor(out=dstc, in0=xb[sl][:, 6:6 + S], scalar=wcol(0), in1=dstc,
                                                                       op0=ALU.mult, op1=ALU.add), r=[B_xb[sl], B_d, B_small], w=[B_d])
                if isz:
                    g2 = j % 2
                    kb.op('act', lambda e: e.activation(out=cz, in_=cz, func=AF.Silu), r=[B_cz], w=[B_cz])
                    kb.op('pool', lambda e, g2=g2: e.tensor_tensor(out=gs[g2], in0=cz, in1=cu, op=ALU.mult),
                          r=[B_cz, B_cu], w=[B_gs[g2]])
                    kb.op('pool', lambda e, g2=g2, j=j: e.dma_start(out=gTr[j], in_=gs[g2]), r=[B_gs[g2]], w=[B_gT],
                          dma=B_gs[g2], accum=True)
            linear_T(hn, B_act, 16, W[('ffn_w_in', layer)], wbufs[('ffn_w_in', layer)], pieces, ffn_epi)
            ar.pop()
            for khalf in range(2):
                kb.barrier()
                gv = act_view(22)
                gTk = gT[khalf * 22 * 128:(khalf + 1) * 22 * 128, :].rearrange("(c p) t -> p c t", p=128)
                for q in range(2):
                    kb.op('sp', lambda e, q=q, gTk=gTk, gv=gv: e.dma_start(out=gv[:, q * 11:(q + 1) * 11, :], in_=gTk[:, q * 11:(q + 1) * 11, :]),
                          r=[B_gT], w=[B_act], dma=B_act, accum=(q > 0))
                ar.push()
                hb = [ar.f32(S), ar.f32(S)]
                B_hb = [Buf('hb0'), Buf('hb1')]
                Wo = W[('ffn_w_out', layer)][khalf * 22 * 128:(khalf + 1) * 22 * 128, :]
                linear_T(gv, B_act, 22, Wo, wbufs[('ffn_w_out', layer)], [(o * 128, 128) for o in range(16)],
                         add_residual_epi((hb, B_hb)))
                ar.pop()

    fo = act_view(16)
    kb.barrier()
    ar.push()
    fin = arena_t[:, actbuf_off:actbuf_off + 16 * 512].rearrange("p (c t) -> p c t", c=16)
    B_fin = Buf('fin')
    ost = [ar.f32(D)] * 2
    B_ost = [Buf('ost0')] * 2
    for t in range(4):
        norm_T(hT[:, t * 512:(t + 1) * 512], B_hT, 16, lambda c: scol('final_norm', c), fin, B_fin, ncols=512)
        kb.barrier()
        for tb in range(4):
            sl = tb % 2
            for q in range(4):
                bk = (tb * 4 + q) % 8
                for j in range(4):
                    fc = q * 4 + j
                    kb.op('pe', lambda e, fc=fc, tb=tb, bk=bk, j=j: e.transpose(
                        psv(bk)[:, j * 128:(j + 1) * 128], fin[:, fc, tb * 128:(tb + 1) * 128], ident),
                        r=[B_fin, B_const], w=[PSB[bk]])
                if q % 2 == 0:
                    kb.op('act', lambda e, sl=sl, q=q, bk=bk: e.copy(ost[sl][:, q * 512:(q + 1) * 512], psv(bk)),
                          r=[PSB[bk]], w=[B_ost[sl]], accum=(q > 0))
                else:
                    kb.op('dve', lambda e, sl=sl, q=q, bk=bk: e.tensor_copy(out=ost[sl][:, q * 512:(q + 1) * 512], in_=psv(bk)),
                          r=[PSB[bk]], w=[B_ost[sl]], accum=True)
            r0 = t * 512 + tb * 128
            kb.op('pool', lambda e, sl=sl, r0=r0: e.dma_start(out=out_t[r0:r0 + 128, :], in_=ost[sl]),
                  r=[B_ost[sl]], w=[Buf('outdram')], dma=B_ost[sl])
    ar.pop()
    kb.emit()
    return nc


def pack_small(inputs):
    rows = [_rows128(inputs[k]) for k in SMALL_KEYS]
    a = np.concatenate(rows, 0)
    pad = np.zeros((SMALL_ROWS_PAD - a.shape[0], 128), np.float32)
    return np.concatenate([a, pad], 0)


def make_in_maps(inputs, ncores, layers=(0, 1, 2, 3), batch_ids=None):
    inputs = {k: np.asarray(v) for k, v in inputs.items()}
    small = pack_small(inputs)
    consts = make_consts()
    hp = np.zeros((8, 4), np.float32)
    for e in range(2):
        hp[:, 2 * e] = inputs['ev_a_log'][e]
        hp[:, 2 * e + 1] = inputs['ev_dt_bias'][e]
    w2 = np.ascontiguousarray(np.transpose(inputs['od_gla_w2'], (1, 0, 2))).astype(np.float32)
    invf = (10000.0 ** (-np.arange(0, 64, 2, dtype=np.float32) / 64)).astype(np.float32).reshape(32, 1)
    needed = []
    for l in layers:
        for it in layer_weights(l):
            if it not in needed:
                needed.append(it)
    maps = []
    if batch_ids is None:
        batch_ids = list(range(ncores))
    for c in range(ncores):
        b = batch_ids[c]
        m = {'x': np.ascontiguousarray(inputs['x'][b]), 'mem': np.ascontiguousarray(inputs['mem'][b]),
             'positions': np.ascontiguousarray(inputs['positions'][b]).reshape(1, S).astype(np.int32),
             'smallp': small, 'consts': consts, 'hp': hp, 'gla_w2': w2, 'invf': invf}
        for (n, li) in needed:
            L, K, N = BIGW[n]
            w = inputs[n][li]
            if ncores == 1:
                m['%s_%d' % (n, li)] = np.ascontiguousarray(w)
            else:
                r = K // ncores
                m['%s_%d' % (n, li)] = np.ascontiguousarray(w[c * r:(c + 1) * r])
        maps.append(m)
    return maps


def kernel(**inputs):
    ncores = 8
    nc = build(ncores)
    maps = make_in_maps(inputs, ncores)
    res = run_bass_kernel_spmd(nc, maps, core_ids=list(range(ncores)))
    return np.stack([np.asarray(r['out']) for r in res.results], 0).astype(np.float32)
```

```python
import numpy as np
import concourse.bass as bass
import concourse.mybir as mybir
from concourse.bass_utils import run_bass_kernel_spmd

F32 = mybir.dt.float32
BF16 = mybir.dt.bfloat16
I32 = mybir.dt.int32
AF = mybir.ActivationFunctionType
ALU = mybir.AluOpType
AX = mybir.AxisListType
P = 128
S = 2048
D = 2048
DEPTH = 4
MEM = 256
DFF = 5632
EPS = 1e-6
EVEN_IN = 7184
ODD_IN = 4176
SEG_OPS = 10 ** 9
ENGS = ['pe', 'act', 'dve', 'pool', 'sp']


class Buf:
    def __init__(self, name):
        self.name = name
        self.writers = {}
        self.readers = {}
        self.sem = None
        self.dcount = 0
        self.epoch = {}


class Op:
    __slots__ = ('eng', 'fn', 'signal', 'dma', 'key', 'sigval', 'deps', 'idx', 'inc', 'seg', 'dname')


class KB:
    def __init__(self, nc):
        self.nc = nc
        self.ops = {e: [] for e in ENGS}
        self.pending = {e: {} for e in ENGS}
        self.all_last = {}
        self.n = 0
        self.dcount = {}
        self.seg = 0
        self.seg_start_idx = 0

    def op(self, eng, fn, r=(), w=(), dma=None, accum=False, inc=16):
        o = Op()
        o.eng = eng
        o.fn = fn
        o.signal = False
        o.dma = dma
        o.inc = inc
        o.idx = self.n
        o.seg = self.seg
        self.n += 1
        if dma is not None:
            o.dname = dma.name + '_' + eng
            self.dcount[o.dname] = self.dcount.get(o.dname, 0) + 1
            o.key = ('d', o.dname)
            o.sigval = inc * self.dcount[o.dname]
            o.signal = True
        else:
            o.key = eng
            o.sigval = None
        deps = self.pending[eng]
        self.pending[eng] = {}

        def add(d):
            if d.key == 'pe' and o.key == 'pe':
                return
            c = deps.get(d.key)
            if c is None or c.idx < d.idx:
                deps[d.key] = d
        for b in r:
            for d in b.writers.values():
                add(d)
        for b in w:
            for d in b.readers.values():
                add(d)
            if accum:
                for d in b.epoch.values():
                    add(d)
            else:
                for d in b.writers.values():
                    add(d)
        o.deps = deps
        for b in w:
            if accum:
                b.epoch.update(b.readers)
                b.writers[o.key] = o
            else:
                ep = dict(b.writers)
                ep.update(b.readers)
                b.epoch = ep
                b.writers = {o.key: o}
            b.readers = {}
        for b in r:
            b.readers[o.key] = o
        self.ops[eng].append(o)
        self.all_last[o.key] = o
        return o

    def barrier(self):
        for e in ENGS:
            p = self.pending[e]
            for k, d in self.all_last.items():
                c = p.get(k)
                if c is None or c.idx < d.idx:
                    p[k] = d
        if self.n - self.seg_start_idx > SEG_OPS:
            self.seg += 1
            self.seg_start_idx = self.n

    def emit(self):
        nc = self.nc
        self.barrier()
        self.op('sp', lambda e: e.nop())
        nseg = self.seg + 1
        esem = {e: nc.alloc_semaphore('es_' + e) for e in ENGS}
        dsem = {n: nc.alloc_semaphore('ds_%s' % n) for n in self.dcount}
        allsems = list(esem.values()) + list(dsem.values())
        print('n dma sems', len(dsem), 'n ops', {e: len(self.ops[e]) for e in ENGS}, 'segments', nseg)
        segops = [{e: [] for e in ENGS} for _ in range(nseg)]
        for e in ENGS:
            for o in self.ops[e]:
                segops[o.seg][e].append(o)

        def semof(d):
            if d.dma is not None:
                return dsem[d.dname]
            return esem[d.eng]
        for sg in range(nseg):
            ops = segops[sg]
            for e in ENGS:
                for o in ops[e]:
                    o.deps = {k: d for k, d in o.deps.items() if d.seg == sg}
                    for d in o.deps.values():
                        d.signal = True
            dcnt = {}
            lastd = {}
            for e in ENGS:
                cnt = 0
                for o in ops[e]:
                    if o.dma is None:
                        if o.signal:
                            cnt += 1
                            o.sigval = cnt
            dmaops = sorted([o for e in ENGS for o in ops[e] if o.dma is not None], key=lambda o: o.idx)
            for o in dmaops:
                dcnt[o.dname] = dcnt.get(o.dname, 0) + 1
                o.sigval = o.inc * dcnt[o.dname]
                lastd[o.dname] = o
            for sm in allsems:
                nc.sync.sem_clear(sm)
            nc.all_engine_barrier()

            def body(engname, ops=ops, lastd=lastd):
                def f(e):
                    waited = {}
                    for o in ops[engname]:
                        for k, d in o.deps.items():
                            v = d.sigval
                            if waited.get(k, 0) < v:
                                e.wait_ge(semof(d), v)
                                waited[k] = v
                        ins = o.fn(e)
                        if o.dma is not None:
                            ins.then_inc(dsem[o.dname], o.inc)
                        elif o.signal:
                            ins.then_inc(esem[engname], 1)
                    if engname == 'sp':
                        for n_, d in lastd.items():
                            if waited.get(d.key, 0) < d.sigval:
                                e.wait_ge(dsem[n_], d.sigval)
                return f
            with nc.Block() as block:
                block.tensor(body('pe'))
                block.scalar(body('act'))
                block.vector(body('dve'))
                block.gpsimd(body('pool'))
                block.sync(body('sp'))


class Arena:
    def __init__(self, tensor, nwords):
        self.t = tensor
        self.n = nwords
        self.top = 0
        self.marks = []

    def alloc(self, nwords):
        nwords = (nwords + 7) // 8 * 8
        off = self.top
        self.top += nwords
        assert self.top <= self.n, ('SBUF arena overflow', self.top, self.n)
        return off

    def f32(self, n):
        off = self.alloc(n)
        return self.t[:, off:off + n]

    def bf16(self, n):
        off = self.alloc((n + 1) // 2)
        return self.t[:, off:off + (n + 1) // 2].bitcast(BF16)[:, 0:n]

    def push(self):
        self.marks.append(self.top)

    def pop(self):
        self.top = self.marks.pop()


def _rows128(a):
    a = np.ascontiguousarray(a, dtype=np.float32).reshape(-1)
    assert a.size % 128 == 0
    return a.reshape(-1, 128)


SMALL_KEYS = ['norm_mix', 'norm_xattn', 'norm_ffn', 'mem_norm', 'final_norm', 'ffn_conv', 'ffn_conv_b',
              'ev_sconv', 'od_q_norm', 'od_kv_norm', 'od_gla_b2', 'ev_gdn_norm', 'od_gla_norm']
SMALL_SIZES = {'norm_mix': 4 * 2048, 'norm_xattn': 4 * 2048, 'norm_ffn': 4 * 2048, 'mem_norm': 2048,
               'final_norm': 2048, 'ffn_conv': 4 * 3 * 11264, 'ffn_conv_b': 4 * 11264, 'ev_sconv': 2 * 4 * 3072,
               'od_q_norm': 2 * 512, 'od_kv_norm': 2 * 512, 'od_gla_b2': 2 * 512, 'ev_gdn_norm': 2 * 128,
               'od_gla_norm': 2 * 256}
SMALL_OFF = {}
_o = 0
for _k in SMALL_KEYS:
    SMALL_OFF[_k] = _o
    _o += SMALL_SIZES[_k] // 128
SMALL_ROWS = _o
SMALL_ROWS_PAD = (SMALL_ROWS + 127) // 128 * 128

C_IDENT = 0
C_ONES = 128
C_LOW = 256
C_SLOW = 384
C_UP = 512
C_SUP = 640
C_NEGUP = 768
C_SEL = 896
NCONST = 896 + 1024


def make_consts():
    c = np.zeros((128, NCONST), np.float32)
    p = np.arange(128)[:, None]
    x = np.arange(128)[None, :]
    c[:, C_IDENT:C_IDENT + 128] = (p == x)
    c[:, C_ONES:C_ONES + 128] = 1.0
    c[:, C_LOW:C_LOW + 128] = (p >= x)
    c[:, C_SLOW:C_SLOW + 128] = (p > x)
    c[:, C_UP:C_UP + 128] = (p <= x)
    c[:, C_SUP:C_SUP + 128] = (p < x)
    c[:, C_NEGUP:C_NEGUP + 128] = np.where(p >= x, 0.0, -30000.0)
    for h in range(8):
        c[h, C_SEL + h * 128:C_SEL + (h + 1) * 128] = 1.0
    return c


BIGW = {
    'ev_w_in': (2, 2048, EVEN_IN), 'ev_w_out': (2, 2048, 2048), 'od_w_in': (2, 2048, ODD_IN),
    'od_w_uq': (2, 512, 1536), 'od_w_ukv': (2, 512, 2048), 'od_w_out': (2, 2048, 2048),
    'xa_wq': (4, 2048, 512), 'xa_wk': (4, 2048, 512), 'xa_wv': (4, 2048, 512), 'xa_wo': (4, 512, 2048),
    'ffn_w_in': (4, 2048, 2 * DFF), 'ffn_w_out': (4, DFF, 2048),
}


def layer_weights(layer):
    names = ['ev_w_in', 'ev_w_out'] if layer % 2 == 0 else ['od_w_in', 'od_w_uq', 'od_w_ukv', 'od_w_out']
    names += ['xa_wq', 'xa_wk', 'xa_wv', 'xa_wo', 'ffn_w_in', 'ffn_w_out']
    return [(n, layer // 2 if n[:2] in ('ev', 'od') else layer) for n in names]


def build(ncores, layers=(0, 1, 2, 3), do_mixer=True, do_xattn=True, do_ffn=True, dbg=None, GDN_ON=True, GL=9, GH=8, GLA_ON=True):
    nc = bass.Bass("TRN2", target_bir_lowering=False)
    kb = KB(nc)
    x_in = nc.dram_tensor("x", [S, D], F32, kind="ExternalInput").ap()
    mem_in = nc.dram_tensor("mem", [MEM, D], F32, kind="ExternalInput").ap()
    pos_in = nc.dram_tensor("positions", [1, S], I32, kind="ExternalInput").ap()
    small_in = nc.dram_tensor("smallp", [SMALL_ROWS_PAD, 128], F32, kind="ExternalInput").ap()
    const_in = nc.dram_tensor("consts", [128, NCONST], F32, kind="ExternalInput").ap()
    hp_in = nc.dram_tensor("hp", [8, 4], F32, kind="ExternalInput").ap()
    w2_in = nc.dram_tensor("gla_w2", [16, 2, 512], F32, kind="ExternalInput").ap()
    invf_in = nc.dram_tensor("invf", [32, 1], F32, kind="ExternalInput").ap()
    out_t = nc.dram_tensor("out", [S, D], F32, kind="ExternalOutput").ap()

    W = {}
    wbufs = {}
    needed = []
    for l in layers:
        for (n, li) in layer_weights(l):
            if (n, li) not in needed:
                needed.append((n, li))
    cc_list = []
    wl = {}
    for l in layers:
        for it in layer_weights(l):
            wl.setdefault(it, l)
    LWB = {l: Buf('ccL%d' % l) for l in layers}
    B_bounce = Buf('bounce')
    for (n, li) in needed:
        L, K, N = BIGW[n]
        if ncores == 1:
            t = nc.dram_tensor("%s_%d" % (n, li), [K, N], F32, kind="ExternalInput").ap()
            W[(n, li)] = t
            wbufs[(n, li)] = Buf('w1')
        else:
            sh = nc.dram_tensor("%s_%d" % (n, li), [K // ncores, N], F32, kind="ExternalInput").ap()
            bo = nc.dram_tensor("b_%s_%d" % (n, li), [K // ncores, N], F32).ap()
            g = nc.dram_tensor("g_%s_%d" % (n, li), [K, N], F32).ap()
            W[(n, li)] = g
            wbufs[(n, li)] = LWB[wl[(n, li)]]
            cc_list.append((sh, bo, g, wbufs[(n, li)], B_bounce))

    hT = nc.dram_tensor("hT", [D, S], F32).ap()
    B_hT = Buf('hT')
    gT = nc.dram_tensor("gT", [DFF, S], BF16).ap()
    B_gT = Buf('gT')
    sbqk = nc.dram_tensor("sbqk", [2048, S], BF16).ap()
    B_sbqk = Buf('sbqk')
    sbv = nc.dram_tensor("sbv", [S, 1024], BF16).ap()
    B_sbv = Buf('sbv')
    gqkv = nc.dram_tensor("gqkv", [3072, S], F32).ap()
    B_gqkv = Buf('gqkv')
    cqkv = nc.dram_tensor("cqkv", [1024, S], F32).ap()
    B_cqkv = Buf('cqkv')
    glqk = nc.dram_tensor("glqk", [1024, S], F32).ap()
    B_glqk = Buf('glqk')
    glv = nc.dram_tensor("glv", [S, 1024], F32).ap()
    B_glv = Buf('glv')
    mqr = nc.dram_tensor("mqr", [512, S], BF16).ap()
    B_mqr = Buf('mqr')
    gateT = nc.dram_tensor("gateT", [1024, S], BF16).ap()
    B_gateT = Buf('gateT')

    NW = 53200
    arena_t = nc.alloc_sbuf_tensor("arena", [P, NW], F32)
    ar = Arena(arena_t, NW)
    ps_t = nc.alloc_psum_tensor("ps", [P, 4096], F32)
    PSB = [Buf('psb%d' % i) for i in range(8)]

    def psv(b0, nb=1):
        return ps_t[:, b0 * 512:(b0 + nb) * 512]

    bounce_sem = Buf('bounce')
    for (sh, bo, g, wb, bb) in cc_list:
        kb.op('pool', lambda e, sh=sh, bo=bo: e.dma_start(out=bo, in_=sh), w=[bb], dma=bb, accum=True)
    for (sh, bo, g, wb, bb) in cc_list:
        kb.op('pool', lambda e, bo=bo, g=g: e.collective_compute(
            "AllGather", ALU.bypass, replica_groups=[list(range(ncores))], ins=[bo.opt()], outs=[g.opt()]),
            r=[bb], w=[wb], dma=wb, inc=1, accum=True)

    consts = ar.f32(NCONST)
    B_const = Buf('consts')
    kb.op('sp', lambda e: e.dma_start(out=consts, in_=const_in), w=[B_const], dma=B_const)
    ident = consts[:, C_IDENT:C_IDENT + 128]
    ones = consts[:, C_ONES:C_ONES + 128]
    identb = ar.bf16(128)
    B_identb = Buf('identb')
    kb.op('dve', lambda e: e.tensor_copy(out=identb, in_=ident), r=[B_const], w=[B_identb])
    smallc = ar.f32(SMALL_ROWS_PAD)
    B_small = Buf('small')
    ar.push()
    stg = ar.f32(128)
    B_stg = Buf('stg')
    for i in range(SMALL_ROWS_PAD // 128):
        kb.op('sp', lambda e, i=i: e.dma_start(out=stg, in_=small_in[i * 128:(i + 1) * 128, :]), w=[B_stg], dma=B_stg)
        kb.op('pe', lambda e: e.transpose(psv(0)[:, 0:128], stg, ident), r=[B_stg, B_const], w=[PSB[0]])
        kb.op('dve', lambda e, i=i: e.tensor_copy(out=smallc[:, i * 128:(i + 1) * 128], in_=psv(0)[:, 0:128]),
              r=[PSB[0]], w=[B_small], accum=True)
    ar.pop()

    def scol(key, idx):
        c = SMALL_OFF[key] + idx
        return smallc[:, c:c + 1]

    memnT = ar.bf16(16 * MEM).rearrange("p (c t) -> p c t", c=16)
    B_memnT = Buf('memnT')

    dbg_t = nc.dram_tensor("dbg", [128, 8 * 2048], F32, kind="ExternalOutput").ap() if dbg else None
    dstage = ar.f32(2048) if dbg else None
    B_dstage = Buf('dstage')
    dbg_slots = {}

    def dump(name, ap, bufs):
        if not dbg or name not in dbg:
            return
        n = ap.shape[-1]
        slot = len(dbg_slots)
        dbg_slots[name] = (slot, n)
        kb.barrier()
        kb.op('dve', lambda e: e.tensor_copy(out=dstage[0:ap.shape[0], 0:n], in_=ap), r=bufs, w=[B_dstage])
        kb.op('sp', lambda e: e.dma_start(out=dbg_t[0:ap.shape[0], slot * 2048:slot * 2048 + n], in_=dstage[0:ap.shape[0], 0:n]), r=[B_dstage], w=[Buf('dbgd')], dma=B_dstage)
        kb.barrier()
    build.dbg_slots = dbg_slots

    def to_featmajor(src, ntok, dstT, B_dst):
        kb.barrier()
        ar.push()
        xin = [ar.f32(D), ar.f32(D)]
        B_xin = [Buf('xin0'), Buf('xin1')]
        st = [ar.f32(16 * 128), ar.f32(16 * 128)]
        B_st = [Buf('st0'), Buf('st1')]
        dT = dstT.rearrange("(c p) t -> p c t", p=128)
        for tb in range(ntok // 128):
            sl = tb % 2
            kb.op('sp', lambda e, sl=sl, tb=tb: e.dma_start(out=xin[sl], in_=src[tb * 128:(tb + 1) * 128, :]),
                  w=[B_xin[sl]], dma=B_xin[sl])
            for q in range(4):
                bk = (tb * 4 + q) % 8
                for j in range(4):
                    fc = q * 4 + j
                    kb.op('pe', lambda e, sl=sl, fc=fc, bk=bk, j=j: e.transpose(
                        psv(bk)[:, j * 128:(j + 1) * 128], xin[sl][:, fc * 128:(fc + 1) * 128], ident),
                        r=[B_xin[sl], B_const], w=[PSB[bk]])
                eng = 'act' if q % 2 == 0 else 'dve'
                if eng == 'act':
                    kb.op('act', lambda e, sl=sl, q=q, bk=bk: e.copy(st[sl][:, q * 512:(q + 1) * 512], psv(bk)),
                          r=[PSB[bk]], w=[B_st[sl]], accum=(q > 0))
                else:
                    kb.op('dve', lambda e, sl=sl, q=q, bk=bk: e.tensor_copy(out=st[sl][:, q * 512:(q + 1) * 512], in_=psv(bk)),
                          r=[PSB[bk]], w=[B_st[sl]], accum=True)
            kb.op('pool', lambda e, sl=sl, tb=tb: e.dma_start(
                out=dT[:, :, tb * 128:(tb + 1) * 128], in_=st[sl].rearrange("p (c t) -> p c t", c=16)),
                r=[B_st[sl]], w=[B_dst], dma=B_st[sl], accum=True)
        ar.pop()

    def norm_T(srcT, B_src, KC, gcol, dst, B_dst, ncols=S, eps_div=None):
        kb.barrier()
        ar.push()
        TW = min(512, ncols)
        xt = [ar.f32(KC * TW).rearrange("p (c t) -> p c t", c=KC) for _ in range(2)]
        B_xt = [Buf('nxt0'), Buf('nxt1')]
        sq = [ar.f32(TW), ar.f32(TW)]
        B_sq = [Buf('nsq0'), Buf('nsq1')]
        rs = [ar.f32(TW), ar.f32(TW)]
        B_rs = [Buf('nrs0'), Buf('nrs1')]
        sT = srcT.rearrange("(c p) t -> p c t", p=128)
        F = KC * 128
        for t in range(ncols // TW):
            sl = t % 2
            kb.op('sp', lambda e, sl=sl, t=t: e.dma_start(out=xt[sl], in_=sT[:, :, t * TW:(t + 1) * TW]),
                  r=[B_src], w=[B_xt[sl]], dma=B_xt[sl])
            bk = sl
            for c in range(KC):
                s2 = c % 2
                kb.op('act', lambda e, sl=sl, c=c, s2=s2: e.activation(out=sq[s2], in_=xt[sl][:, c, :], func=AF.Square),
                      r=[B_xt[sl]], w=[B_sq[s2]])
                kb.op('pe', lambda e, c=c, s2=s2, bk=bk: e.matmul(psv(bk)[:, 0:TW], ones, sq[s2], start=(c == 0), stop=(c == KC - 1)),
                      r=[B_sq[s2], B_const], w=[PSB[bk]])
            kb.op('act', lambda e, sl=sl, bk=bk: e.activation(out=rs[sl], in_=psv(bk)[:, 0:TW], func=AF.Sqrt, scale=1.0 / F, bias=epsc),
                  r=[PSB[bk], B_eps], w=[B_rs[sl]])
            kb.op('dve', lambda e, sl=sl: e.reciprocal(out=rs[sl], in_=rs[sl]), r=[B_rs[sl]], w=[B_rs[sl]])
            for c in range(KC):
                kb.op('dve', lambda e, sl=sl, c=c, t=t: e.scalar_tensor_tensor(
                    out=dst[:, c, t * TW:(t + 1) * TW], in0=xt[sl][:, c, :], scalar=gcol(c), in1=rs[sl],
                    op0=ALU.mult, op1=ALU.mult), r=[B_xt[sl], B_rs[sl], B_small], w=[B_dst], accum=True)
        ar.pop()

    epsc = ar.f32(1)
    B_eps = Buf('eps')
    kb.op('pool', lambda e: e.memset(epsc, EPS), w=[B_eps])

    def linear_T(xT, B_x, KC, Wap, B_w, pieces, epi, ntok=S, wdt=BF16, SW=256):
        kb.barrier()
        ar.push()
        wf = [ar.f32(KC * SW).rearrange("p (c n) -> p c n", c=KC) for _ in range(2)]
        B_wf = [Buf('wf0'), Buf('wf1')]
        wb = [ar.bf16(KC * SW).rearrange("p (c n) -> p c n", c=KC) for _ in range(2)]
        B_wb = [Buf('wb0'), Buf('wb1')]
        Wr = Wap.rearrange("(c p) n -> p c n", p=128)
        slabs = []
        for pi, (c0, n) in enumerate(pieces):
            if slabs and slabs[-1][0] + slabs[-1][1] == c0 and slabs[-1][1] + n <= SW:
                slabs[-1][1] += n
                slabs[-1][2].append(pi)
            else:
                slabs.append([c0, n, [pi]])
        nt = (ntok + 511) // 512
        cnt = 0
        for si, (c0, n, pis) in enumerate(slabs):
            sl = si % 2
            kb.op('sp', lambda e, sl=sl, c0=c0, n=n: e.dma_start(out=wf[sl][:, :, 0:n], in_=Wr[:, :, c0:c0 + n]),
                  r=[B_w], w=[B_wf[sl]], dma=B_wf[sl])
            kb.op('pool', lambda e, sl=sl, n=n: e.tensor_copy(out=wb[sl][:, :, 0:n], in_=wf[sl][:, :, 0:n]),
                  r=[B_wf[sl]], w=[B_wb[sl]])
            for pi in pis:
                pc0, pn = pieces[pi]
                lo = pc0 - c0
                half = cnt % 2
                cnt += 1
                for t in range(nt):
                    tw = min(512, ntok - t * 512)
                    bk = half * 4 + t
                    for kc in range(KC):
                        kb.op('pe', lambda e, sl=sl, kc=kc, lo=lo, pn=pn, bk=bk, t=t, tw=tw: e.matmul(
                            psv(bk)[0:pn, 0:tw], wb[sl][:, kc, lo:lo + pn], xT[:, kc, t * 512:t * 512 + tw],
                            start=(kc == 0), stop=(kc == KC - 1)), r=[B_wb[sl], B_x], w=[PSB[bk]])
                epi(pi, half)
        ar.pop()

    def ps_half(half, n=P, ntok=S):
        return ps_t[0:n, half * 2048:half * 2048 + ntok]

    def PSH(half):
        return PSB[half * 4:half * 4 + 4]

    to_featmajor(x_in, S, hT, B_hT)
    memT = nc.dram_tensor("memT", [D, MEM], F32).ap()
    B_memT = Buf('memT')
    to_featmajor(mem_in, MEM, memT, B_memT)
    norm_T(memT, B_memT, 16, lambda c: scol('mem_norm', c), memnT, B_memnT, ncols=MEM)

    actbuf_off = ar.alloc(22 * S // 2)
    actbuf = arena_t[:, actbuf_off:actbuf_off + 22 * S // 2].bitcast(BF16)
    B_act = Buf('actbuf')

    def act_view(KC, ntok=S):
        return actbuf[:, 0:KC * ntok].rearrange("p (c t) -> p c t", c=KC)

    def add_residual_epi(ar_bufs):
        hb, B_hb = ar_bufs
        hTr = hT.rearrange("(c p) t -> c p t", p=128)

        def epi(pi, half):
            sl = pi % 2
            kb.op('sp', lambda e, sl=sl, pi=pi: e.dma_start(out=hb[sl], in_=hTr[pi]), r=[B_hT], w=[B_hb[sl]], dma=B_hb[sl])
            kb.op('dve', lambda e, sl=sl, half=half: e.tensor_tensor(out=hb[sl], in0=ps_half(half), in1=hb[sl], op=ALU.add),
                  r=PSH(half) + [B_hb[sl]], w=[B_hb[sl]])
            kb.op('pool', lambda e, sl=sl, pi=pi: e.dma_start(out=hTr[pi], in_=hb[sl]), r=[B_hb[sl]], w=[B_hT], dma=B_hb[sl], accum=True)
        return epi


    def linear_tok(xT, B_x, KC, Wsel, B_w, ncols, ntok, epi, CW=256):
        kb.barrier()
        ar.push()
        wf = [ar.f32(KC * CW).rearrange("p (c n) -> p c n", c=KC) for _ in range(2)]
        B_wf = [Buf('wf0'), Buf('wf1')]
        wb = [ar.bf16(KC * CW).rearrange("p (c n) -> p c n", c=KC) for _ in range(2)]
        B_wb = [Buf('wb0'), Buf('wb1')]
        cnt = 0
        for cb in range((ncols + CW - 1) // CW):
            n = min(CW, ncols - cb * CW)
            sl = cb % 2
            parts = Wsel(cb * CW, n)
            if not isinstance(parts, list):
                parts = [(0, n, parts)]
            for ip, (poff, pw, pap) in enumerate(parts):
                kb.op('sp', lambda e, sl=sl, poff=poff, pw=pw, pap=pap: e.dma_start(out=wf[sl][:, :, poff:poff + pw], in_=pap),
                      r=[B_w], w=[B_wf[sl]], dma=B_wf[sl], accum=(ip > 0))
            kb.op('pool', lambda e, sl=sl, n=n: e.tensor_copy(out=wb[sl][:, :, 0:n], in_=wf[sl][:, :, 0:n]),
                  r=[B_wf[sl]], w=[B_wb[sl]])
            for tb in range(ntok // 128):
                bk = cnt % 8
                cnt += 1
                for kc in range(KC):
                    kb.op('pe', lambda e, sl=sl, kc=kc, n=n, bk=bk, tb=tb: e.matmul(
                        psv(bk)[:, 0:n], xT[:, kc, tb * 128:(tb + 1) * 128], wb[sl][:, kc, 0:n],
                        start=(kc == 0), stop=(kc == KC - 1)), r=[B_wb[sl], B_x], w=[PSB[bk]])
                epi(tb, cb, n, bk)
        ar.pop()

    def copy_epi(dst_of, B_dst, ntok=S):
        def epi(pi, half, pieces=None):
            d = dst_of(pi)
            n = d.shape[0]
            if pi % 2 == 0:
                kb.op('act', lambda e: e.copy(d, ps_half(half, n, ntok)), r=PSH(half), w=[B_dst], accum=True)
            else:
                kb.op('dve', lambda e: e.tensor_copy(out=d, in_=ps_half(half, n, ntok)), r=PSH(half), w=[B_dst], accum=True)
        return epi

    def psbf(bk):
        return psv(bk).bitcast(BF16)


    hpt = ar.f32(8)
    B_hp = Buf('hp')
    kb.op('sp', lambda e: e.dma_start(out=hpt[0:8, 0:4], in_=hp_in), w=[B_hp], dma=B_hp)

    def gdn_mixer(ev, aT, B_aT, bT, B_bT, mixT):
        kb.barrier()
        ar.push()
        a8, b8 = aT[0:8, :], bT[0:8, :]
        sel = lambda h: consts[0:8, C_SEL + h * 128:C_SEL + (h + 1) * 128]
        low, slow, up, sup = (consts[:, c:c + 128] for c in (C_LOW, C_SLOW, C_UP, C_SUP))
        sm = ar.f32(8)
        B_sm = Buf('gsm')
        names = ['qn', 'kn', 'kbt', 'vb', 'egc', 'gcb', 'oT']
        Bf = {n: ar.f32(S) for n in names}
        BB = {n: Buf('g_' + n) for n in names}
        rmask = Bf['oT']
        B_rm = BB['oT']
        kb.op('pool', lambda e: e.memset(rmask[0:8, :], 1.0), w=[B_rm])
        kb.op('pool', lambda e: e.memset(rmask[0:8, :].rearrange("p (c t) -> p c t", t=128)[:, :, 0:1], 0.0), r=[B_rm], w=[B_rm])
        kb.op('act', lambda e: e.activation(out=sm[0:8, 0:1], in_=hpt[0:8, 2 * ev:2 * ev + 1], func=AF.Exp), r=[B_hp], w=[B_sm])
        kb.op('dve', lambda e: e.tensor_scalar(out=sm[0:8, 0:1], in0=sm[0:8, 0:1], scalar1=-1.0, scalar2=None, op0=ALU.mult), r=[B_sm], w=[B_sm])
        kb.op('act', lambda e: e.activation(out=a8, in_=a8, func=AF.Exp, bias=hpt[0:8, 2 * ev + 1:2 * ev + 2]), r=[B_aT, B_hp], w=[B_aT])
        kb.op('act', lambda e: e.activation(out=a8, in_=a8, func=AF.Ln, bias=1.0), r=[B_aT], w=[B_aT])
        kb.op('dve', lambda e: e.tensor_scalar(out=a8, in0=a8, scalar1=sm[0:8, 0:1], scalar2=None, op0=ALU.mult), r=[B_aT, B_sm], w=[B_aT])
        kb.op('dve', lambda e: e.tensor_tensor_scan(out=a8, data0=rmask[0:8, :], data1=a8, initial=0.0, op0=ALU.mult, op1=ALU.add),
              r=[B_aT, B_rm], w=[B_aT])
        kb.op('act', lambda e: e.activation(out=b8, in_=b8, func=AF.Sigmoid), r=[B_bT], w=[B_bT])
        gccol = ar.f32(128)
        B_gccol = Buf('gccol')
        for c in range(16):
            kb.op('pe', lambda e, c=c: e.transpose(psv(0)[:, c * 8:(c + 1) * 8], a8[:, c * 128:(c + 1) * 128], ident[0:8, 0:8]),
                  r=[B_aT, B_const], w=[PSB[0]])
        kb.op('dve', lambda e: e.tensor_copy(out=gccol, in_=psv(0)[:, 0:128]), r=[PSB[0]], w=[B_gccol])
        if GL < 1:
            ar.pop()
            return
        gate = ar.bf16(S)
        B_gate = Buf('g_gate')
        small = {}
        for n in ['Pm0', 'Pm1', 'PT0', 'PT1', 'TT', 'dec', 'decT', 'attnT', 'negwT', 'vnew', 'Sst', 't1', 'kbgc', 'qgc', 'kdc']:
            small[n] = (ar.f32(128), Buf('g_' + n))
        tok3, B_tok3 = ar.f32(384), Buf('g_tok3')
        egl, B_egl = ar.f32(16), Buf('g_egl')
        gqkvr = gqkv.rearrange("(c p) t -> c p t", p=128)
        gateTr = gateT.rearrange("(c p) t -> c p t", p=128)

        def PSALL():
            return PSB[0:4]
        for h in range(GH):
            kb.barrier()
            qn, kn, vb = Bf['qn'], Bf['kn'], Bf['vb']
            kb.op('sp', lambda e, h=h: e.dma_start(out=qn, in_=gqkvr[h]), r=[B_gqkv], w=[BB['qn']], dma=BB['qn'])
            kb.op('sp', lambda e, h=h: e.dma_start(out=kn, in_=gqkvr[8 + h]), r=[B_gqkv], w=[BB['kn']], dma=BB['kn'])
            kb.op('sp', lambda e, h=h: e.dma_start(out=vb, in_=gqkvr[16 + h]), r=[B_gqkv], w=[BB['vb']], dma=BB['vb'])
            kb.op('sp', lambda e, h=h: e.dma_start(out=gate, in_=gateTr[h]), r=[B_gateT], w=[B_gate], dma=B_gate)
            for nm, scl in (('qn', 128 ** -0.5), ('kn', 1.0)):
                x_ = Bf[nm]
                tmp, B_tmp = Bf['egc'], BB['egc']
                kb.op('act', lambda e, x_=x_, tmp=tmp: e.activation(out=tmp, in_=x_, func=AF.Square), r=[BB[nm]], w=[B_tmp])
                for t in range(4):
                    kb.op('pe', lambda e, t=t, tmp=tmp: e.matmul(psv(t), ones, tmp[:, t * 512:(t + 1) * 512], start=True, stop=True),
                          r=[B_tmp, B_const], w=[PSB[t]])
                kb.op('act', lambda e, tmp=tmp: e.activation(out=tmp, in_=ps_t[:, 0:S], func=AF.Sqrt, bias=epsc), r=PSALL() + [B_eps], w=[B_tmp])
                kb.op('dve', lambda e, tmp=tmp: e.reciprocal(out=tmp, in_=tmp), r=[B_tmp], w=[B_tmp])
                kb.op('dve', lambda e, x_=x_, tmp=tmp, scl=scl: e.scalar_tensor_tensor(out=x_, in0=x_, scalar=scl, in1=tmp, op0=ALU.mult, op1=ALU.mult),
                      r=[BB[nm], B_tmp], w=[BB[nm]])
            for t in range(4):
                kb.op('pe', lambda e, t=t, h=h: e.matmul(psv(t), sel(h), a8[:, t * 512:(t + 1) * 512], start=True, stop=True), r=[B_aT, B_const], w=[PSB[t]])
                kb.op('pe', lambda e, t=t, h=h: e.matmul(psv(4 + t), sel(h), b8[:, t * 512:(t + 1) * 512], start=True, stop=True), r=[B_bT, B_const], w=[PSB[4 + t]])
            gcb, egc = Bf['gcb'], Bf['egc']
            kb.op('act', lambda e: e.copy(gcb, ps_t[:, 0:S]), r=PSALL(), w=[BB['gcb']])
            kb.op('act', lambda e: e.activation(out=egc, in_=ps_t[:, 0:S], func=AF.Exp), r=PSALL(), w=[BB['egc']])
            betab = ps_t[:, S:2 * S]
            kb.op('dve', lambda e: e.tensor_tensor(out=Bf['kbt'], in0=betab, in1=kn, op=ALU.mult), r=PSB[4:8] + [BB['kn']], w=[BB['kbt']])
            kb.op('dve', lambda e: e.tensor_tensor(out=vb, in0=betab, in1=vb, op=ALU.mult), r=PSB[4:8] + [BB['vb']], w=[BB['vb']])
            gl = gcb.rearrange("p (c t) -> p c t", t=128)[:, :, 127]
            kb.op('act', lambda e: e.activation(out=egl, in_=gl, func=AF.Exp), r=[BB['gcb']], w=[B_egl])
            Sst, B_S = small['Sst']
            kb.op('pool', lambda e: e.memset(Sst, 0.0), w=[B_S])
            for c in range(16 if GL >= 2 else 0):
                cs = slice(c * 128, (c + 1) * 128)
                gcc = gccol[:, c * 8 + h:c * 8 + h + 1]
                kbgc, B_kbgc = small['kbgc']
                kb.op('dve', lambda e, cs=cs: e.tensor_tensor(out=kbgc, in0=Bf['kbt'][:, cs], in1=Bf['egc'][:, cs], op=ALU.mult), r=[BB['kbt'], BB['egc']], w=[B_kbgc])
                qgc, B_qgc = small['qgc']
                kdc, B_kdc = small['kdc']
                kb.op('dve', lambda e, cs=cs: e.tensor_tensor(out=qgc, in0=qn[:, cs], in1=Bf['egc'][:, cs], op=ALU.mult), r=[BB['qn'], BB['egc']], w=[B_qgc])
                kb.op('act', lambda e, c=c, cs=cs: e.activation(out=kdc, in_=gcb[:, cs], func=AF.Exp, scale=-1.0, bias=gcb[:, c * 128 + 127:c * 128 + 128]),
                      r=[BB['gcb']], w=[B_kdc])
                kb.op('dve', lambda e, cs=cs: e.tensor_tensor(out=kdc, in0=kdc, in1=kn[:, cs], op=ALU.mult), r=[B_kdc, BB['kn']], w=[B_kdc])
                for i_, (src_, B_src_) in enumerate(((Bf['vb'][:, cs], BB['vb']), (kbgc, B_kbgc), (kdc, B_kdc))):
                    kb.op('pe', lambda e, i_=i_, src_=src_: e.transpose(psv(4)[:, i_ * 128:(i_ + 1) * 128], src_, ident), r=[B_src_, B_const], w=[PSB[4]])
                kb.op('act', lambda e: e.copy(tok3, psv(4)[:, 0:384]), r=[PSB[4]], w=[B_tok3])
                vb_tok, kbg_tok, kdec_tok = tok3[:, 0:128], tok3[:, 128:256], tok3[:, 256:384]
                dec, B_dec = small['dec']
                decT, B_decT = small['decT']
                kb.op('dve', lambda e, cs=cs, gcc=gcc: e.tensor_scalar(out=dec, in0=gcb[:, cs], scalar1=gcc, scalar2=0.0, op0=ALU.subtract, op1=ALU.max),
                      r=[BB['gcb'], B_gccol], w=[B_dec])
                kb.op('dve', lambda e, cs=cs, gcc=gcc: e.tensor_scalar(out=decT, in0=gcb[:, cs], scalar1=gcc, scalar2=0.0, op0=ALU.subtract, op1=ALU.min),
                      r=[BB['gcb'], B_gccol], w=[B_decT])
                kb.op('act', lambda e: e.activation(out=dec, in_=dec, func=AF.Exp, scale=-1.0), r=[B_dec], w=[B_dec])
                kb.op('act', lambda e: e.activation(out=decT, in_=decT, func=AF.Exp), r=[B_decT], w=[B_decT])
                kb.op('pe', lambda e, cs=cs: e.matmul(psv(5)[:, 0:128], Bf['kbt'][:, cs], kn[:, cs], start=True, stop=True), r=[BB['kbt'], BB['kn']], w=[PSB[5]])
                kb.op('pe', lambda e, cs=cs: e.matmul(psv(5)[:, 128:256], kn[:, cs], Bf['kbt'][:, cs], start=True, stop=True), r=[BB['kbt'], BB['kn']], w=[PSB[5]])
                kb.op('pe', lambda e, cs=cs: e.matmul(psv(5)[:, 256:384], kn[:, cs], qn[:, cs], start=True, stop=True), r=[BB['qn'], BB['kn']], w=[PSB[5]])
                t1, B_t1 = small['t1']
                Pm, B_Pm = small['Pm0']
                PT, B_PT = small['PT0']
                TT, B_TT = small['TT']
                attnT, B_attnT = small['attnT']
                kb.op('dve', lambda e: e.tensor_tensor(out=t1, in0=psv(5)[:, 0:128], in1=dec, op=ALU.mult), r=[PSB[5], B_dec], w=[B_t1])
                kb.op('dve', lambda e: e.scalar_tensor_tensor(out=Pm, in0=t1, scalar=-1.0, in1=slow, op0=ALU.mult, op1=ALU.mult), r=[B_t1, B_const], w=[B_Pm])
                kb.op('dve', lambda e: e.tensor_tensor(out=t1, in0=psv(5)[:, 128:256], in1=decT, op=ALU.mult), r=[PSB[5], B_decT], w=[B_t1])
                kb.op('dve', lambda e: e.scalar_tensor_tensor(out=PT, in0=t1, scalar=-1.0, in1=sup, op0=ALU.mult, op1=ALU.mult), r=[B_t1, B_const], w=[B_PT])
                kb.op('dve', lambda e: e.tensor_tensor(out=t1, in0=psv(5)[:, 256:384], in1=decT, op=ALU.mult), r=[PSB[5], B_decT], w=[B_t1])
                kb.op('dve', lambda e: e.tensor_tensor(out=attnT, in0=t1, in1=up, op=ALU.mult), r=[B_t1, B_const], w=[B_attnT])
                kb.op('dve', lambda e: e.tensor_copy(out=TT, in_=ident), r=[B_const], w=[B_TT])
                if GL < 3:
                    continue
                cur = 0
                for m in range(7):
                    Pc, B_Pc = small['Pm%d' % cur]
                    PTc, B_PTc = small['PT%d' % cur]
                    Pn, B_Pn = small['Pm%d' % (1 - cur)]
                    PTn, B_PTn = small['PT%d' % (1 - cur)]
                    if m < 6:
                        kb.op('pe', lambda e, Pc=Pc, PTc=PTc: e.matmul(psv(6)[:, 0:128], PTc, Pc, start=True, stop=True), r=[B_Pc, B_PTc], w=[PSB[6]])
                    if m < 5:
                        kb.op('pe', lambda e, Pc=Pc, PTc=PTc: e.matmul(psv(6)[:, 128:256], Pc, PTc, start=True, stop=True), r=[B_Pc, B_PTc], w=[PSB[6]])
                    kb.op('pe', lambda e, Pc=Pc: e.matmul(psv(7)[:, 0:128], Pc, TT, start=True, stop=True), r=[B_Pc, B_TT], w=[PSB[7]])
                    if m < 6:
                        kb.op('act', lambda e, Pn=Pn: e.copy(Pn, psv(6)[:, 0:128]), r=[PSB[6]], w=[B_Pn])
                    if m < 5:
                        kb.op('act', lambda e, PTn=PTn: e.copy(PTn, psv(6)[:, 128:256]), r=[PSB[6]], w=[B_PTn])
                    kb.op('dve', lambda e: e.tensor_tensor(out=TT, in0=psv(7)[:, 0:128], in1=TT, op=ALU.add), r=[PSB[7], B_TT], w=[B_TT])
                    cur = 1 - cur
                if GL < 4:
                    continue
                negwT, B_nw = small['negwT']
                vnew, B_vn = small['vnew']
                kb.op('pe', lambda e: e.matmul(psv(6)[:, 256:384], kbg_tok, TT, start=True, stop=True), r=[B_tok3, B_TT], w=[PSB[6]])
                kb.op('act', lambda e: e.mul(negwT, psv(6)[:, 256:384], -1.0), r=[PSB[6]], w=[B_nw])
                kb.op('pe', lambda e: e.matmul(psv(7)[:, 128:256], TT, vb_tok, start=True, stop=False), r=[B_tok3, B_TT], w=[PSB[7]])
                kb.op('pe', lambda e: e.matmul(psv(7)[:, 128:256], negwT, Sst, start=False, stop=True), r=[B_nw, B_S], w=[PSB[7]])
                kb.op('act', lambda e: e.copy(vnew, psv(7)[:, 128:256]), r=[PSB[7]], w=[B_vn])
                kb.op('pe', lambda e, cs=cs: e.matmul(psv(7)[:, 256:384], Sst, qgc, start=True, stop=False), r=[B_S, B_qgc], w=[PSB[7]])
                kb.op('pe', lambda e: e.matmul(psv(7)[:, 256:384], vnew, attnT, start=False, stop=True), r=[B_vn, B_attnT], w=[PSB[7]])
                kb.op('act', lambda e, cs=cs: e.copy(Bf['oT'][:, cs], psv(7)[:, 256:384]), r=[PSB[7]], w=[BB['oT']], accum=True)
                kb.op('pe', lambda e: e.matmul(psv(6)[:, 384:512], kdec_tok, vnew, start=True, stop=True), r=[B_tok3, B_vn], w=[PSB[6]])
                kb.op('dve', lambda e, c=c: e.scalar_tensor_tensor(out=Sst, in0=Sst, scalar=egl[:, c:c + 1], in1=psv(6)[:, 384:512], op0=ALU.mult, op1=ALU.add),
                      r=[B_S, B_egl, PSB[6]], w=[B_S])
            oT = Bf['oT']
            tmp, B_tmp = Bf['egc'], BB['egc']
            kb.op('act', lambda e: e.activation(out=tmp, in_=oT, func=AF.Square), r=[BB['oT']], w=[B_tmp])
            for t in range(4):
                kb.op('pe', lambda e, t=t: e.matmul(psv(t), ones, tmp[:, t * 512:(t + 1) * 512], start=True, stop=True), r=[B_tmp, B_const], w=[PSB[t]])
            kb.op('act', lambda e: e.activation(out=tmp, in_=ps_t[:, 0:S], func=AF.Sqrt, scale=1.0 / 128, bias=epsc), r=PSALL() + [B_eps], w=[B_tmp])
            kb.op('dve', lambda e: e.reciprocal(out=tmp, in_=tmp), r=[B_tmp], w=[B_tmp])
            kb.op('dve', lambda e: e.scalar_tensor_tensor(out=oT, in0=oT, scalar=scol('ev_gdn_norm', ev), in1=tmp, op0=ALU.mult, op1=ALU.mult),
                  r=[BB['oT'], B_tmp, B_small], w=[BB['oT']])
            kb.op('dve', lambda e, h=h: e.tensor_tensor(out=mixT[:, 8 + h, :], in0=oT, in1=gate, op=ALU.mult), r=[BB['oT'], B_gate], w=[B_act], accum=True)
        ar.pop()


    def gla_mixer(od, lgT, B_lgT, mixT):
        kb.barrier()
        ar.push()
        up64 = consts[0:64, C_UP:C_UP + 64]
        w2t = ar.f32(512)
        B_w2 = Buf('w2t')
        kb.op('sp', lambda e: e.dma_start(out=w2t[0:16, :], in_=w2_in[:, od, :]), w=[B_w2], dma=B_w2)
        names = ['qt', 'kt', 'cum', 'tmp', 'oT0', 'oT1']
        Bf = {n: ar.f32(S) for n in names}
        BB = {n: Buf('l_' + n) for n in names}
        vtb = ar.f32(16 * 256)
        vt = vtb[0:64, :].rearrange("p (c n) -> p c n", c=16)
        B_vt = Buf('l_vt')
        gate = ar.bf16(S)
        B_gate = Buf('l_gate')
        negb2 = ar.f32(8)
        B_nb = Buf('l_nb')
        alast = ar.f32(32)
        B_al = Buf('l_al')
        attnT, B_attnT = ar.f32(64), Buf('l_attnT')
        kdc, B_kdc = ar.f32(64), Buf('l_kdc')
        kdtok, B_kdtok = ar.f32(128), Buf('l_kdtok')
        Sst, B_S = ar.f32(256), Buf('l_S')
        rmask = Bf['oT0']
        B_rm = BB['oT0']
        rmask2 = ar.f32(S)
        B_rm2 = Buf('l_rm')
        kb.op('pool', lambda e: e.memset(rmask2, 1.0), w=[B_rm2])
        kb.op('pool', lambda e: e.memset(rmask2.rearrange("p (c t) -> p c t", t=64)[:, :, 0:1], 0.0), r=[B_rm2], w=[B_rm2])
        for h in range(4):
            kb.op('dve', lambda e, h=h: e.tensor_scalar(out=negb2[:, h:h + 1], in0=scol('od_gla_b2', od * 4 + h), scalar1=-1.0, scalar2=None, op0=ALU.mult),
                  r=[B_small], w=[B_nb], accum=(h > 0))
        glqkr = glqk.rearrange("(c p) t -> c p t", p=128)
        gateTr = gateT.rearrange("(c p) t -> c p t", p=128)
        qt, kt, cum, tmp = Bf['qt'], Bf['kt'], Bf['cum'], Bf['tmp']
        oTs = [Bf['oT0'], Bf['oT1']]
        B_oTs = [BB['oT0'], BB['oT1']]
        for h in range(4):
            kb.barrier()
            kb.op('sp', lambda e, h=h: e.dma_start(out=qt, in_=glqkr[h]), r=[B_glqk], w=[BB['qt']], dma=BB['qt'])
            kb.op('sp', lambda e, h=h: e.dma_start(out=kt, in_=glqkr[4 + h]), r=[B_glqk], w=[BB['kt']], dma=BB['kt'])
            for t in range(4):
                kb.op('pe', lambda e, t=t, h=h: e.matmul(psv(t), w2t[0:16, h * 128:(h + 1) * 128], lgT[0:16, t * 512:(t + 1) * 512], start=True, stop=True),
                      r=[B_w2, B_lgT], w=[PSB[t]])
            kb.op('act', lambda e, h=h: e.activation(out=tmp, in_=ps_t[:, 0:S], func=AF.Exp, scale=-1.0, bias=negb2[:, h:h + 1]), r=PSB[0:4] + [B_nb], w=[BB['tmp']])
            kb.op('act', lambda e: e.activation(out=tmp, in_=tmp, func=AF.Ln, bias=1.0), r=[BB['tmp']], w=[BB['tmp']])
            kb.op('dve', lambda e: e.tensor_scalar(out=tmp, in0=tmp, scalar1=-1.0 / 16.0, scalar2=None, op0=ALU.mult), r=[BB['tmp']], w=[BB['tmp']])
            kb.op('dve', lambda e: e.tensor_tensor_scan(out=cum, data0=rmask2, data1=tmp, initial=0.0, op0=ALU.mult, op1=ALU.add), r=[BB['tmp'], B_rm2], w=[BB['cum']])
            kb.op('act', lambda e: e.activation(out=tmp, in_=cum, func=AF.Exp), r=[BB['cum']], w=[BB['tmp']])
            kb.op('dve', lambda e: e.tensor_copy(out=alast, in_=tmp.rearrange("p (c t) -> p c t", t=64)[:, :, 63]), r=[BB['tmp']], w=[B_al])
            kb.op('dve', lambda e: e.scalar_tensor_tensor(out=qt, in0=qt, scalar=128 ** -0.5, in1=tmp, op0=ALU.mult, op1=ALU.mult), r=[BB['qt'], BB['tmp']], w=[BB['qt']])
            kb.op('act', lambda e: e.activation(out=tmp, in_=cum, func=AF.Exp, scale=-1.0), r=[BB['cum'], BB['qt']], w=[BB['tmp']])
            kb.op('dve', lambda e: e.tensor_tensor(out=kt, in0=kt, in1=tmp, op=ALU.mult), r=[BB['kt'], BB['tmp']], w=[BB['kt']])
            kb.op('pool', lambda e: e.memset(Sst, 0.0), w=[B_S])
            for c in range(32):
                cs = slice(c * 64, (c + 1) * 64)
                ci = c % 16
                if ci == 0:
                    kb.op('sp', lambda e, c=c, h=h: e.dma_start(out=vt, in_=glv[c * 64:(c + 16) * 64, h * 256:(h + 1) * 256].rearrange("(c p) n -> p c n", p=64)),
                          r=[B_glv], w=[B_vt], dma=B_vt)
                kb.op('pe', lambda e, cs=cs: e.matmul(psv(4)[0:64, 0:64], kt[:, cs], qt[:, cs], start=True, stop=True), r=[BB['kt'], BB['qt']], w=[PSB[4]])
                kb.op('dve', lambda e: e.tensor_tensor(out=attnT[0:64, :], in0=psv(4)[0:64, 0:64], in1=up64, op=ALU.mult), r=[PSB[4], B_const], w=[B_attnT])
                kb.op('dve', lambda e, cs=cs, c=c: e.tensor_scalar(out=kdc, in0=kt[:, cs], scalar1=alast[:, c:c + 1], scalar2=None, op0=ALU.mult),
                      r=[BB['kt'], B_al], w=[B_kdc])
                kb.op('pe', lambda e: e.transpose(psv(3)[0:64, 0:128], kdc, ident), r=[B_kdc, B_const], w=[PSB[3]])
                kb.op('act', lambda e: e.copy(kdtok[0:64, :], psv(3)[0:64, 0:128]), r=[PSB[3]], w=[B_kdtok])
                for half in range(2):
                    hs = slice(half * 128, (half + 1) * 128)
                    kb.op('pe', lambda e, half=half, hs=hs, cs=cs: e.matmul(psv(5 + half)[:, 0:64], Sst[:, hs], qt[:, cs], start=True, stop=False),
                          r=[B_S, BB['qt']], w=[PSB[5 + half]])
                    kb.op('pe', lambda e, half=half, hs=hs, ci=ci: e.matmul(psv(5 + half)[:, 0:64], vt[:, ci, hs], attnT[0:64, :], start=False, stop=True),
                          r=[B_vt, B_attnT], w=[PSB[5 + half]])
                    kb.op('act', lambda e, half=half, cs=cs: e.copy(oTs[half][:, cs], psv(5 + half)[:, 0:64]), r=[PSB[5 + half]], w=[B_oTs[half]], accum=True)
                kb.op('pe', lambda e, ci=ci: e.matmul(psv(7)[:, 0:256], kdtok[0:64, :], vt[:, ci, :], start=True, stop=True), r=[B_kdtok, B_vt], w=[PSB[7]])
                kb.op('dve', lambda e, c=c: e.scalar_tensor_tensor(out=Sst, in0=Sst, scalar=alast[:, c:c + 1], in1=psv(7)[:, 0:256], op0=ALU.mult, op1=ALU.add),
                      r=[B_S, B_al, PSB[7]], w=[B_S])
            kb.op('act', lambda e: e.activation(out=tmp, in_=oTs[0], func=AF.Square), r=[B_oTs[0]], w=[BB['tmp']])
            kb.op('act', lambda e: e.activation(out=cum, in_=oTs[1], func=AF.Square), r=[B_oTs[1]], w=[BB['cum']])
            for t in range(4):
                kb.op('pe', lambda e, t=t: e.matmul(psv(t), ones, tmp[:, t * 512:(t + 1) * 512], start=True, stop=False), r=[BB['tmp'], B_const], w=[PSB[t]])
                kb.op('pe', lambda e, t=t: e.matmul(psv(t), ones, cum[:, t * 512:(t + 1) * 512], start=False, stop=True), r=[BB['cum'], B_const], w=[PSB[t]])
            kb.op('act', lambda e: e.activation(out=tmp, in_=ps_t[:, 0:S], func=AF.Sqrt, scale=1.0 / 256, bias=epsc), r=PSB[0:4] + [B_eps], w=[BB['tmp']])
            kb.op('dve', lambda e: e.reciprocal(out=tmp, in_=tmp), r=[BB['tmp']], w=[BB['tmp']])
            for half in range(2):
                kb.op('sp', lambda e, h=h, half=half: e.dma_start(out=gate, in_=gateTr[h * 2 + half]), r=[B_gateT], w=[B_gate], dma=B_gate)
                kb.op('dve', lambda e, half=half: e.scalar_tensor_tensor(out=oTs[half], in0=oTs[half], scalar=scol('od_gla_norm', od * 2 + half), in1=tmp,
                                                                         op0=ALU.mult, op1=ALU.mult), r=[B_oTs[half], BB['tmp'], B_small], w=[B_oTs[half]])
                kb.op('dve', lambda e, h=h, half=half: e.tensor_tensor(out=mixT[:, 8 + h * 2 + half, :], in0=oTs[half], in1=gate, op=ALU.mult),
                      r=[B_oTs[half], B_gate], w=[B_act], accum=True)
        ar.pop()

    for layer in layers:
        if do_mixer and layer % 2 == 0:
            ev = layer // 2
            hn = act_view(16)
            norm_T(hT, B_hT, 16, lambda c, layer=layer: scol('norm_mix', layer * 16 + c), hn, B_act)
            kb.barrier()
            ar.push()
            Win = W[('ev_w_in', ev)]
            B_Win = wbufs[('ev_w_in', ev)]
            aT = ar.f32(S)
            B_aT = Buf('aT')
            bT = ar.f32(S)
            B_bT = Buf('bT')
            ar.push()
            stb = [ar.bf16(S), ar.bf16(S)]
            B_stb = [Buf('stb0'), Buf('stb1')]
            sbqkr = sbqk.rearrange("(c p) t -> c p t", p=128)

            def qk_epi(pi, half):
                sl = pi % 2
                kb.op('act', lambda e: e.copy(stb[sl], ps_half(half)), r=PSH(half), w=[B_stb[sl]])
                kb.op('pool', lambda e: e.dma_start(out=sbqkr[pi], in_=stb[sl]), r=[B_stb[sl]], w=[B_sbqk], dma=B_stb[sl], accum=True)
            linear_T(hn, B_act, 16, Win, B_Win, [(c * 128, 128) for c in range(16)], qk_epi)
            Wr_ = Win.rearrange("(c p) n -> p c n", p=128)
            stv = [ar.bf16(256), ar.bf16(256)]
            B_stv = [Buf('stv0'), Buf('stv1')]
            vcnt = [0]

            def v_epi(tb, cb, n, bk):
                sl = vcnt[0] % 2
                vcnt[0] += 1
                kb.op('act', lambda e: e.copy(stv[sl][:, 0:n], psv(bk)[:, 0:n]), r=[PSB[bk]], w=[B_stv[sl]])
                kb.op('pool', lambda e: e.dma_start(out=sbv[tb * 128:(tb + 1) * 128, cb * 256:cb * 256 + n], in_=stv[sl][:, 0:n]),
                      r=[B_stv[sl]], w=[B_sbv], dma=B_stv[sl], accum=True)
            linear_tok(hn, B_act, 16, lambda c0, n: Wr_[:, :, 2048 + c0:2048 + c0 + n], B_Win, 1024, S, v_epi)
            kb.barrier()
            ar.pop()
            ar.push()
            xb4 = [ar.f32(S + 8) for _ in range(2)]
            B_xb4 = [Buf('xb40'), Buf('xb41')]
            cv = [ar.f32(S), ar.f32(S)]
            B_cv = [Buf('cv0'), Buf('cv1')]
            for i in range(2):
                kb.op('pool', lambda e, i=i: e.memset(xb4[i][:, 0:8], 0.0), w=[B_xb4[i]])
            gqkvr = gqkv.rearrange("(c p) t -> c p t", p=128)

            def gq_epi(pi, half, ev=ev):
                sl = pi % 2

                def wc(tap):
                    return scol('ev_sconv', (ev * 4 + tap) * 24 + pi)
                kb.op('act', lambda e: e.copy(xb4[sl][:, 8:8 + S], ps_half(half)), r=PSH(half), w=[B_xb4[sl]], accum=True)
                kb.op('act', lambda e: e.activation(out=cv[sl], in_=xb4[sl][:, 8:8 + S], func=AF.Identity, scale=wc(3)),
                      r=[B_xb4[sl], B_small], w=[B_cv[sl]])
                for tap in range(3):
                    kb.op('dve', lambda e, tap=tap: e.scalar_tensor_tensor(out=cv[sl], in0=xb4[sl][:, 5 + tap:5 + tap + S], scalar=wc(tap), in1=cv[sl],
                                                                          op0=ALU.mult, op1=ALU.add), r=[B_xb4[sl], B_cv[sl], B_small], w=[B_cv[sl]])
                kb.op('act', lambda e: e.activation(out=cv[sl], in_=cv[sl], func=AF.Silu), r=[B_cv[sl]], w=[B_cv[sl]])
                kb.op('pool', lambda e: e.dma_start(out=gqkvr[pi], in_=cv[sl]), r=[B_cv[sl]], w=[B_gqkv], dma=B_cv[sl], accum=True)
            linear_T(hn, B_act, 16, Win, B_Win, [(3072 + c * 128, 128) for c in range(24)], gq_epi, SW=(128 if dbg else 256))
            kb.barrier()
            ar.pop()
            ar.push()
            stb = [ar.bf16(S), ar.bf16(S)]
            B_stb = [Buf('stb0'), Buf('stb1')]

            def ab_epi(pi, half):
                d, B_d = (aT, B_aT) if pi == 0 else (bT, B_bT)
                kb.op('act', lambda e: e.copy(d[0:8, :], ps_half(half, 8)), r=PSH(half), w=[B_d])
            linear_T(hn, B_act, 16, Win, B_Win, [(6144, 8), (6152, 8)], ab_epi)
            gateTr = gateT.rearrange("(c p) t -> c p t", p=128)

            def gate_epi(pi, half):
                sl = pi % 2
                kb.op('act', lambda e: e.activation(out=stb[sl], in_=ps_half(half), func=AF.Silu), r=PSH(half), w=[B_stb[sl]])
                kb.op('pool', lambda e: e.dma_start(out=gateTr[pi], in_=stb[sl]), r=[B_stb[sl]], w=[B_gateT], dma=B_stb[sl], accum=True)
            linear_T(hn, B_act, 16, Win, B_Win, [(6160 + c * 128, 128) for c in range(8)], gate_epi)
            ar.pop()
            kb.barrier()
            mixT = act_view(16)
            ar.push()
            sc = 128 ** -0.5
            qh = ar.bf16(S)
            kh = ar.bf16(S)
            vh = ar.bf16(S).rearrange("p (c d) -> p c d", c=16)
            B_qkv = Buf('sbqkvh')
            onesS = ar.f32(S)
            B_onesS = Buf('onesS')
            kb.op('pool', lambda e: e.memset(onesS, 1.0), w=[B_onesS])
            ebuf = ar.f32(S)
            B_e = Buf('sbe')
            spb_full = ar.f32(S + 8)
            spb = spb_full[:, 8:8 + S]
            B_sp = Buf('sbsp')
            kb.op('pool', lambda e: e.memset(spb_full[:, 0:8], 0.0), w=[B_sp])
            cb_ = ar.f32(S)
            B_c = Buf('sbc')
            t2 = ar.f32(S)
            B_t2 = Buf('sbt2')
            wbuf = ar.bf16(S)
            B_w_ = Buf('sbw')
            wT = ar.bf16(S).rearrange("p (c q) -> p c q", c=16)
            B_wT = Buf('sbwT')
            ngT = ar.f32(8)
            B_ngT = Buf('sbngT')
            sbvr = sbv.rearrange("(c p) n -> p c n", p=128)
            slow = consts[:, C_SLOW:C_SLOW + 128]
            for h in range(8):
                kb.op('sp', lambda e, h=h: e.dma_start(out=qh, in_=sbqkr[h]), r=[B_sbqk], w=[B_qkv], dma=B_qkv)
                kb.op('sp', lambda e, h=h: e.dma_start(out=kh, in_=sbqkr[8 + h]), r=[B_sbqk], w=[B_qkv], dma=B_qkv, accum=True)
                kb.op('sp', lambda e, h=h: e.dma_start(out=vh, in_=sbvr[:, :, h * 128:(h + 1) * 128]), r=[B_sbv], w=[B_qkv], dma=B_qkv, accum=True)
                for qb in range(16):
                    nk = (qb + 1) * 128
                    nt = (nk + 511) // 512
                    for t in range(nt):
                        w_ = min(512, nk - t * 512)
                        kb.op('pe', lambda e, qb=qb, t=t, w_=w_: e.matmul(psv(t)[:, 0:w_], qh[:, qb * 128:(qb + 1) * 128], kh[:, t * 512:t * 512 + w_],
                                                                         start=True, stop=True), r=[B_qkv], w=[PSB[t]])
                    zps = ps_t[:, 0:nk]
                    ZB = PSB[0:nt]
                    dg = slice(nk - 128, nk)
                    kb.op('act', lambda e, zps=zps, nk=nk: e.activation(out=ebuf[:, 0:nk], in_=zps, func=AF.Exp, scale=sc), r=ZB, w=[B_e])
                    kb.op('act', lambda e, nk=nk: e.activation(out=spb[:, 0:nk], in_=ebuf[:, 0:nk], func=AF.Ln, bias=1.0), r=[B_e], w=[B_sp])
                    kb.op('dve', lambda e, dg=dg: e.tensor_tensor(out=spb[:, dg], in0=spb[:, dg], in1=slow, op=ALU.mult), r=[B_sp, B_const], w=[B_sp])
                    kb.op('dve', lambda e, nk=nk: e.tensor_tensor_scan(out=cb_[:, 0:nk], data0=onesS[:, 0:nk], data1=spb_full[:, 7:7 + nk], initial=0.0,
                                                                       op0=ALU.mult, op1=ALU.add), r=[B_sp, B_onesS], w=[B_c])
                    kb.op('dve', lambda e, nk=nk: e.tensor_scalar(out=ngT[:, 0:1], in0=cb_[:, nk - 1:nk], scalar1=-1.0, scalar2=None, op0=ALU.mult),
                          r=[B_c], w=[B_ngT])
                    kb.op('dve', lambda e, zps=zps, nk=nk: e.scalar_tensor_tensor(out=t2[:, 0:nk], in0=zps, scalar=sc, in1=cb_[:, 0:nk],
                                                                               op0=ALU.mult, op1=ALU.add), r=ZB + [B_c], w=[B_t2])
                    kb.op('act', lambda e, nk=nk: e.activation(out=wbuf[:, 0:nk], in_=t2[:, 0:nk], func=AF.Exp, bias=ngT[:, 0:1]),
                          r=[B_t2, B_ngT], w=[B_w_])
                    kb.op('dve', lambda e, dg=dg: e.tensor_tensor(out=wbuf[:, dg], in0=wbuf[:, dg], in1=slow, op=ALU.mult), r=[B_w_, B_const], w=[B_w_])
                    for k_ in range(qb + 1):
                        bk = 4 + k_ // 8
                        kb.op('pe', lambda e, k_=k_, bk=bk: e.transpose(psbf(bk)[:, (k_ % 8) * 128:(k_ % 8 + 1) * 128], wbuf[:, k_ * 128:(k_ + 1) * 128], identb),
                              r=[B_w_, B_identb], w=[PSB[bk]])
                    for g_ in range((qb + 8) // 8):
                        n_ = min(8, qb + 1 - g_ * 8)
                        if g_ == 0:
                            kb.op('act', lambda e, n_=n_: e.copy(wT.rearrange("p c q -> p (c q)")[:, 0:n_ * 128], psbf(4)[:, 0:n_ * 128]),
                                  r=[PSB[4]], w=[B_wT])
                        else:
                            kb.op('dve', lambda e, n_=n_: e.tensor_copy(out=wT.rearrange("p c q -> p (c q)")[:, 1024:1024 + n_ * 128], in_=psbf(5)[:, 0:n_ * 128]),
                                  r=[PSB[5]], w=[B_wT], accum=True)
                    ob = 6 + qb % 2
                    for k_ in range(qb + 1):
                        kb.op('pe', lambda e, k_=k_, ob=ob, qb=qb: e.matmul(psv(ob)[:, 0:128], vh[:, k_, :], wT[:, k_, :], start=(k_ == 0), stop=(k_ == qb)),
                              r=[B_qkv, B_wT], w=[PSB[ob]])
                    kb.op('act', lambda e, ob=ob, h=h, qb=qb: e.copy(mixT[:, h, qb * 128:(qb + 1) * 128], psv(ob)[:, 0:128]), r=[PSB[ob]], w=[B_act], accum=True)
            ar.pop()
            dump('sb0', mixT[:, 0, :], [B_act])
            if GDN_ON:
                gdn_mixer(ev, aT, B_aT, bT, B_bT, mixT)
                dump('gd0', mixT[:, 8, :], [B_act])
            kb.barrier()
            hb = [ar.f32(S), ar.f32(S)]
            B_hb = [Buf('hb0'), Buf('hb1')]
            linear_T(mixT, B_act, 16, W[('ev_w_out', ev)], wbufs[('ev_w_out', ev)], [(o * 128, 128) for o in range(16)],
                     add_residual_epi((hb, B_hb)))
            ar.pop()
        if do_mixer and layer % 2 == 1:
            od = layer // 2
            hn = act_view(16)
            norm_T(hT, B_hT, 16, lambda c, layer=layer: scol('norm_mix', layer * 16 + c), hn, B_act)
            kb.barrier()
            ar.push()
            oWin = W[('od_w_in', od)]
            B_Win = wbufs[('od_w_in', od)]
            lgT = ar.f32(S)
            B_lgT = Buf('lgT')
            ar.push()
            cosT = ar.f32(S)
            sinT = ar.f32(S)
            B_cs = Buf('cossin')
            kpe = [ar.bf16(S), ar.bf16(S)]
            B_kpe = Buf('kpe')
            ar.push()
            posi = ar.f32(S).bitcast(I32)
            B_posi = Buf('posi')
            invf = ar.f32(8)
            B_invf = Buf('invf')
            u_ = ar.f32(S)
            B_u = Buf('ropeu')
            ki = ar.f32(S).bitcast(I32)
            B_ki = Buf('ropeki')
            kf = ar.f32(S)
            B_kf = Buf('ropekf')
            m_ = ar.f32(S)
            B_m = Buf('ropem')
            kb.op('sp', lambda e: e.dma_start(out=posi[0:32, :], in_=pos_in.partition_broadcast(32)), w=[B_posi], dma=B_posi)
            kb.op('sp', lambda e: e.dma_start(out=invf[0:32, 0:1], in_=invf_in), w=[B_invf], dma=B_invf)
            kb.op('dve', lambda e: e.tensor_copy(out=u_[0:32, :], in_=posi[0:32, :]), r=[B_posi], w=[B_u])
            kb.op('dve', lambda e: e.tensor_scalar(out=u_[0:32, :], in0=u_[0:32, :], scalar1=invf[0:32, 0:1], scalar2=float(1.0 / (2 * np.pi)),
                                                   op0=ALU.mult, op1=ALU.mult), r=[B_u, B_invf], w=[B_u])
            kb.op('dve', lambda e: e.tensor_copy(out=ki[0:32, :], in_=u_[0:32, :]), r=[B_u], w=[B_ki])
            kb.op('dve', lambda e: e.tensor_copy(out=kf[0:32, :], in_=ki[0:32, :]), r=[B_ki], w=[B_kf])
            kb.op('dve', lambda e: e.tensor_tensor(out=u_[0:32, :], in0=u_[0:32, :], in1=kf[0:32, :], op=ALU.subtract), r=[B_u, B_kf], w=[B_u])
            for dst, shift in ((sinT, 0.0), (cosT, 0.25)):
                kb.op('dve', lambda e, shift=shift: e.tensor_scalar(out=kf[0:32, :], in0=u_[0:32, :], scalar1=shift, scalar2=None, op0=ALU.add), r=[B_u], w=[B_kf])
                for _rep in range(2):
                    kb.op('dve', lambda e: e.tensor_scalar(out=m_[0:32, :], in0=kf[0:32, :], scalar1=0.5, scalar2=None, op0=ALU.is_gt), r=[B_kf], w=[B_m])
                    kb.op('dve', lambda e: e.tensor_tensor(out=kf[0:32, :], in0=kf[0:32, :], in1=m_[0:32, :], op=ALU.subtract), r=[B_kf, B_m], w=[B_kf])
                    kb.op('dve', lambda e: e.tensor_scalar(out=m_[0:32, :], in0=kf[0:32, :], scalar1=-0.5, scalar2=None, op0=ALU.is_lt), r=[B_kf], w=[B_m])
                    kb.op('dve', lambda e: e.tensor_tensor(out=kf[0:32, :], in0=kf[0:32, :], in1=m_[0:32, :], op=ALU.add), r=[B_kf, B_m], w=[B_kf])
                kb.op('act', lambda e, dst=dst: e.activation(out=dst[0:32, :], in_=kf[0:32, :], func=AF.Sin, scale=float(2 * np.pi * 0.999999)), r=[B_kf], w=[B_cs], accum=True)
            ar.pop()
            kb.barrier()
            ar.push()
            stf = [ar.f32(S), ar.f32(S)]
            B_stf = [Buf('stf0'), Buf('stf1')]
            tmpr = [ar.f32(S), ar.f32(S)]
            B_tmpr = Buf('tmpr')

            def rope_apply(x1, x2, B_x, o1, o2, B_o, acc):
                c_, s_ = cosT[0:32, :], sinT[0:32, :]
                t1, t2 = tmpr[0][0:32, :], tmpr[1][0:32, :]
                kb.op('dve', lambda e: e.tensor_tensor(out=t1, in0=x1, in1=c_, op=ALU.mult), r=B_x + [B_cs], w=[B_tmpr])
                kb.op('dve', lambda e: e.tensor_tensor(out=t2, in0=x2, in1=s_, op=ALU.mult), r=B_x + [B_cs], w=[B_tmpr], accum=True)
                kb.op('dve', lambda e: e.tensor_tensor(out=o1, in0=t1, in1=t2, op=ALU.subtract), r=[B_tmpr], w=[B_o], accum=acc)
                kb.op('dve', lambda e: e.tensor_tensor(out=t1, in0=x2, in1=c_, op=ALU.mult), r=B_x + [B_cs, B_o], w=[B_tmpr])
                kb.op('dve', lambda e: e.tensor_tensor(out=t2, in0=x1, in1=s_, op=ALU.mult), r=B_x + [B_cs], w=[B_tmpr], accum=True)
                kb.op('dve', lambda e: e.tensor_tensor(out=o2, in0=t1, in1=t2, op=ALU.add), r=[B_tmpr], w=[B_o], accum=True)

            cqkvr = cqkv.rearrange("(c p) t -> c p t", p=128)
            glqkr = glqk.rearrange("(c p) t -> c p t", p=128)

            def f32_store_epi(dstr, B_d):
                def epi(pi, half):
                    sl = pi % 2
                    kb.op('act', lambda e: e.copy(stf[sl], ps_half(half)), r=PSH(half), w=[B_stf[sl]])
                    kb.op('pool', lambda e: e.dma_start(out=dstr[pi], in_=stf[sl]), r=[B_stf[sl]], w=[B_d], dma=B_stf[sl], accum=True)
                return epi
            linear_T(hn, B_act, 16, oWin, B_Win, [(c * 128, 128) for c in range(8)], f32_store_epi(cqkvr, B_cqkv), SW=128)
            linear_T(hn, B_act, 16, oWin, B_Win, [(1088 + c * 128, 128) for c in range(8)], f32_store_epi(glqkr, B_glqk), SW=128)
            def kr_epi(pi, half):
                if pi < 2:
                    kb.op('act', lambda e: e.copy(stf[pi][0:32, :], ps_half(half, 32)), r=PSH(half), w=[B_stf[pi]])
                    if pi == 1:
                        rope_apply(stf[0][0:32, :], stf[1][0:32, :], [B_stf[0], B_stf[1]], kpe[0][0:32, :], kpe[1][0:32, :], B_kpe, False)
                else:
                    kb.op('act', lambda e: e.copy(lgT[0:16, :], ps_half(half, 16)), r=PSH(half), w=[B_lgT])
            linear_T(hn, B_act, 16, oWin, B_Win, [(1024, 32), (1056, 32), (3136, 16)], kr_epi, SW=128)
            gateTr = gateT.rearrange("(c p) t -> c p t", p=128)
            stbb = [stf[0].bitcast(BF16)[:, 0:S], stf[1].bitcast(BF16)[:, 0:S]]

            def gate_epi(pi, half):
                sl = pi % 2
                kb.op('act', lambda e: e.activation(out=stbb[sl], in_=ps_half(half), func=AF.Silu), r=PSH(half), w=[B_stf[sl]])
                kb.op('pool', lambda e: e.dma_start(out=gateTr[pi], in_=stbb[sl]), r=[B_stf[sl]], w=[B_gateT], dma=B_stf[sl], accum=True)
            linear_T(hn, B_act, 16, oWin, B_Win, [(3152 + c * 128, 128) for c in range(8)], gate_epi, SW=128)
            oWr_ = oWin.rearrange("(c p) n -> p c n", p=128)
            lvc = [0]

            def lv_epi(tb, cb, n, bk):
                sl = lvc[0] % 2
                lvc[0] += 1
                kb.op('act', lambda e: e.copy(stf[sl][:, 0:n], psv(bk)[:, 0:n]), r=[PSB[bk]], w=[B_stf[sl]])
                kb.op('pool', lambda e: e.dma_start(out=glv[tb * 128:(tb + 1) * 128, cb * 128:cb * 128 + n], in_=stf[sl][:, 0:n]),
                      r=[B_stf[sl]], w=[B_glv], dma=B_stf[sl], accum=True)
            linear_tok(hn, B_act, 16, lambda c0, n: oWr_[:, :, 2112 + c0:2112 + c0 + n], B_Win, 1024, S, lv_epi, CW=128)
            cqn = actbuf[:, 0:4 * S].rearrange("p (c t) -> p c t", c=4)
            ckvn = actbuf[:, 4 * S:8 * S].rearrange("p (c t) -> p c t", c=4)
            B_cqn = Buf('cqn')
            B_ckvn = Buf('ckvn')
            norm_T(cqkv[0:512, :], B_cqkv, 4, lambda c, od=od: scol('od_q_norm', od * 4 + c), cqn, B_cqn)
            norm_T(cqkv[512:1024, :], B_cqkv, 4, lambda c, od=od: scol('od_kv_norm', od * 4 + c), ckvn, B_ckvn)
            kb.barrier()
            sbqkr = sbqk.rearrange("(c p) t -> c p t", p=128)
            mqrr = mqr.rearrange("(c p) t -> c p t", p=32)
            qrs = [stf[0][:, 1024:2048].bitcast(BF16), stf[1][:, 1024:2048].bitcast(BF16)]
            B_qrs = Buf('qrs')
            Wuq = W[('od_w_uq', od)]
            opieces = []
            for h in range(8):
                opieces += [(h * 192, 128), (h * 192 + 128, 32), (h * 192 + 160, 32)]

            def uq_epi(pi, half):
                h, k_ = pi // 3, pi % 3
                if k_ == 0:
                    sl = h % 2
                    kb.op('act', lambda e: e.copy(stbb[sl], ps_half(half)), r=PSH(half), w=[B_stf[sl]])
                    kb.op('pool', lambda e: e.dma_start(out=sbqkr[h], in_=stbb[sl]), r=[B_stf[sl]], w=[B_sbqk], dma=B_stf[sl], accum=True)
                else:
                    xb_ = tmpr
                    kb.op('act', lambda e: e.copy(xr[k_ - 1][0:32, :], ps_half(half, 32)), r=PSH(half), w=[B_xr[k_ - 1]])
                    if k_ == 2:
                        rope_apply(xr[0][0:32, :], xr[1][0:32, :], [B_xr[0], B_xr[1]], qrs[0][0:32, :], qrs[1][0:32, :], B_qrs, False)
                        for j in range(2):
                            kb.op('pool', lambda e, j=j: e.dma_start(out=mqrr[h * 2 + j], in_=qrs[j][0:32, :]), r=[B_qrs], w=[B_mqr], dma=B_qrs, accum=True)
            xr = [ar.f32(S), ar.f32(S)]
            B_xr = [Buf('xr0'), Buf('xr1')]
            linear_T(cqn, B_cqn, 4, Wuq, wbufs[('od_w_uq', od)], opieces, uq_epi, SW=(128 if dbg else 256))
            Wukv = W[('od_w_ukv', od)]

            def uk_epi(pi, half):
                sl = pi % 2
                kb.op('act', lambda e: e.copy(stbb[sl], ps_half(half)), r=PSH(half), w=[B_stf[sl]])
                kb.op('pool', lambda e: e.dma_start(out=sbqkr[8 + pi], in_=stbb[sl]), r=[B_stf[sl]], w=[B_sbqk], dma=B_stf[sl], accum=True)
            linear_T(ckvn, B_ckvn, 4, Wukv, wbufs[('od_w_ukv', od)], [(h * 256, 128) for h in range(8)], uk_epi, SW=128)
            Wukvr = Wukv.rearrange("(c p) n -> p c n", p=128)
            mstv = [ar.bf16(256), ar.bf16(256)]
            B_stv = [Buf('stv0'), Buf('stv1')]
            mvcnt = [0]

            def mv_epi(tb, cb, n, bk):
                sl = mvcnt[0] % 2
                mvcnt[0] += 1
                kb.op('act', lambda e: e.copy(mstv[sl][:, 0:n], psv(bk)[:, 0:n]), r=[PSB[bk]], w=[B_stv[sl]])
                kb.op('pool', lambda e: e.dma_start(out=sbv[tb * 128:(tb + 1) * 128, cb * 128:cb * 128 + n], in_=mstv[sl][:, 0:n]),
                      r=[B_stv[sl]], w=[B_sbv], dma=B_stv[sl], accum=True)

            def vsel(c0, n):
                h0 = c0 // 128
                return [(j * 128, 128, Wukvr[:, :, (h0 + j) * 256 + 128:(h0 + j) * 256 + 256]) for j in range(n // 128)]
            linear_tok(ckvn, B_ckvn, 4, vsel, wbufs[('od_w_ukv', od)], 1024, S, mv_epi, CW=128)
            ar.pop()
            kb.barrier()
            mixT = act_view(16)
            ar.push()
            scm = 192 ** -0.5
            mqh = ar.bf16(S)
            kh_ = ar.bf16(S)
            mvh = ar.bf16(S).rearrange("p (c d) -> p c d", c=16)
            qr_ = [ar.bf16(S), ar.bf16(S)]
            B_qkv = Buf('sbqkvh')
            scb = ar.f32(S)
            B_scb = Buf('mlasc')
            mwbuf = ar.bf16(S)
            B_w_ = Buf('sbw')
            mwT = ar.bf16(S).rearrange("p (c q) -> p c q", c=16)
            B_wT = Buf('sbwT')
            mstat = ar.f32(8)
            B_stat = Buf('mlastat')
            sbvr = sbv.rearrange("(c p) n -> p c n", p=128)
            negup = consts[:, C_NEGUP:C_NEGUP + 128]
            for h in range(8):
                kb.op('sp', lambda e, h=h: e.dma_start(out=mqh, in_=sbqkr[h]), r=[B_sbqk], w=[B_qkv], dma=B_qkv)
                kb.op('sp', lambda e, h=h: e.dma_start(out=kh_, in_=sbqkr[8 + h]), r=[B_sbqk], w=[B_qkv], dma=B_qkv, accum=True)
                kb.op('sp', lambda e, h=h: e.dma_start(out=mvh, in_=sbvr[:, :, h * 128:(h + 1) * 128]), r=[B_sbv], w=[B_qkv], dma=B_qkv, accum=True)
                for j in range(2):
                    kb.op('sp', lambda e, h=h, j=j: e.dma_start(out=qr_[j][0:32, :], in_=mqrr[h * 2 + j]), r=[B_mqr], w=[B_qkv], dma=B_qkv, accum=True)
                for qb in range(16):
                    nk = (qb + 1) * 128
                    nt = (nk + 511) // 512
                    qs = slice(qb * 128, (qb + 1) * 128)
                    for t in range(nt):
                        w_ = min(512, nk - t * 512)
                        ks = slice(t * 512, t * 512 + w_)
                        kb.op('pe', lambda e, qs=qs, ks=ks, t=t, w_=w_: e.matmul(psv(t)[:, 0:w_], mqh[:, qs], kh_[:, ks], start=True, stop=False), r=[B_qkv], w=[PSB[t]])
                        kb.op('pe', lambda e, qs=qs, ks=ks, t=t, w_=w_: e.matmul(psv(t)[:, 0:w_], qr_[0][0:32, qs], kpe[0][0:32, ks], start=False, stop=False), r=[B_qkv, B_kpe], w=[PSB[t]])
                        kb.op('pe', lambda e, qs=qs, ks=ks, t=t, w_=w_: e.matmul(psv(t)[:, 0:w_], qr_[1][0:32, qs], kpe[1][0:32, ks], start=False, stop=True), r=[B_qkv, B_kpe], w=[PSB[t]])
                    zps = ps_t[:, 0:nk]
                    ZB = PSB[0:nt]
                    dg = slice(nk - 128, nk)
                    kb.op('act', lambda e, zps=zps, nk=nk: e.activation(out=scb[:, 0:nk], in_=zps, func=AF.Identity, scale=scm), r=ZB, w=[B_scb])
                    kb.op('dve', lambda e, dg=dg: e.tensor_tensor(out=scb[:, dg], in0=scb[:, dg], in1=negup, op=ALU.add), r=[B_scb, B_const], w=[B_scb])
                    kb.op('dve', lambda e, nk=nk: e.tensor_reduce(out=mstat[:, 0:1], in_=scb[:, 0:nk], axis=AX.X, op=ALU.max), r=[B_scb], w=[B_stat])
                    kb.op('dve', lambda e: e.tensor_scalar(out=mstat[:, 1:2], in0=mstat[:, 0:1], scalar1=-1.0, scalar2=None, op0=ALU.mult), r=[B_stat], w=[B_stat])
                    kb.op('act', lambda e, nk=nk: e.activation(out=mwbuf[:, 0:nk], in_=scb[:, 0:nk], func=AF.Exp, bias=mstat[:, 1:2]), r=[B_scb, B_stat], w=[B_w_])
                    kb.op('dve', lambda e, nk=nk: e.tensor_reduce(out=mstat[:, 2:3], in_=mwbuf[:, 0:nk], axis=AX.X, op=ALU.add), r=[B_w_], w=[B_stat])
                    kb.op('dve', lambda e: e.reciprocal(out=mstat[:, 3:4], in_=mstat[:, 2:3]), r=[B_stat], w=[B_stat])
                    kb.op('dve', lambda e, nk=nk: e.tensor_scalar(out=mwbuf[:, 0:nk], in0=mwbuf[:, 0:nk], scalar1=mstat[:, 3:4], scalar2=None, op0=ALU.mult), r=[B_w_, B_stat], w=[B_w_])
                    for k_ in range(qb + 1):
                        bk = 4 + k_ // 8
                        kb.op('pe', lambda e, k_=k_, bk=bk: e.transpose(psbf(bk)[:, (k_ % 8) * 128:(k_ % 8 + 1) * 128], mwbuf[:, k_ * 128:(k_ + 1) * 128], identb),
                              r=[B_w_, B_identb], w=[PSB[bk]])
                    for g_ in range((qb + 8) // 8):
                        n_ = min(8, qb + 1 - g_ * 8)
                        if g_ == 0:
                            kb.op('act', lambda e, n_=n_: e.copy(mwT.rearrange("p c q -> p (c q)")[:, 0:n_ * 128], psbf(4)[:, 0:n_ * 128]), r=[PSB[4]], w=[B_wT])
                        else:
                            kb.op('dve', lambda e, n_=n_: e.tensor_copy(out=mwT.rearrange("p c q -> p (c q)")[:, 1024:1024 + n_ * 128], in_=psbf(5)[:, 0:n_ * 128]),
                                  r=[PSB[5]], w=[B_wT], accum=True)
                    ob = 6 + qb % 2
                    for k_ in range(qb + 1):
                        kb.op('pe', lambda e, k_=k_, ob=ob, qb=qb: e.matmul(psv(ob)[:, 0:128], mvh[:, k_, :], mwT[:, k_, :], start=(k_ == 0), stop=(k_ == qb)),
                              r=[B_qkv, B_wT], w=[PSB[ob]])
                    kb.op('act', lambda e, ob=ob, h=h, qs=qs: e.copy(mixT[:, h, qs], psv(ob)[:, 0:128]), r=[PSB[ob]], w=[B_act], accum=True)
            ar.pop()
            ar.pop()
            dump('ml0', mixT[:, 0, :], [B_act])
            if GLA_ON:
                gla_mixer(od, lgT, B_lgT, mixT)
                dump('gl0', mixT[:, 8, :], [B_act])
            kb.barrier()
            ohb = [ar.f32(S), ar.f32(S)]
            B_hb = [Buf('hb0'), Buf('hb1')]
            linear_T(mixT, B_act, 16, W[('od_w_out', od)], wbufs[('od_w_out', od)], [(o_ * 128, 128) for o_ in range(16)],
                     add_residual_epi((ohb, B_hb)), SW=128)
            ar.pop()
        if do_xattn:
            hn = act_view(16)
            norm_T(hT, B_hT, 16, lambda c, layer=layer: scol('norm_xattn', layer * 16 + c), hn, B_act)
            kb.barrier()
            ar.push()
            qT = ar.bf16(4 * S).rearrange("p (c t) -> p c t", c=4)
            B_qT = Buf('xqT')
            kT = ar.bf16(4 * MEM).rearrange("p (c t) -> p c t", c=4)
            B_kT = Buf('xkT')
            vtk = ar.bf16(2 * 512).rearrange("p (c t) -> p c t", c=2)
            B_vtk = Buf('xv')
            xoT = ar.bf16(4 * S).rearrange("p (c t) -> p c t", c=4)
            B_xoT = Buf('xoT')
            linear_T(hn, B_act, 16, W[('xa_wq', layer)], wbufs[('xa_wq', layer)], [(h * 128, 128) for h in range(4)],
                     copy_epi(lambda pi: qT[:, pi, :], B_qT))
            linear_T(memnT, B_memnT, 16, W[('xa_wk', layer)], wbufs[('xa_wk', layer)], [(h * 128, 128) for h in range(4)],
                     copy_epi(lambda pi: kT[:, pi, :], B_kT, ntok=MEM), ntok=MEM)
            Wv = W[('xa_wv', layer)].rearrange("(c p) n -> p c n", p=128)

            def v_epi(tb, cb, n, bk):
                kb.op('act', lambda e: e.copy(vtk[:, tb, cb * 256:cb * 256 + n], psv(bk)[:, 0:n]), r=[PSB[bk]], w=[B_vtk], accum=True)
            linear_tok(memnT, B_memnT, 16, lambda c0, n: Wv[:, :, c0:c0 + n], wbufs[('xa_wv', layer)], 512, MEM, v_epi)
            dump('xq0', qT[:, 0, :], [B_qT])
            dump('xk0', kT[:, 0, :], [B_kT])
            dump('xv', vtk.rearrange("p c t -> p (c t)"), [B_vtk])
            kb.barrier()
            sc = 128 ** -0.5
            NS = 3
            pr = [ar.bf16(MEM) for _ in range(NS)]
            B_pr = [Buf('xpr%d' % i) for i in range(NS)]
            pT = [ar.bf16(MEM).rearrange("p (c t) -> p c t", c=2) for _ in range(NS)]
            B_pT = [Buf('xpT%d' % i) for i in range(NS)]
            stat = [ar.f32(8) for _ in range(NS)]
            B_stat = [Buf('xstat%d' % i) for i in range(NS)]
            it = 0
            for h in range(4):
                for qb in range(16):
                    sl = it % NS
                    b1 = (it * 2) % 8
                    b2 = (it * 2 + 1) % 8
                    it += 1
                    st_ = stat[sl]
                    kb.op('pe', lambda e, h=h, qb=qb, b1=b1: e.matmul(psv(b1)[:, 0:MEM], qT[:, h, qb * 128:(qb + 1) * 128], kT[:, h, :],
                                                                      start=True, stop=True), r=[B_qT, B_kT], w=[PSB[b1]])
                    kb.op('dve', lambda e, b1=b1, st_=st_: e.tensor_reduce(out=st_[:, 0:1], in_=psv(b1)[:, 0:MEM], axis=AX.X, op=ALU.max),
                          r=[PSB[b1]], w=[B_stat[sl]])
                    kb.op('dve', lambda e, st_=st_: e.tensor_scalar(out=st_[:, 1:2], in0=st_[:, 0:1], scalar1=-sc, scalar2=None, op0=ALU.mult),
                          r=[B_stat[sl]], w=[B_stat[sl]])
                    kb.op('act', lambda e, b1=b1, sl=sl, st_=st_: e.activation(out=pr[sl], in_=psv(b1)[:, 0:MEM], func=AF.Exp, scale=sc,
                                                                             bias=st_[:, 1:2]),
                          r=[PSB[b1], B_stat[sl]], w=[B_pr[sl]])
                    kb.op('dve', lambda e, sl=sl, st_=st_: e.tensor_reduce(out=st_[:, 2:3], in_=pr[sl], axis=AX.X, op=ALU.add),
                          r=[B_pr[sl]], w=[B_stat[sl]])
                    kb.op('dve', lambda e, st_=st_: e.reciprocal(out=st_[:, 3:4], in_=st_[:, 2:3]), r=[B_stat[sl]], w=[B_stat[sl]])
                    kb.op('dve', lambda e, sl=sl, st_=st_: e.tensor_scalar(out=pr[sl], in0=pr[sl], scalar1=st_[:, 3:4], scalar2=None, op0=ALU.mult),
                          r=[B_pr[sl], B_stat[sl]], w=[B_pr[sl]])
                    for mb in range(2):
                        kb.op('pe', lambda e, sl=sl, mb=mb, b2=b2: e.transpose(psbf(b2)[:, mb * 128:(mb + 1) * 128], pr[sl][:, mb * 128:(mb + 1) * 128], identb),
                              r=[B_pr[sl], B_identb], w=[PSB[b2]])
                    kb.op('act', lambda e, sl=sl, b2=b2: e.copy(pT[sl].rearrange("p c t -> p (c t)"), psbf(b2)[:, 0:256]), r=[PSB[b2]], w=[B_pT[sl]])
                    for mb in range(2):
                        kb.op('pe', lambda e, sl=sl, mb=mb, b2=b2, h=h: e.matmul(psv(b2)[:, 256:384], vtk[:, mb, h * 128:(h + 1) * 128], pT[sl][:, mb, :],
                                                                               start=(mb == 0), stop=(mb == 1)), r=[B_vtk, B_pT[sl]], w=[PSB[b2]])
                    kb.op('dve', lambda e, b2=b2, h=h, qb=qb: e.tensor_copy(out=xoT[:, h, qb * 128:(qb + 1) * 128], in_=psv(b2)[:, 256:384]),
                          r=[PSB[b2]], w=[B_xoT], accum=True)
            dump('xo0', xoT[:, 0, :], [B_xoT])
            kb.barrier()
            hb = [ar.f32(S), ar.f32(S)]
            B_hb = [Buf('hb0'), Buf('hb1')]
            linear_T(xoT, B_xoT, 4, W[('xa_wo', layer)], wbufs[('xa_wo', layer)], [(o * 128, 128) for o in range(16)],
                     add_residual_epi((hb, B_hb)))
            ar.pop()
        if do_ffn:
            hn = act_view(16)
            norm_T(hT, B_hT, 16, lambda c, layer=layer: scol('norm_ffn', layer * 16 + c), hn, B_act)
            kb.barrier()
            ar.push()
            xb = [ar.f32(S + 8) for _ in range(2)]
            B_xb = [Buf('xb0'), Buf('xb1')]
            cus = [ar.f32(S), ar.f32(S)]
            B_cus = [Buf('cu0'), Buf('cu1')]
            cz = ar.f32(S)
            B_cz = Buf('cz')
            gs = [ar.bf16(S), ar.bf16(S)]
            B_gs = [Buf('gs0'), Buf('gs1')]
            for i in range(2):
                kb.op('pool', lambda e, i=i: e.memset(xb[i][:, 0:8], 0.0), w=[B_xb[i]])
            pieces = []
            for j0 in range(0, 44, 2):
                pieces += [(j0 * 128, 128), ((j0 + 1) * 128, 128), (DFF + j0 * 128, 128), (DFF + (j0 + 1) * 128, 128)]
            gTr = gT.rearrange("(c p) t -> c p t", p=128)

            def ffn_epi(pi, half, layer=layer):
                q4 = pi % 4
                j = (pi // 4) * 2 + (q4 % 2)
                isz = q4 // 2
                ch = j + 44 * isz
                sl = isz
                cu = cus[q4 % 2]
                B_cu = B_cus[q4 % 2]
                dstc = cz if isz else cu
                B_d = B_cz if isz else B_cu

                def wcol(tap):
                    return scol('ffn_conv', (layer * 3 + tap) * 88 + ch)
                bcol = scol('ffn_conv_b', layer * 88 + ch)
                kb.op('act', lambda e, sl=sl, half=half: e.copy(xb[sl][:, 8:8 + S], ps_half(half)),
                      r=PSH(half), w=[B_xb[sl]], accum=True)
                kb.op('act', lambda e, sl=sl: e.activation(out=dstc, in_=xb[sl][:, 8:8 + S], func=AF.Identity,
                                                            scale=wcol(2), bias=bcol), r=[B_xb[sl], B_small], w=[B_d])
                kb.op('dve', lambda e, sl=sl: e.scalar_tensor_tensor(out=dstc, in0=xb[sl][:, 7:7 + S], scalar=wcol(1), in1=dstc,
                                                                       op0=ALU.mult, op1=ALU.add), r=[B_xb[sl], B_d, B_small], w=[B_d])
                kb.op('dve', lambda e, sl=sl: e.scalar_tensor_tensor(out=dstc, in0=xb[sl][:, 6:6 + S], scalar=wcol(0), in1=dstc,
                                                                       op0=ALU.mult, op1=ALU.add), r=[B_xb[sl], B_d, B_small], w=[B_d])
                if isz:
                    g2 = j % 2
                    kb.op('act', lambda e: e.activation(out=cz, in_=cz, func=AF.Silu), r=[B_cz], w=[B_cz])
                    kb.op('pool', lambda e, g2=g2: e.tensor_tensor(out=gs[g2], in0=cz, in1=cu, op=ALU.mult),
                          r=[B_cz, B_cu], w=[B_gs[g2]])
                    kb.op('pool', lambda e, g2=g2, j=j: e.dma_start(out=gTr[j], in_=gs[g2]), r=[B_gs[g2]], w=[B_gT],
                          dma=B_gs[g2], accum=True)
            linear_T(hn, B_act, 16, W[('ffn_w_in', layer)], wbufs[('ffn_w_in', layer)], pieces, ffn_epi)
            ar.pop()
            for khalf in range(2):
                kb.barrier()
                gv = act_view(22)
                gTk = gT[khalf * 22 * 128:(khalf + 1) * 22 * 128, :].rearrange("(c p) t -> p c t", p=128)
                for q in range(2):
                    kb.op('sp', lambda e, q=q, gTk=gTk, gv=gv: e.dma_start(out=gv[:, q * 11:(q + 1) * 11, :], in_=gTk[:, q * 11:(q + 1) * 11, :]),
                          r=[B_gT], w=[B_act], dma=B_act, accum=(q > 0))
                ar.push()
                hb = [ar.f32(S), ar.f32(S)]
                B_hb = [Buf('hb0'), Buf('hb1')]
                Wo = W[('ffn_w_out', layer)][khalf * 22 * 128:(khalf + 1) * 22 * 128, :]
                linear_T(gv, B_act, 22, Wo, wbufs[('ffn_w_out', layer)], [(o * 128, 128) for o in range(16)],
                         add_residual_epi((hb, B_hb)))
                ar.pop()

    fo = act_view(16)
    kb.barrier()
    ar.push()
    fin = arena_t[:, actbuf_off:actbuf_off + 16 * 512].rearrange("p (c t) -> p c t", c=16)
    B_fin = Buf('fin')
    ost = [ar.f32(D)] * 2
    B_ost = [Buf('ost0')] * 2
    for t in range(4):
        norm_T(hT[:, t * 512:(t + 1) * 512], B_hT, 16, lambda c: scol('final_norm', c), fin, B_fin, ncols=512)
        kb.barrier()
        for tb in range(4):
            sl = tb % 2
            for q in range(4):
                bk = (tb * 4 + q) % 8
                for j in range(4):
                    fc = q * 4 + j
                    kb.op('pe', lambda e, fc=fc, tb=tb, bk=bk, j=j: e.transpose(
                        psv(bk)[:, j * 128:(j + 1) * 128], fin[:, fc, tb * 128:(tb + 1) * 128], ident),
                        r=[B_fin, B_const], w=[PSB[bk]])
                if q % 2 == 0:
                    kb.op('act', lambda e, sl=sl, q=q, bk=bk: e.copy(ost[sl][:, q * 512:(q + 1) * 512], psv(bk)),
                          r=[PSB[bk]], w=[B_ost[sl]], accum=(q > 0))
                else:
                    kb.op('dve', lambda e, sl=sl, q=q, bk=bk: e.tensor_copy(out=ost[sl][:, q * 512:(q + 1) * 512], in_=psv(bk)),
                          r=[PSB[bk]], w=[B_ost[sl]], accum=True)
            r0 = t * 512 + tb * 128
            kb.op('pool', lambda e, sl=sl, r0=r0: e.dma_start(out=out_t[r0:r0 + 128, :], in_=ost[sl]),
                  r=[B_ost[sl]], w=[Buf('outdram')], dma=B_ost[sl])
    ar.pop()
    kb.emit()
    return nc


def pack_small(inputs):
    rows = [_rows128(inputs[k]) for k in SMALL_KEYS]
    a = np.concatenate(rows, 0)
    pad = np.zeros((SMALL_ROWS_PAD - a.shape[0], 128), np.float32)
    return np.concatenate([a, pad], 0)


def make_in_maps(inputs, ncores, layers=(0, 1, 2, 3), batch_ids=None):
    inputs = {k: np.asarray(v) for k, v in inputs.items()}
    small = pack_small(inputs)
    consts = make_consts()
    hp = np.zeros((8, 4), np.float32)
    for e in range(2):
        hp[:, 2 * e] = inputs['ev_a_log'][e]
        hp[:, 2 * e + 1] = inputs['ev_dt_bias'][e]
    w2 = np.ascontiguousarray(np.transpose(inputs['od_gla_w2'], (1, 0, 2))).astype(np.float32)
    invf = (10000.0 ** (-np.arange(0, 64, 2, dtype=np.float32) / 64)).astype(np.float32).reshape(32, 1)
    needed = []
    for l in layers:
        for it in layer_weights(l):
            if it not in needed:
                needed.append(it)
    maps = []
    if batch_ids is None:
        batch_ids = list(range(ncores))
    for c in range(ncores):
        b = batch_ids[c]
        m = {'x': np.ascontiguousarray(inputs['x'][b]), 'mem': np.ascontiguousarray(inputs['mem'][b]),
             'positions': np.ascontiguousarray(inputs['positions'][b]).reshape(1, S).astype(np.int32),
             'smallp': small, 'consts': consts, 'hp': hp, 'gla_w2': w2, 'invf': invf}
        for (n, li) in needed:
            L, K, N = BIGW[n]
            w = inputs[n][li]
            if ncores == 1:
                m['%s_%d' % (n, li)] = np.ascontiguousarray(w)
            else:
                r = K // ncores
                m['%s_%d' % (n, li)] = np.ascontiguousarray(w[c * r:(c + 1) * r])
        maps.append(m)
    return maps


def kernel(**inputs):
    ncores = 8
    nc = build(ncores)
    maps = make_in_maps(inputs, ncores)
    res = run_bass_kernel_spmd(nc, maps, core_ids=list(range(ncores)))
    return np.stack([np.asarray(r['out']) for r in res.results], 0).astype(np.float32)
```
